# Optimizing a Trainium2 kernel written in Bass

```python
import math
import jax, jax.numpy as jnp
from jax import lax
import numpy as np

D_MODEL = 1024
BATCH = 8
SEQ = 2048
DEPTH = 2
DEC_BATCH = 128
DEC_SEQ = 8
PAST_LEN = 16384
PAGE_SIZE = 128

N_AB_LAYERS = (DEPTH + 1) // 2
N_C_LAYERS = DEPTH // 2
A_HEADS = 16
A_HEAD_DIM = 64
A_D_INNER = A_HEADS * A_HEAD_DIM
A_GROUPS = 4
A_STATE = 128
A_CONV = 4
A_CONV_DIM = A_D_INNER + 2 * A_GROUPS * A_STATE
A_CHUNK = 128
A_PROJ = A_D_INNER + A_CONV_DIM + A_HEADS
B_CH = 1024
B_GROUP_CH = 16
B_GROUPS = B_CH // B_GROUP_CH
B_STATE = 64
IN_COLS = A_PROJ + B_CH
MIX_WIDTH = A_D_INNER + B_CH
C_CH = D_MODEL
C_WIDTH = 31
D_FF = 4 * D_MODEL
EPS = 1e-6

kernel_name = "hybrid_ssd_s5_conformer_decode_step"


def rms_norm(x, w):
    xf = x.astype(jnp.float32)
    y = xf * lax.rsqrt(jnp.mean(xf * xf, axis=-1, keepdims=True) + EPS)
    return (y * w.astype(jnp.float32)).astype(x.dtype)


def layer_norm(x, w, b):
    xf = x.astype(jnp.float32)
    mu = jnp.mean(xf, axis=-1, keepdims=True)
    xc = xf - mu
    y = xc * lax.rsqrt(jnp.mean(xc * xc, axis=-1, keepdims=True) + EPS)
    return (y * w.astype(jnp.float32) + b.astype(jnp.float32)).astype(x.dtype)


def causal_depthwise_conv(x, buf, w, b):
    width = w.shape[0]
    xp = jnp.concatenate([buf.astype(x.dtype), x], axis=1)
    y = lax.conv_general_dilated(xp, w.astype(x.dtype)[:, None, :], window_strides=(1,), padding="VALID",
                                 dimension_numbers=("NWC", "WIO", "NWC"), feature_group_count=x.shape[-1])
    return y + b.astype(x.dtype), xp[:, xp.shape[1] - (width - 1):]


def ssd_chunked_scan(xh, dt, a, bm, cm, h0):
    f32 = jnp.float32
    bsz, seqlen = xh.shape[0], xh.shape[1]
    q = math.gcd(seqlen, A_CHUNK)
    nc = seqlen // q
    hpg = A_HEADS // A_GROUPS
    xf = xh.astype(f32).reshape(bsz, nc, q, A_GROUPS, hpg, A_HEAD_DIM)
    dtc = dt.reshape(bsz, nc, q, A_GROUPS, hpg)
    bc = bm.astype(f32).reshape(bsz, nc, q, A_GROUPS, A_STATE)
    cc = cm.astype(f32).reshape(bsz, nc, q, A_GROUPS, A_STATE)
    cum = jnp.cumsum(dtc * a.reshape(A_GROUPS, hpg), axis=2)
    causal = jnp.tril(jnp.ones((q, q), dtype=bool))[:, :, None, None]
    seg = cum[:, :, :, None] - cum[:, :, None, :]
    decay = jnp.exp(jnp.where(causal, seg, -jnp.inf))
    scores = jnp.einsum("bcign,bcjgn->bcijg", cc, bc)
    y_diag = jnp.einsum("bcijge,bcjgep->bcigep", scores[..., None] * decay * dtc[:, :, None], xf)
    decay_end = jnp.exp(cum[:, :, -1:] - cum)
    chunk_states = jnp.einsum("bcjgn,bcjge,bcjgep->bcgepn", bc, decay_end * dtc, xf)
    chunk_decay = jnp.exp(cum[:, :, -1])

    def carry_step(h, inp):
        dec, st = inp
        return h * dec[..., None, None] + st, h

    h_start = h0.astype(f32).reshape(bsz, A_GROUPS, hpg, A_HEAD_DIM, A_STATE)
    h_final, h_in = lax.scan(carry_step, h_start,
                             (jnp.moveaxis(chunk_decay, 1, 0), jnp.moveaxis(chunk_states, 1, 0)))
    h_in = jnp.moveaxis(h_in, 0, 1)
    y_off = jnp.einsum("bcign,bcgepn,bcige->bcigep", cc, h_in, jnp.exp(cum))
    y = (y_diag + y_off).reshape(bsz, seqlen, A_HEADS, A_HEAD_DIM)
    return y, h_final.reshape(bsz, A_HEADS, A_HEAD_DIM, A_STATE)


def mamba2_mixer(proj, conv_buf, ssm_state, conv_w, conv_b, dt_bias, a_log, d_skip, norm_w):
    f32 = jnp.float32
    bsz, seqlen = proj.shape[0], proj.shape[1]
    z = proj[..., :A_D_INNER]
    xbc = proj[..., A_D_INNER:A_D_INNER + A_CONV_DIM]
    dt_raw = proj[..., A_D_INNER + A_CONV_DIM:]
    xbc, new_buf = causal_depthwise_conv(xbc, conv_buf, conv_w, conv_b)
    xbc = jax.nn.silu(xbc)
    xh = xbc[..., :A_D_INNER].reshape(bsz, seqlen, A_HEADS, A_HEAD_DIM)
    bm = xbc[..., A_D_INNER:A_D_INNER + A_GROUPS * A_STATE].reshape(bsz, seqlen, A_GROUPS, A_STATE)
    cm = xbc[..., A_D_INNER + A_GROUPS * A_STATE:].reshape(bsz, seqlen, A_GROUPS, A_STATE)
    dt = jax.nn.softplus(dt_raw.astype(f32) + dt_bias.astype(f32))
    a = -jnp.exp(a_log.astype(f32))
    y, h_final = ssd_chunked_scan(xh, dt, a, bm, cm, ssm_state)
    y = y + d_skip.astype(f32)[:, None] * xh.astype(f32)
    g = (y.reshape(bsz, seqlen, A_D_INNER) * jax.nn.silu(z.astype(f32)))
    g = g.reshape(bsz, seqlen, A_GROUPS, A_D_INNER // A_GROUPS)
    g = g * lax.rsqrt(jnp.mean(g * g, axis=-1, keepdims=True) + EPS)
    y = g.reshape(bsz, seqlen, A_D_INNER) * norm_w.astype(f32)
    return y.astype(proj.dtype), new_buf, h_final


def complex_affine_combine(left, right):
    ar_l, ai_l, br_l, bi_l = left
    ar_r, ai_r, br_r, bi_r = right
    ar = ar_r * ar_l - ai_r * ai_l
    ai = ar_r * ai_l + ai_r * ar_l
    br = ar_r * br_l - ai_r * bi_l + br_r
    bi = ar_r * bi_l + ai_r * br_l + bi_r
    return ar, ai, br, bi


def s5_mixer(u, h0_re, h0_im, lam_re, lam_im, log_step, b_re, b_im, c_re, c_im, d_skip, w_glu, b_glu):
    f32 = jnp.float32
    bsz, seqlen = u.shape[0], u.shape[1]
    uf = u.astype(f32).reshape(bsz, seqlen, B_GROUPS, B_GROUP_CH)
    lr, li = lam_re.astype(f32), lam_im.astype(f32)
    step = jnp.exp(log_step.astype(f32))[:, None]
    mag = jnp.exp(lr * step)
    ab_re, ab_im = mag * jnp.cos(li * step), mag * jnp.sin(li * step)
    den = lr * lr + li * li
    k_re = ((ab_re - 1.0) * lr + ab_im * li) / den
    k_im = (ab_im * lr - (ab_re - 1.0) * li) / den
    br, bi = b_re.astype(f32), b_im.astype(f32)
    bb_re = k_re[..., None] * br - k_im[..., None] * bi
    bb_im = k_re[..., None] * bi + k_im[..., None] * br
    bu_re = jnp.einsum("gpc,blgc->blgp", bb_re, uf)
    bu_im = jnp.einsum("gpc,blgc->blgp", bb_im, uf)
    a_re = jnp.broadcast_to(ab_re, (1, seqlen, B_GROUPS, B_STATE))
    a_im = jnp.broadcast_to(ab_im, (1, seqlen, B_GROUPS, B_STATE))
    p_re, p_im, h_re, h_im = lax.associative_scan(complex_affine_combine, (a_re, a_im, bu_re, bu_im), axis=1)
    s_re, s_im = h0_re.astype(f32)[:, None], h0_im.astype(f32)[:, None]
    h_re = h_re + p_re * s_re - p_im * s_im
    h_im = h_im + p_re * s_im + p_im * s_re
    y = (jnp.einsum("gcp,blgp->blgc", c_re.astype(f32), h_re)
         - jnp.einsum("gcp,blgp->blgc", c_im.astype(f32), h_im)
         + d_skip.astype(f32) * uf)
    y = jax.nn.gelu(y.reshape(bsz, seqlen, B_CH)).astype(u.dtype)
    out = y * jax.nn.sigmoid(y @ w_glu + b_glu)
    return out, h_re[:, -1], h_im[:, -1]


def conformer_conv_module(xn, buf, w_pw1, b_pw1, w_dw, b_dw, ln_w, ln_b, w_pw2, b_pw2):
    h = xn @ w_pw1 + b_pw1
    h = h[..., :C_CH] * jax.nn.sigmoid(h[..., C_CH:])
    h, new_buf = causal_depthwise_conv(h, buf, w_dw, b_dw)
    h = jax.nn.silu(layer_norm(h, ln_w, ln_b))
    return h @ w_pw2 + b_pw2, new_buf


def run_trunk(x, a_conv_state, a_ssm_state, b_state_re, b_state_im, c_conv_state, p):
    new_a_conv, new_a_ssm, new_b_re, new_b_im, new_c_conv = [], [], [], [], []
    ab_i = 0
    c_i = 0
    for layer in range(DEPTH):
        xn = rms_norm(x, p["norm_mix"][layer])
        if layer % 2 == 0:
            proj = xn @ p["w_in_ab"][ab_i]
            ya, buf_a, h_a = mamba2_mixer(proj[..., :A_PROJ], a_conv_state[ab_i], a_ssm_state[ab_i],
                                          p["a_conv_w"][ab_i], p["a_conv_b"][ab_i], p["a_dt_bias"][ab_i],
                                          p["a_log"][ab_i], p["a_d"][ab_i], p["a_norm"][ab_i])
            yb, hb_re, hb_im = s5_mixer(proj[..., A_PROJ:], b_state_re[ab_i], b_state_im[ab_i],
                                        p["s5_lam_re"][ab_i], p["s5_lam_im"][ab_i], p["s5_log_step"][ab_i],
                                        p["s5_b_re"][ab_i], p["s5_b_im"][ab_i], p["s5_c_re"][ab_i],
                                        p["s5_c_im"][ab_i], p["s5_d"][ab_i], p["s5_w_glu"][ab_i],
                                        p["s5_b_glu"][ab_i])
            x = x + jnp.concatenate([ya, yb], axis=-1) @ p["w_out_ab"][ab_i]
            new_a_conv.append(buf_a)
            new_a_ssm.append(h_a)
            new_b_re.append(hb_re)
            new_b_im.append(hb_im)
            ab_i += 1
        else:
            yc, buf_c = conformer_conv_module(xn, c_conv_state[c_i], p["c_w_pw1"][c_i], p["c_b_pw1"][c_i],
                                              p["c_w_dw"][c_i], p["c_b_dw"][c_i], p["c_ln_w"][c_i],
                                              p["c_ln_b"][c_i], p["c_w_pw2"][c_i], p["c_b_pw2"][c_i])
            x = x + yc
            new_c_conv.append(buf_c)
            c_i += 1
        xn = rms_norm(x, p["norm_ff"][layer])
        x = x + jnp.square(jax.nn.relu(xn @ p["w_ff1"][layer])) @ p["w_ff2"][layer]
    y = rms_norm(x, p["norm_final"])
    return (y, jnp.stack(new_a_conv), jnp.stack(new_a_ssm), jnp.stack(new_b_re), jnp.stack(new_b_im),
            jnp.stack(new_c_conv))


def setup_inputs(seed: int = 0) -> dict:
    key = jax.random.key(seed)
    ks = iter(jax.random.split(key, 64))
    f32 = jnp.float32

    def nrm(shape, scale):
        return scale * jax.random.normal(next(ks), shape, f32)

    def unif(shape, lo, hi):
        return jax.random.uniform(next(ks), shape, f32, lo, hi)

    n_ab, n_c = N_AB_LAYERS, N_C_LAYERS
    x_prompt = nrm((BATCH, SEQ, D_MODEL), 1.0)
    x_sample = nrm((DEC_BATCH, DEC_SEQ, D_MODEL), 1.0)
    state_a_conv = nrm((n_ab, DEC_BATCH, A_CONV - 1, A_CONV_DIM), 1.0)
    state_a_ssm = nrm((n_ab, DEC_BATCH, A_HEADS, A_HEAD_DIM, A_STATE), 0.5)
    state_b_re = nrm((n_ab, DEC_BATCH, B_GROUPS, B_STATE), 0.5)
    state_b_im = nrm((n_ab, DEC_BATCH, B_GROUPS, B_STATE), 0.5)
    state_c_conv = nrm((n_c, DEC_BATCH, C_WIDTH - 1, C_CH), 1.0)
    norm_mix = 1.0 + nrm((DEPTH, D_MODEL), 0.02)
    norm_ff = 1.0 + nrm((DEPTH, D_MODEL), 0.02)
    norm_final = 1.0 + nrm((D_MODEL,), 0.02)
    w_in_ab = nrm((n_ab, D_MODEL, IN_COLS), D_MODEL ** -0.5)
    a_conv_w = nrm((n_ab, A_CONV, A_CONV_DIM), A_CONV ** -0.5)
    a_conv_b = nrm((n_ab, A_CONV_DIM), 0.02)
    dt0 = jnp.exp(unif((n_ab, A_HEADS), math.log(1e-3), math.log(1e-1)))
    a_dt_bias = dt0 + jnp.log(-jnp.expm1(-dt0))
    a_log = jnp.log(unif((n_ab, A_HEADS), 1.0, 16.0))
    a_d = 1.0 + nrm((n_ab, A_HEADS), 0.1)
    a_norm = 1.0 + nrm((n_ab, A_D_INNER), 0.02)
    s5_lam_re = -0.5 + nrm((n_ab, B_GROUPS, B_STATE), 0.01)
    s5_lam_im = math.pi * jnp.arange(B_STATE, dtype=f32) + nrm((n_ab, B_GROUPS, B_STATE), 0.01)
    s5_log_step = unif((n_ab, B_GROUPS), math.log(1e-3), math.log(1e-1))
    s5_b_re = nrm((n_ab, B_GROUPS, B_STATE, B_GROUP_CH), (2 * B_GROUP_CH) ** -0.5)
    s5_b_im = nrm((n_ab, B_GROUPS, B_STATE, B_GROUP_CH), (2 * B_GROUP_CH) ** -0.5)
    s5_c_re = nrm((n_ab, B_GROUPS, B_GROUP_CH, B_STATE), (2 * B_STATE) ** -0.5)
    s5_c_im = nrm((n_ab, B_GROUPS, B_GROUP_CH, B_STATE), (2 * B_STATE) ** -0.5)
    s5_d = nrm((n_ab, B_GROUPS, B_GROUP_CH), 0.5)
    s5_w_glu = nrm((n_ab, B_CH, B_CH), B_CH ** -0.5)
    s5_b_glu = nrm((n_ab, B_CH), 0.02)
    w_out_ab = nrm((n_ab, MIX_WIDTH, D_MODEL), MIX_WIDTH ** -0.5)
    c_w_pw1 = nrm((n_c, D_MODEL, 2 * C_CH), D_MODEL ** -0.5)
    c_b_pw1 = nrm((n_c, 2 * C_CH), 0.02)
    c_w_dw = nrm((n_c, C_WIDTH, C_CH), C_WIDTH ** -0.5)
    c_b_dw = nrm((n_c, C_CH), 0.02)
    c_ln_w = 1.0 + nrm((n_c, C_CH), 0.02)
    c_ln_b = nrm((n_c, C_CH), 0.02)
    c_w_pw2 = nrm((n_c, C_CH, D_MODEL), C_CH ** -0.5)
    c_b_pw2 = nrm((n_c, D_MODEL), 0.02)
    w_ff1 = nrm((DEPTH, D_MODEL, D_FF), D_MODEL ** -0.5)
    w_ff2 = nrm((DEPTH, D_FF, D_MODEL), D_FF ** -0.5)
    return {"x_prompt": x_prompt, "x_sample": x_sample,
            "state_a_conv": state_a_conv, "state_a_ssm": state_a_ssm,
            "state_b_re": state_b_re, "state_b_im": state_b_im, "state_c_conv": state_c_conv,
            "norm_mix": norm_mix, "norm_ff": norm_ff, "norm_final": norm_final,
            "w_in_ab": w_in_ab, "a_conv_w": a_conv_w, "a_conv_b": a_conv_b, "a_dt_bias": a_dt_bias,
            "a_log": a_log, "a_d": a_d, "a_norm": a_norm,
            "s5_lam_re": s5_lam_re, "s5_lam_im": s5_lam_im, "s5_log_step": s5_log_step,
            "s5_b_re": s5_b_re, "s5_b_im": s5_b_im, "s5_c_re": s5_c_re, "s5_c_im": s5_c_im,
            "s5_d": s5_d, "s5_w_glu": s5_w_glu, "s5_b_glu": s5_b_glu, "w_out_ab": w_out_ab,
            "c_w_pw1": c_w_pw1, "c_b_pw1": c_b_pw1, "c_w_dw": c_w_dw, "c_b_dw": c_b_dw,
            "c_ln_w": c_ln_w, "c_ln_b": c_ln_b, "c_w_pw2": c_w_pw2, "c_b_pw2": c_b_pw2,
            "w_ff1": w_ff1, "w_ff2": w_ff2}


def reference(x_prompt, x_sample, state_a_conv, state_a_ssm, state_b_re, state_b_im, state_c_conv,
              norm_mix, norm_ff, norm_final, w_in_ab, a_conv_w, a_conv_b, a_dt_bias, a_log, a_d, a_norm,
              s5_lam_re, s5_lam_im, s5_log_step, s5_b_re, s5_b_im, s5_c_re, s5_c_im, s5_d, s5_w_glu,
              s5_b_glu, w_out_ab, c_w_pw1, c_b_pw1, c_w_dw, c_b_dw, c_ln_w, c_ln_b, c_w_pw2, c_b_pw2,
              w_ff1, w_ff2):
    p = dict(norm_mix=norm_mix, norm_ff=norm_ff, norm_final=norm_final, w_in_ab=w_in_ab,
             a_conv_w=a_conv_w, a_conv_b=a_conv_b, a_dt_bias=a_dt_bias, a_log=a_log, a_d=a_d, a_norm=a_norm,
             s5_lam_re=s5_lam_re, s5_lam_im=s5_lam_im, s5_log_step=s5_log_step, s5_b_re=s5_b_re,
             s5_b_im=s5_b_im, s5_c_re=s5_c_re, s5_c_im=s5_c_im, s5_d=s5_d, s5_w_glu=s5_w_glu,
             s5_b_glu=s5_b_glu, w_out_ab=w_out_ab, c_w_pw1=c_w_pw1, c_b_pw1=c_b_pw1, c_w_dw=c_w_dw,
             c_b_dw=c_b_dw, c_ln_w=c_ln_w, c_ln_b=c_ln_b, c_w_pw2=c_w_pw2, c_b_pw2=c_b_pw2,
             w_ff1=w_ff1, w_ff2=w_ff2)
    nb = x_prompt.shape[0]
    zero_a_conv = jnp.zeros((N_AB_LAYERS, nb, A_CONV - 1, A_CONV_DIM), x_prompt.dtype)
    zero_a_ssm = jnp.zeros((N_AB_LAYERS, nb, A_HEADS, A_HEAD_DIM, A_STATE), jnp.float32)
    zero_b = jnp.zeros((N_AB_LAYERS, nb, B_GROUPS, B_STATE), jnp.float32)
    zero_c_conv = jnp.zeros((N_C_LAYERS, nb, C_WIDTH - 1, C_CH), x_prompt.dtype)
    y_prompt, p_a_conv, p_a_ssm, p_b_re, p_b_im, p_c_conv = run_trunk(
        x_prompt, zero_a_conv, zero_a_ssm, zero_b, zero_b, zero_c_conv, p)
    y_sample, s_a_conv, s_a_ssm, s_b_re, s_b_im, s_c_conv = run_trunk(
        x_sample, state_a_conv, state_a_ssm, state_b_re, state_b_im, state_c_conv, p)
    return (y_prompt, y_sample, p_a_conv, p_a_ssm, p_b_re, p_b_im, p_c_conv,
            s_a_conv, s_a_ssm, s_b_re, s_b_im, s_c_conv)
```

```python
import math
import numpy as np
import concourse.bass as bass
import concourse.mybir as mybir
from concourse.bass_utils import run_bass_kernel_spmd

F32 = mybir.dt.float32
BF16 = mybir.dt.bfloat16
I32 = mybir.dt.int32
U8 = mybir.dt.uint8
AF = mybir.ActivationFunctionType
ALU = mybir.AluOpType
DTSIZE = {F32: 4, BF16: 2, I32: 4, U8: 1}

ENGS = ("pe", "act", "dve", "pool", "sp")
ROLL = 3000
N_DMA_SEMS = 24

D = 1024
NT = 17
NTOK = NT * 128
A_PROJ = 3088
IN_COLS = 4112
EPS = 1e-6
TWO_PI = 2.0 * math.pi


class _Op:
    __slots__ = ("eng", "emit", "deps", "is_dma", "signal", "sig", "waits", "idx", "dma_prev")

    def __init__(self, eng, emit, is_dma):
        self.eng = eng
        self.emit = emit
        self.is_dma = is_dma
        self.deps = []
        self.signal = False
        self.sig = None
        self.waits = []
        self.dma_prev = None


class Prog:
    def __init__(self, nc):
        self.nc = nc
        self.ops = []
        self.last_w = {}
        self.readers = {}
        self._sem_cms = []
        self.alias = {}

    def add(self, eng, emit, reads=(), writes=(), dma=False):
        op = _Op(eng, emit, dma)
        op.idx = len(self.ops)
        deps = set()
        for k in list(reads) + list(writes):
            for a in self.alias.get(k, ()):
                w = self.last_w.get(a)
                if w is not None:
                    deps.add(w)
                for r in self.readers.get(a, ()):
                    deps.add(r)
        for k in reads:
            w = self.last_w.get(k)
            if w is not None:
                deps.add(w)
            if k.startswith(("bank", "tbank")):
                for r in self.readers.get(k, ()):
                    if r.eng != eng:
                        deps.add(r)
        for k in writes:
            w = self.last_w.get(k)
            if w is not None:
                deps.add(w)
            for r in self.readers.get(k, ()):
                deps.add(r)
        for k in reads:
            self.readers.setdefault(k, []).append(op)
        for k in writes:
            self.last_w[k] = op
            self.readers[k] = []
        deps.discard(op)
        op.deps = [d for d in deps if not (d.eng == "pe" and eng == "pe" and not d.is_dma and not dma)]
        if dma:
            op.signal = True
        for d in op.deps:
            d.signal = True
        self.ops.append(op)
        return op

    def _new_sem(self, name):
        cm = self.nc.semaphore(name)
        s = cm.__enter__()
        self._sem_cms.append(cm)
        return s

    def finalize(self, final_keys=()):
        nc = self.nc
        fin_deps = set()
        for k in final_keys:
            w = self.last_w.get(k)
            if w is not None:
                fin_deps.add(w)
        for d in fin_deps:
            d.signal = True
        eng_sem, eng_cnt = {}, {}
        dma_sems = [self._new_sem(f"dq{i}") for i in range(N_DMA_SEMS)]
        dma_cnt = [0] * N_DMA_SEMS
        dma_last = [None] * N_DMA_SEMS
        nd = 0
        for op in self.ops:
            if not op.signal:
                continue
            if op.is_dma:
                i = nd % N_DMA_SEMS
                nd += 1
                op.dma_prev = dma_last[i]
                dma_cnt[i] += 16
                op.sig = (dma_sems[i], dma_cnt[i])
                dma_last[i] = op
            else:
                e = op.eng
                if e not in eng_sem or eng_cnt[e] >= ROLL:
                    eng_sem[e] = self._new_sem(f"s_{e}_{len(self._sem_cms)}")
                    eng_cnt[e] = 0
                eng_cnt[e] += 1
                op.sig = (eng_sem[e], eng_cnt[e])
        waited = {e: {} for e in ENGS}
        per_eng = {e: [] for e in ENGS}
        for op in self.ops:
            need = {}
            deps = list(op.deps)
            if op.is_dma and op.dma_prev is not None:
                deps.append(op.dma_prev)
            for d in deps:
                sem, val = d.sig
                key = id(sem)
                if waited[op.eng].get(key, (None, 0))[1] >= val:
                    continue
                if key not in need or need[key][1] < val:
                    need[key] = (sem, val)
            for key, sv in need.items():
                waited[op.eng][key] = sv
            op.waits = list(need.values())
            per_eng[op.eng].append(op)
        fin_waits = {}
        for d in fin_deps:
            sem, val = d.sig
            if id(sem) not in fin_waits or fin_waits[id(sem)][1] < val:
                fin_waits[id(sem)] = (sem, val)

        def run(engine_obj, lst, final=False):
            for op in lst:
                for sem, val in op.waits:
                    engine_obj.wait_ge(sem, val)
                ins = op.emit(engine_obj)
                if op.sig is not None:
                    ins.then_inc(op.sig[0], 16 if op.is_dma else 1)
            if final:
                for sem, val in fin_waits.values():
                    engine_obj.wait_ge(sem, val)

        with nc.Block() as block:
            @block.sync
            def _(e):
                run(e, per_eng["sp"], final=True)

            @block.tensor
            def _(e):
                run(e, per_eng["pe"])

            @block.scalar
            def _(e):
                run(e, per_eng["act"])

            @block.vector
            def _(e):
                run(e, per_eng["dve"])

            @block.gpsimd
            def _(e):
                run(e, per_eng["pool"])
        for cm in reversed(self._sem_cms):
            cm.__exit__(None, None, None)
        self.stats = {e: len(per_eng[e]) for e in ENGS}
        self.stats["waits"] = sum(len(o.waits) for o in self.ops)


class T:
    def __init__(self, name, ap):
        self.name = name
        self.ap = ap

    def __getitem__(self, k):
        return self.ap[k]


def _prod(s):
    r = 1
    for v in s:
        r *= v
    return r


class B:
    ARENA = 206 * 1024

    def __init__(self):
        self.nc = bass.Bass("TRN2", target_bir_lowering=False)
        nc = self.nc
        self.P = Prog(nc)
        self.arena = nc.alloc_sbuf_tensor("arena", [128, self.ARENA], U8)
        self.top = 0
        self.uid = 0
        self.banks = [T(f"bank{i}", nc.alloc_psum_tensor(f"bank{i}", [128, 512], F32)[:, :]) for i in range(6)]
        self.tbanks = [T(f"tbank{i}", nc.alloc_psum_tensor(f"tbank{i}", [128, 1024], BF16)[:, :]) for i in range(2)]
        self.dram = {}
        self.out_keys = []
        self.dbg = []
        self.regions = []

    def alloc(self, name, free_shape, dt):
        free_shape = tuple(free_shape)
        n = _prod(free_shape) * DTSIZE[dt]
        off = (self.top + 63) // 64 * 64
        self.top = off + n
        assert self.top <= self.ARENA, f"SBUF arena overflow at {name}: {self.top}"
        ap = self.arena[:, off:off + n].bitcast(dt)
        if len(free_shape) == 2:
            ap = ap.rearrange("p (a b) -> p a b", a=free_shape[0])
        elif len(free_shape) == 3:
            ap = ap.rearrange("p (a b c) -> p a b c", a=free_shape[0], b=free_shape[1])
        elif len(free_shape) == 4:
            ap = ap.rearrange("p (a b c d) -> p a b c d", a=free_shape[0], b=free_shape[1], c=free_shape[2])
        self.uid += 1
        nm = f"{name}#{self.uid}"
        al = [r[0] for r in self.regions if r[1] < off + n and off < r[2]]
        if al:
            self.P.alias[nm] = list(al)
            for o in al:
                self.P.alias.setdefault(o, []).append(nm)
        self.regions.append((nm, off, off + n))
        return T(nm, ap)

    def mark(self):
        return self.top

    def release(self, m):
        self.top = m

    def din(self, name, shape):
        t = self.nc.dram_tensor(name, list(shape), F32, kind="ExternalInput").ap()
        self.dram[name] = t
        return t

    def dout(self, name, shape):
        t = self.nc.dram_tensor(name, list(shape), F32, kind="ExternalOutput").ap()
        self.dram[name] = t
        self.out_keys.append(name)
        return t

    def mm(self, out, lhsT, rhs, start, stop, r, w):
        self.P.add("pe", lambda e: e.matmul(out, lhsT, rhs, start=start, stop=stop), r, w)

    def tr(self, out, in_, ident, r, w):
        self.P.add("pe", lambda e: e.transpose(out, in_, ident), r, w)

    def act(self, out, in_, func, r, w, bias=None, scale=None, accum=None):
        kw = {}
        if bias is not None:
            kw["bias"] = bias
        if scale is not None:
            kw["scale"] = scale
        if accum is not None:
            kw["accum_out"] = accum
        self.P.add("act", lambda e: e.activation(out, in_, func, **kw), r, w)

    def ts(self, eng, out, in0, s1, s2, op0, op1, r, w):
        if op1 is None:
            self.P.add(eng, lambda e: e.tensor_scalar(out, in0, s1, None, op0), r, w)
        else:
            self.P.add(eng, lambda e: e.tensor_scalar(out, in0, s1, s2, op0, op1), r, w)

    def tt(self, eng, out, in0, in1, op, r, w):
        self.P.add(eng, lambda e: e.tensor_tensor(out, in0, in1, op), r, w)

    def stt(self, out, in0, scalar, in1, op0, op1, r, w):
        self.P.add("dve", lambda e: e.scalar_tensor_tensor(out, in0, scalar, in1, op0, op1), r, w)

    def cp(self, eng, out, in_, r, w):
        if eng == "act":
            self.P.add("act", lambda e: e.activation(out, in_, AF.Copy), r, w)
        else:
            self.P.add(eng, lambda e: e.tensor_copy(out, in_), r, w)

    def memset(self, eng, out, val, w):
        self.P.add(eng, lambda e: e.memset(out, val), (), w)

    def scan(self, out, d0, d1, init, r, w):
        self.P.add("dve", lambda e: e.tensor_tensor_scan(out, d0, d1, init, ALU.mult, ALU.add), r, w)

    def dma(self, eng, out, in_, r, w, **kw):
        self.P.add(eng, lambda e: e.dma_start(out=out, in_=in_, **kw), r, w, dma=True)

    def recip(self, out, in_, r, w):
        self.P.add("dve", lambda e: e.reciprocal(out, in_), r, w)

    def debug_dump(self, name, t, shape, dt=F32):
        d = self.nc.dram_tensor("dbg_" + name, list(shape), dt, kind="ExternalOutput").ap()
        self.out_keys.append("dbg_" + name)
        self.dma("sp", d, t.ap if isinstance(t, T) else t, [t.name] if isinstance(t, T) else [], ["dbg_" + name])


def build(stop_after=None, debug=False, cut=None, skip=()):
    b = B()
    nc, P = b.nc, b.P
    NCD = dict(allow_slow_non_contiguous=True)

    xp = b.din("xp", [2048, D])
    xs = b.din("xs", [128, D])
    sa_conv = b.din("sa_conv", [48, 2048])
    sa_ssm = b.din("sa_ssm", [16, 1024, 128])
    sb_re = b.din("sb_re", [16, 4096])
    sb_im = b.din("sb_im", [16, 4096])
    sc_conv = b.din("sc_conv", [16, 30, D])
    norm_mix = b.din("norm_mix", [2, D])
    norm_ff = b.din("norm_ff", [2, D])
    norm_final = b.din("norm_final", [1, D])
    w_in = b.din("w_in", [D, IN_COLS])
    a_conv_w = b.din("a_conv_w", [4, 2048])
    a_conv_b = b.din("a_conv_b", [1, 2048])
    a_dt_bias = b.din("a_dt_bias", [1, 16])
    a_log = b.din("a_log", [1, 16])
    a_d = b.din("a_d", [1, 16])
    a_norm = b.din("a_norm", [1, 1024])
    lam_re = b.din("lam_re", [64, 64])
    lam_im = b.din("lam_im", [64, 64])
    log_step = b.din("log_step", [1, 64])
    s5_b_re = b.din("s5_b_re", [64, 64, 16])
    s5_b_im = b.din("s5_b_im", [64, 64, 16])
    s5_c_re = b.din("s5_c_re", [64, 16, 64])
    s5_c_im = b.din("s5_c_im", [64, 16, 64])
    s5_d = b.din("s5_d", [1, 1024])
    w_glu = b.din("w_glu", [1024, 1024])
    b_glu = b.din("b_glu", [1, 1024])
    w_out = b.din("w_out", [2048, D])
    c_w_pw1 = b.din("c_w_pw1", [D, 2048])
    c_b_pw1 = b.din("c_b_pw1", [1, 2048])
    c_w_dw = b.din("c_w_dw", [31, D])
    c_b_dw = b.din("c_b_dw", [1, D])
    c_ln_w = b.din("c_ln_w", [1, D])
    c_ln_b = b.din("c_ln_b", [1, D])
    c_w_pw2 = b.din("c_w_pw2", [D, D])
    c_b_pw2 = b.din("c_b_pw2", [1, D])
    w_ff1 = b.din("w_ff1", [2, D, 4096])
    w_ff2 = b.din("w_ff2", [2, 4096, D])

    y_p = b.dout("y_p", [2048, D])
    y_s = b.dout("y_s", [128, D])
    p_a_conv = b.dout("p_a_conv", [3, 2048])
    p_a_ssm = b.dout("p_a_ssm", [1024, 128])
    p_b_re = b.dout("p_b_re", [32, 128])
    p_b_im = b.dout("p_b_im", [32, 128])
    p_c_conv = b.dout("p_c_conv", [30, D])
    s_a_conv = b.dout("s_a_conv", [48, 2048])
    s_a_ssm = b.dout("s_a_ssm", [16, 1024, 128])
    s_b_re = b.dout("s_b_re", [16, 4096])
    s_b_im = b.dout("s_b_im", [16, 4096])
    s_c_conv = b.dout("s_c_conv", [16, 30, D])

    def x_src(t):
        return xp[t * 128:(t + 1) * 128, :] if t < 16 else xs

    bank = b.banks
    tbank = b.tbanks

    io = b.alloc("io", [128], I32)
    identf = b.alloc("identf", [128], F32)
    identb = b.alloc("identb", [128], BF16)
    tri_p = b.alloc("tri_p", [128], F32)
    tri_s = b.alloc("tri_s", [128], F32)
    blk_s = b.alloc("blk_s", [128], F32)
    ones = b.alloc("ones", [128], F32)
    neg_p4 = b.alloc("neg_p4", [4, 128], BF16)
    neg_s4 = b.alloc("neg_s4", [4, 128], BF16)
    seqmask = b.alloc("seqmask", [16], F32)
    lmask = b.alloc("lmask", [16, 8], F32)
    tau1 = b.alloc("tau1", [128], F32)
    cm = b.mark()
    tmpi = b.alloc("tmpi", [128], I32)
    tmpf = b.alloc("tmpf", [128], F32)
    tmpg = b.alloc("tmpg", [128], F32)
    P.add("pool", lambda e: e.iota(io.ap, [[1, 128]], base=0, channel_multiplier=-1), (), [io.name])
    b.ts("dve", identf.ap, io.ap, 0, None, ALU.is_equal, None, [io.name], [identf.name])
    b.cp("dve", identb.ap, identf.ap, [identf.name], [identb.name])
    b.ts("dve", tri_p.ap, io.ap, 0, None, ALU.is_ge, None, [io.name], [tri_p.name])
    b.memset("pool", ones.ap, 1.0, [ones.name])
    b.ts("dve", tmpf.ap, io.ap, 0, -32768.0, ALU.is_lt, ALU.mult, [io.name], [tmpf.name])
    b.cp("dve", neg_p4.ap, tmpf.ap.unsqueeze(1).to_broadcast([128, 4, 128]), [tmpf.name], [neg_p4.name])
    P.add("pool", lambda e: e.iota(tmpi.ap[:, 0:16], [[8, 16]], base=0, channel_multiplier=-1), (), [tmpi.name])
    b.ts("dve", tmpf.ap[:, 0:16], tmpi.ap[:, 0:16], 0, None, ALU.is_le, None, [tmpi.name], [tmpf.name])
    b.ts("dve", tmpg.ap[:, 0:16], tmpi.ap[:, 0:16], -7, None, ALU.is_ge, None, [tmpi.name], [tmpg.name])
    b.tt("dve", seqmask.ap, tmpf.ap[:, 0:16], tmpg.ap[:, 0:16], ALU.mult, [tmpf.name, tmpg.name], [seqmask.name])
    b.cp("dve", blk_s.ap.rearrange("p (s l) -> p s l", s=16), seqmask.ap.unsqueeze(2).to_broadcast([128, 16, 8]),
         [seqmask.name], [blk_s.name])
    b.tt("dve", tri_s.ap, tri_p.ap, blk_s.ap, ALU.mult, [tri_p.name, blk_s.name], [tri_s.name])
    b.ts("dve", tmpf.ap, tri_s.ap, -1.0, 32768.0, ALU.add, ALU.mult, [tri_s.name], [tmpf.name])
    b.cp("dve", neg_s4.ap, tmpf.ap.unsqueeze(1).to_broadcast([128, 4, 128]), [tmpf.name], [neg_s4.name])
    P.add("pool", lambda e: e.iota(tmpi.ap, [[0, 16], [1, 8]], base=0, channel_multiplier=0), (), [tmpi.name])
    b.ts("dve", lmask.ap.rearrange("p s l -> p (s l)"), tmpi.ap, 0, None, ALU.is_gt, None, [tmpi.name], [lmask.name])
    P.add("pool", lambda e: e.iota(tmpi.ap, [[1, 128]], base=1, channel_multiplier=0), (), [tmpi.name])
    b.cp("dve", tau1.ap, tmpi.ap, [tmpi.name], [tau1.name])
    b.release(cm)

    def load_cols(name, src_row, ncols):
        t = b.alloc(name, [ncols], F32)
        b.dma("sp", t.ap, src_row.rearrange("o (k p) -> p (o k)", p=128), [], [t.name], **NCD)
        return t

    def load_bcast(name, src_row, n):
        t = b.alloc(name, [n], F32)
        b.dma("sp", t.ap, src_row.partition_broadcast(128), [], [t.name])
        return t

    nmix0 = load_cols("nmix0", norm_mix[0:1, :], 8)
    nmix1 = load_cols("nmix1", norm_mix[1:2, :], 8)
    nff0 = load_cols("nff0", norm_ff[0:1, :], 8)
    nff1 = load_cols("nff1", norm_ff[1:2, :], 8)

    def norm_tile(xt_ap, xt_keys, wcol, dst_ap, dst_key, scr):
        junk, ss, xsb = scr
        b.act(junk.ap, xt_ap, AF.Square, xt_keys, [junk.name, ss.name], accum=ss.ap[:, 0:1])
        b.act(ss.ap[:, 1:2], ss.ap[:, 0:1], AF.Sqrt, [ss.name], [ss.name], bias=EPS, scale=1.0 / D)
        b.recip(ss.ap[:, 2:3], ss.ap[:, 1:2], [ss.name], [ss.name])
        b.ts("dve", xsb.ap, xt_ap, ss.ap[:, 2:3], None, ALU.mult, None, xt_keys + [ss.name], [xsb.name])
        if cut == 81:
            return
        tb = tbank[norm_tile.i % 2]
        norm_tile.i += 1
        for k in range(8):
            b.tr(tb.ap[:, k * 128:(k + 1) * 128], xsb.ap[:, k * 128:(k + 1) * 128], identb.ap,
                 [xsb.name, identb.name], [tb.name])
        if cut == 82:
            return
        b.tt("dve", dst_ap, tb.ap.rearrange("p (k t) -> p k t", k=8),
             wcol.ap.unsqueeze(2).to_broadcast([128, 8, 128]), ALU.mult, [tb.name, wcol.name], [dst_key])
    norm_tile.i = 0

    def norm_scratch():
        return (b.alloc("junk", [1024], BF16), b.alloc("ss", [4], F32), b.alloc("xsb", [1024], BF16))

    GROUPS = [(0, 4), (4, 4), (8, 4), (12, 4), (16, 1)]

    YB = b.alloc("YB", [8, NTOK], BF16)

    if "B" not in skip:
        mB = b.mark()
        LR = b.alloc("LR", [32], F32)
        LI = b.alloc("LI", [32], F32)
        ST = b.alloc("ST", [32], F32)
        MG = b.alloc("MG", [32], F32)
        TH = b.alloc("TH", [32], F32)
        KR = b.alloc("KR", [32], F32)
        KI = b.alloc("KI", [32], F32)
        COS = b.alloc("COS", [32, 128], F32)
        SIN = b.alloc("SIN", [32, 128], F32)
        BTz = b.alloc("BTz", [32, 2, 128], BF16)
        CTz = b.alloc("CTz", [32, 2, 128], BF16)
        dskip = load_cols("dskip", s5_d, 8)
        bglu = load_cols("bglu", b_glu, 8)
        for gl in range(2):
            rows = slice(64 * gl, 64 * gl + 64)
            b.dma("sp", LR.ap[rows, :], lam_re.rearrange("(gp gl) p -> gl p gp", gl=2)[gl], [], [LR.name], **NCD)
            b.dma("sp", LI.ap[rows, :], lam_im.rearrange("(gp gl) p -> gl p gp", gl=2)[gl], [], [LI.name], **NCD)
            b.dma("sp", ST.ap[rows, :], log_step.rearrange("o (gp gl) -> gl o gp", gl=2)[gl].partition_broadcast(64),
                  [], [ST.name], **NCD)

        def sin_rr(out_ap, in_ap, shift, t1, t2, keys_r, key_w, shape_kw=None):
            rk = keys_r
            b.ts("dve", out_ap, in_ap, 1.0 / TWO_PI, shift / TWO_PI, ALU.mult, ALU.add, rk, [key_w])
            b.cp("dve", t1.ap, out_ap, [key_w], [t1.name])
            b.cp("dve", t2.ap, t1.ap, [t1.name], [t2.name])
            b.ts("dve", out_ap, in_ap, shift, None, ALU.add, None, rk + [t1.name], [key_w])
            b.stt(out_ap, t2.ap, -TWO_PI, out_ap, ALU.mult, ALU.add, [t2.name, key_w], [key_w])
            b.ts("dve", t2.ap, out_ap, math.pi, -TWO_PI, ALU.is_gt, ALU.mult, [key_w], [t2.name])
            b.tt("dve", out_ap, out_ap, t2.ap, ALU.add, [key_w, t2.name], [key_w])
            b.ts("dve", t2.ap, out_ap, -math.pi, TWO_PI, ALU.is_lt, ALU.mult, [key_w], [t2.name])
            b.tt("dve", out_ap, out_ap, t2.ap, ALU.add, [key_w, t2.name], [key_w])
            b.act(out_ap, out_ap, AF.Sin, [key_w], [key_w])

        if cut == 1:
            P.finalize(final_keys=b.out_keys)
            return b
        mT = b.mark()
        ABR = b.alloc("ABR", [32], F32)
        ABI = b.alloc("ABI", [32], F32)
        w1 = b.alloc("w1", [32], F32)
        w2 = b.alloc("w2", [32], F32)
        w3 = b.alloc("w3", [32], F32)
        wi = b.alloc("wi", [32], I32)
        b.act(ST.ap, ST.ap, AF.Exp, [ST.name], [ST.name])
        b.tt("dve", w1.ap, LR.ap, ST.ap, ALU.mult, [LR.name, ST.name], [w1.name])
        b.act(MG.ap, w1.ap, AF.Exp, [w1.name], [MG.name])
        b.tt("dve", TH.ap, LI.ap, ST.ap, ALU.mult, [LI.name, ST.name], [TH.name])
        sin_rr(ABR.ap, TH.ap, math.pi / 2, wi, w2, [TH.name], ABR.name)
        sin_rr(ABI.ap, TH.ap, 0.0, wi, w2, [TH.name], ABI.name)
        b.tt("dve", ABR.ap, ABR.ap, MG.ap, ALU.mult, [ABR.name, MG.name], [ABR.name])
        b.tt("dve", ABI.ap, ABI.ap, MG.ap, ALU.mult, [ABI.name, MG.name], [ABI.name])
        b.tt("dve", w1.ap, LR.ap, LR.ap, ALU.mult, [LR.name], [w1.name])
        b.tt("dve", w2.ap, LI.ap, LI.ap, ALU.mult, [LI.name], [w2.name])
        b.tt("dve", w1.ap, w1.ap, w2.ap, ALU.add, [w1.name, w2.name], [w1.name])
        b.recip(w3.ap, w1.ap, [w1.name], [w3.name])
        b.ts("dve", w1.ap, ABR.ap, -1.0, None, ALU.add, None, [ABR.name], [w1.name])
        b.tt("dve", KR.ap, w1.ap, LR.ap, ALU.mult, [w1.name, LR.name], [KR.name])
        b.tt("dve", w2.ap, ABI.ap, LI.ap, ALU.mult, [ABI.name, LI.name], [w2.name])
        b.tt("dve", KR.ap, KR.ap, w2.ap, ALU.add, [KR.name, w2.name], [KR.name])
        b.tt("dve", KR.ap, KR.ap, w3.ap, ALU.mult, [KR.name, w3.name], [KR.name])
        b.tt("dve", KI.ap, ABI.ap, LR.ap, ALU.mult, [ABI.name, LR.name], [KI.name])
        b.tt("dve", w2.ap, w1.ap, LI.ap, ALU.mult, [w1.name, LI.name], [w2.name])
        b.tt("dve", KI.ap, KI.ap, w2.ap, ALU.subtract, [KI.name, w2.name], [KI.name])
        b.tt("dve", KI.ap, KI.ap, w3.ap, ALU.mult, [KI.name, w3.name], [KI.name])
        b.release(mT)
        if cut == 2:
            P.finalize(final_keys=b.out_keys)
            return b
        mT = b.mark()
        ti = b.alloc("ti", [8, 128], I32)
        tf = b.alloc("tf", [8, 128], F32)
        ang = b.alloc("ang", [8, 128], F32)
        for s8 in range(4):
            sl = slice(8 * s8, 8 * s8 + 8)
            b.tt("dve", ang.ap, TH.ap[:, sl].unsqueeze(2).to_broadcast([128, 8, 128]),
                 tau1.ap.unsqueeze(1).to_broadcast([128, 8, 128]), ALU.mult, [TH.name, tau1.name], [ang.name])
            sin_rr(COS.ap[:, sl, :], ang.ap, math.pi / 2, ti, tf, [ang.name], COS.name)
            sin_rr(SIN.ap[:, sl, :], ang.ap, 0.0, ti, tf, [ang.name], SIN.name)
        b.release(mT)
        if cut == 3:
            P.finalize(final_keys=b.out_keys)
            return b
        mT = b.mark()
        BR = b.alloc("BR", [32, 16], F32)
        BI = b.alloc("BI", [32, 16], F32)
        BBR = b.alloc("BBR", [32, 16], F32)
        BBI = b.alloc("BBI", [32, 16], F32)
        BX = b.alloc("BX", [32, 128], F32)
        for gl in range(2):
            rows = slice(64 * gl, 64 * gl + 64)
            b.dma("sp", BR.ap[rows], s5_b_re.rearrange("(gp gl) p c -> gl p gp c", gl=2)[gl], [], [BR.name])
            b.dma("sp", BI.ap[rows], s5_b_im.rearrange("(gp gl) p c -> gl p gp c", gl=2)[gl], [], [BI.name])
        krb = KR.ap.unsqueeze(2).to_broadcast([128, 32, 16])
        kib = KI.ap.unsqueeze(2).to_broadcast([128, 32, 16])
        b.tt("dve", BBR.ap, BR.ap, krb, ALU.mult, [BR.name, KR.name], [BBR.name])
        b.tt("dve", BX.ap[:, :, 0:16], BI.ap, kib, ALU.mult, [BI.name, KI.name], [BX.name])
        b.tt("dve", BBR.ap, BBR.ap, BX.ap[:, :, 0:16], ALU.subtract, [BBR.name, BX.name], [BBR.name])
        b.tt("dve", BBI.ap, BI.ap, krb, ALU.mult, [BI.name, KR.name], [BBI.name])
        b.tt("dve", BX.ap[:, :, 0:16], BR.ap, kib, ALU.mult, [BR.name, KI.name], [BX.name])
        b.tt("dve", BBI.ap, BBI.ap, BX.ap[:, :, 0:16], ALU.add, [BBI.name, BX.name], [BBI.name])
        for ri, src in enumerate((BBR, BBI)):
            b.memset("pool", BX.ap, 0.0, [BX.name])
            bxv = BX.ap.rearrange("p (q j) c -> p q j c", j=4)
            srcv = src.ap.rearrange("p (q j) c -> p q j c", j=4)
            for gl in range(2):
                rows = slice(64 * gl, 64 * gl + 64)
                for j in range(4):
                    c0 = 32 * j + 16 * gl
                    b.cp("dve", bxv[rows, :, j, c0:c0 + 16], srcv[rows, :, j, :], [src.name, BX.name], [BX.name])
            for gp in range(32):
                bk = bank[gp % 2]
                b.tr(bk.ap[:, 0:128], BX.ap[:, gp, :], identf.ap, [BX.name, identf.name], [bk.name])
                b.cp("act", BTz.ap[:, gp, ri, :], bk.ap[:, 0:128], [bk.name], [BTz.name])
        b.release(mT)
        if cut == 4:
            P.finalize(final_keys=b.out_keys)
            return b
        mT = b.mark()
        CR = b.alloc("CR", [32, 16], F32)
        b.memset("pool", CTz.ap, 0.0, [CTz.name])
        for ri, src in enumerate((s5_c_re, s5_c_im)):
            for gl in range(2):
                rows = slice(64 * gl, 64 * gl + 64)
                srcv = src.rearrange("(gp gl) c p -> gl c p gp", gl=2)
                for c in range(16):
                    b.dma("sp" if c % 2 else "act", CR.ap[rows, :, c], srcv[gl, c], [], [CR.name], **NCD)
            ctv = CTz.ap.rearrange("p (q j) r c -> p q j r c", j=4)
            crv = CR.ap.rearrange("p (q j) c -> p q j c", j=4)
            for gl in range(2):
                rows = slice(64 * gl, 64 * gl + 64)
                for j in range(4):
                    c0 = 32 * j + 16 * gl
                    b.ts("dve", ctv[rows, :, j, ri, c0:c0 + 16], crv[rows, :, j, :], (1.0 if ri == 0 else -1.0), None,
                         ALU.mult, None, [CR.name, CTz.name], [CTz.name])
        b.release(mT)

        if cut == 5:
            P.finalize(final_keys=b.out_keys)
            return b
        WU = b.alloc("WU", [8, 1024], BF16)
        WG = b.alloc("WG", [8, 1024], BF16)
        for k in range(8):
            b.dma("pool", WU.ap[:, k, :], w_in[k * 128:(k + 1) * 128, A_PROJ:IN_COLS], [], [WU.name])
            b.dma("pool", WG.ap[:, k, :], w_glu[k * 128:(k + 1) * 128, :], [], [WG.name])

        if cut == 6:
            P.finalize(final_keys=b.out_keys)
            return b
        CAR = [b.alloc("CARr", [32], F32), b.alloc("CARi", [32], F32)]
        H0M = [b.alloc("H0Mr", [32, 16], F32), b.alloc("H0Mi", [32, 16], F32)]
        SOUT = [b.alloc("SOUTr", [32, 16], F32), b.alloc("SOUTi", [32, 16], F32)]
        b.memset("pool", CAR[0].ap, 0.0, [CAR[0].name])
        b.memset("pool", CAR[1].ap, 0.0, [CAR[1].name])
        mT = b.mark()
        h0n = b.alloc("h0n", [4096], F32)
        for ri, src in enumerate((sb_re, sb_im)):
            b.dma("sp", h0n.ap[0:16, :], src, [], [h0n.name])
            for gp in range(32):
                bk = bank[gp % 2]
                b.tr(bk.ap[:, 0:16], h0n.ap[0:16, gp * 128:(gp + 1) * 128], identf.ap[0:16, 0:16],
                     [h0n.name, identf.name], [bk.name])
                b.ts("dve", H0M[ri].ap[:, gp, :], bk.ap[:, 0:16], MG.ap[:, gp:gp + 1], None, ALU.mult, None,
                     [bk.name, MG.name], [H0M[ri].name])
        b.release(mT)

        if cut == 7:
            P.finalize(final_keys=b.out_keys)
            return b
        if stop_after == "B0":
            for nm, t_, shp, *dt_ in (("COS", COS, [128, 32, 128]), ("SIN", SIN, [128, 32, 128]), ("BTz", BTz, [128, 32, 2, 128], BF16),
                                ("CTz", CTz, [128, 32, 2, 128], BF16), ("KR", KR, [128, 32]), ("KI", KI, [128, 32]),
                                ("MG", MG, [128, 32]), ("H0Mr", H0M[0], [128, 32, 16])):
                b.debug_dump(nm, t_, shp, *dt_)
            P.finalize(final_keys=b.out_keys)
            return b
        mS = b.mark()
        scrN = norm_scratch()
        XT = [b.alloc("XT0", [1024], F32)]
        XNg = b.alloc("XNg", [8, 256], BF16)
        U16 = b.alloc("U16", [8, 256], BF16)
        U32 = b.alloc("U32", [8, 256], F32)
        MSq = b.alloc("MSq", [4, 128], F32)
        RR = b.alloc("RR", [512], F32)
        RI = b.alloc("RI", [512], F32)
        Q1 = b.alloc("Q1", [512], F32)
        Q2 = b.alloc("Q2", [512], F32)
        HR = b.alloc("HR", [512], F32)
        HI = b.alloc("HI", [512], F32)
        HRf = b.alloc("HRf", [512], F32)
        HIf = b.alloc("HIf", [512], F32)
        HRb = b.alloc("HRb", [512], BF16)
        HIb = b.alloc("HIb", [512], BF16)
        Y32 = b.alloc("Y32", [128], F32)
        G1 = b.alloc("G1", [128], F32)
        G2 = b.alloc("G2", [128], F32)
        YG32 = b.alloc("YG32", [8, 128], F32)
        YGb = b.alloc("YGb", [8, 128], BF16)
        SG = b.alloc("SG", [128], F32)
        xi = 0
        for (t0, ntile) in [(2 * i, 2) for i in range(8)] + [(16, 1)]:
            n = ntile * 128
            for tt_ in range(ntile):
                t = t0 + tt_
                xt = XT[0]
                xi += 1
                b.dma("sp", xt.ap, x_src(t), [], [xt.name])
                norm_tile(xt.ap, [xt.name], nmix0, XNg.ap[:, :, tt_ * 128:(tt_ + 1) * 128], XNg.name, scrN)
            if cut in (81, 82, 83):
                P.finalize(final_keys=b.out_keys)
                return b
            for q in range(8):
                bk = bank[2 + q % 2]
                for k in range(8):
                    b.mm(bk.ap[:, 0:n], WU.ap[:, k, q * 128:(q + 1) * 128], XNg.ap[:, k, 0:n], k == 0, k == 7,
                         [WU.name, XNg.name], [bk.name])
                if cut == 84:
                    P.finalize(final_keys=b.out_keys)
                    return b
                b.cp("act", U16.ap[:, q, 0:n], bk.ap[:, 0:n], [bk.name], [U16.name])
                if cut == 85:
                    P.finalize(final_keys=b.out_keys)
                    return b
                b.cp("dve", U32.ap[:, q, 0:n], bk.ap[:, 0:n], [bk.name], [U32.name])
                if cut == 86:
                    P.finalize(final_keys=b.out_keys)
                    return b
            if cut == 10:
                P.finalize(final_keys=b.out_keys)
                return b
            for tt_ in range(ntile):
                t = t0 + tt_
                sample = (t == 16)
                cs = slice(tt_ * 128, (tt_ + 1) * 128)
                tok = slice(t * 128, (t + 1) * 128)
                for q in range(8):
                    pr = slice(4 * q, 4 * q + 4)
                    for ri in range(2):
                        bk = bank[ri]
                        for j in range(4):
                            b.mm(bk.ap[:, j * 128:(j + 1) * 128], BTz.ap[:, 4 * q + j, ri, :], U16.ap[:, q, cs], True, True,
                                 [BTz.name, U16.name], [bk.name])
                    if sample:
                        cosq = COS.ap[:, pr, 0:8].unsqueeze(2).to_broadcast([128, 4, 16, 8])
                        sinq = SIN.ap[:, pr, 0:8].unsqueeze(2).to_broadcast([128, 4, 16, 8])
                        v = lambda tile_: tile_.ap.rearrange("p (j s l) -> p j s l", j=4, s=16)
                    else:
                        cosq = COS.ap[:, pr, :]
                        sinq = SIN.ap[:, pr, :]
                        v = lambda tile_: tile_.ap.rearrange("p (j t) -> p j t", j=4)
                    if sample:
                        b.tt("dve", MSq.ap, MG.ap[:, pr].unsqueeze(2).to_broadcast([128, 4, 128]),
                             lmask.ap.rearrange("p s l -> p (s l)").unsqueeze(1).to_broadcast([128, 4, 128]), ALU.mult,
                             [MG.name, lmask.name], [MSq.name])
                    A_, B_ = v(bank[0]), v(bank[1])
                    ck = [COS.name, SIN.name]
                    b.tt("dve", v(Q1), A_, cosq, ALU.mult, [bank[0].name] + ck, [Q1.name])
                    b.tt("dve", v(Q2), B_, sinq, ALU.mult, [bank[1].name] + ck, [Q2.name])
                    b.tt("pool", v(RR), v(Q1), v(Q2), ALU.add, [Q1.name, Q2.name], [RR.name])
                    b.tt("dve", v(Q1), B_, cosq, ALU.mult, [bank[1].name] + ck, [Q1.name])
                    b.tt("dve", v(Q2), A_, sinq, ALU.mult, [bank[0].name] + ck, [Q2.name])
                    b.tt("pool", v(RI), v(Q1), v(Q2), ALU.subtract, [Q1.name, Q2.name], [RI.name])
                    if cut == 11:
                        P.finalize(final_keys=b.out_keys)
                        return b
                    for ri, (rt, ht) in enumerate(((RR, HR), (RI, HI))):
                        if sample:
                            b.tt("dve", v(rt)[:, :, :, 0], v(rt)[:, :, :, 0], H0M[ri].ap[:, pr, :], ALU.add,
                                 [rt.name, H0M[ri].name], [rt.name])
                        for j in range(4):
                            gp = 4 * q + j
                            if sample:
                                d0 = MSq.ap[:, j, :]
                                init = 0.0
                                rk = [rt.name, MSq.name]
                            else:
                                d0 = MG.ap[:, gp:gp + 1].to_broadcast([128, 128])
                                init = CAR[ri].ap[:, gp:gp + 1]
                                rk = [rt.name, MG.name, CAR[ri].name]
                            b.scan(ht.ap[:, j * 128:(j + 1) * 128], d0, rt.ap[:, j * 128:(j + 1) * 128], init,
                                   rk, [ht.name])
                    if cut == 12:
                        P.finalize(final_keys=b.out_keys)
                        return b
                    b.tt("pool", v(Q1), v(HR), cosq, ALU.mult, [HR.name] + ck, [Q1.name])
                    b.tt("pool", v(Q2), v(HI), sinq, ALU.mult, [HI.name] + ck, [Q2.name])
                    b.tt("pool", v(HRf), v(Q1), v(Q2), ALU.subtract, [Q1.name, Q2.name], [HRf.name])
                    b.tt("pool", v(Q1), v(HR), sinq, ALU.mult, [HR.name] + ck, [Q1.name])
                    b.tt("pool", v(Q2), v(HI), cosq, ALU.mult, [HI.name] + ck, [Q2.name])
                    b.tt("pool", v(HIf), v(Q1), v(Q2), ALU.add, [Q1.name, Q2.name], [HIf.name])
                    b.cp("act", HRb.ap, HRf.ap, [HRf.name], [HRb.name])
                    b.cp("act", HIb.ap, HIf.ap, [HIf.name], [HIb.name])
                    for ri, hf in enumerate((HRf, HIf)):
                        if sample:
                            b.cp("act", SOUT[ri].ap[:, pr, :], v(hf)[:, :, :, 7], [hf.name], [SOUT[ri].name])
                        else:
                            b.cp("act", CAR[ri].ap[:, pr], v(hf)[:, :, 127], [hf.name], [CAR[ri].name])
                    if cut == 13:
                        P.finalize(final_keys=b.out_keys)
                        return b
                    bk = bank[4]
                    i = 0
                    for j in range(4):
                        for ri, hb in enumerate((HRb, HIb)):
                            b.mm(bk.ap[:, 0:128], CTz.ap[:, 4 * q + j, ri, :], hb.ap[:, j * 128:(j + 1) * 128], i == 0, i == 7,
                                 [CTz.name, hb.name], [bk.name])
                            i += 1
                    b.stt(Y32.ap, U32.ap[:, q, cs], dskip.ap[:, q:q + 1], bk.ap[:, 0:128], ALU.mult, ALU.add,
                          [U32.name, dskip.name, bk.name], [Y32.name])
                    b.act(G1.ap, Y32.ap, AF.Square, [Y32.name], [G1.name])
                    b.ts("dve", G1.ap, G1.ap, 0.044715, 1.0, ALU.mult, ALU.add, [G1.name], [G1.name])
                    b.tt("dve", G1.ap, G1.ap, Y32.ap, ALU.mult, [G1.name, Y32.name], [G1.name])
                    b.act(G2.ap, G1.ap, AF.Sigmoid, [G1.name], [G2.name], scale=1.5957691216057308)
                    b.tt("dve", YG32.ap[:, q, :], Y32.ap, G2.ap, ALU.mult, [Y32.name, G2.name], [YG32.name])
                    b.cp("act", YGb.ap[:, q, :], YG32.ap[:, q, :], [YG32.name], [YGb.name])
                if cut == 14:
                    P.finalize(final_keys=b.out_keys)
                    return b
                for o in range(8):
                    bk = bank[5]
                    for k in range(8):
                        b.mm(bk.ap[:, 0:128], WG.ap[:, k, o * 128:(o + 1) * 128], YGb.ap[:, k, :], k == 0, k == 7,
                             [WG.name, YGb.name], [bk.name])
                    b.act(SG.ap, bk.ap[:, 0:128], AF.Sigmoid, [bk.name, bglu.name], [SG.name], bias=bglu.ap[:, o:o + 1])
                    b.tt("dve", YB.ap[:, o, tok], YG32.ap[:, o, :], SG.ap, ALU.mult, [YG32.name, SG.name], [YB.name])
        b.release(mS)
        mT = b.mark()
        so = b.alloc("so", [4096], F32)
        for ri, (dst_s, dst_p) in enumerate(((s_b_re, p_b_re), (s_b_im, p_b_im))):
            for gp in range(32):
                bk = bank[gp % 2]
                b.tr(bk.ap[0:16, 0:128], SOUT[ri].ap[:, gp, :], identf.ap, [SOUT[ri].name, identf.name], [bk.name])
                b.cp("act", so.ap[0:16, gp * 128:(gp + 1) * 128], bk.ap[0:16, 0:128], [bk.name], [so.name])
            b.dma("sp", dst_s, so.ap[0:16, :], [so.name], ["s_b_re" if ri == 0 else "s_b_im"])
            bk = bank[2]
            b.tr(bk.ap[0:32, 0:128], CAR[ri].ap, identf.ap, [CAR[ri].name, identf.name], [bk.name])
            b.cp("act", so.ap[0:32, 0:128], bk.ap[0:32, 0:128], [bk.name], [so.name])
            b.dma("sp", dst_p, so.ap[0:32, 0:128], [so.name], ["p_b_re" if ri == 0 else "p_b_im"])
        b.release(mT)
        if debug:
            b.debug_dump("YB", YB, [128, 8, NTOK], BF16)
        b.release(mB)
    if stop_after == "B":
        P.finalize(final_keys=b.out_keys)
        return b


    YA = b.alloc("YA", [8, NTOK], BF16)
    if "A" not in skip:
        mA = b.mark()
        WA = b.alloc("WA", [8, A_PROJ], BF16)
        for k in range(8):
            b.dma("pool", WA.ap[:, k, :], w_in[k * 128:(k + 1) * 128, 0:A_PROJ], [], [WA.name])
        cw = b.alloc("cw", [16, 4], F32)
        for w_ in range(4):
            b.dma("sp", cw.ap[:, :, w_], a_conv_w[w_:w_ + 1, :].rearrange("o (c p) -> p (o c)", p=128), [], [cw.name], **NCD)
        cbias = load_cols("cbias", a_conv_b, 16)
        dtb = load_bcast("dtb", a_dt_bias, 16)
        abc = load_bcast("abc", a_log, 16)
        dbc = load_bcast("dbc", a_d, 16)
        anorm = load_bcast("anorm", a_norm, 1024)
        b.act(abc.ap, abc.ap, AF.Exp, [abc.name], [abc.name])
        b.ts("dve", abc.ap, abc.ap, -1.0, None, ALU.mult, None, [abc.name], [abc.name])
        SEL = b.alloc("SEL", [16, 128], BF16)
        mT = b.mark()
        seli = b.alloc("seli", [16, 128], I32)
        P.add("pool", lambda e: e.iota(seli.ap, [[-1, 16], [1, 16], [0, 8]], base=0, channel_multiplier=0), (), [seli.name])
        b.ts("dve", SEL.ap, seli.ap, 0, None, ALU.is_equal, None, [seli.name], [SEL.name])
        b.release(mT)

        scrN = norm_scratch()
        mT = b.mark()
        XT0 = b.alloc("XT0a", [1024], F32)
        b.release(mT)
        ZS = b.alloc("ZS", [1024], F32)
        XNg = b.alloc("XNgA", [8, 256], BF16)
        XPRE = b.alloc("XPRE", [16, 259], F32)
        XC = b.alloc("XCv", [16, 256], BF16)
        XCa = b.alloc("XCa", [256], F32)
        XTM = b.alloc("XTM", [1024], BF16)
        BTM = b.alloc("BTM", [512], BF16)
        sm = b.alloc("sm", [16, 16], F32)
        Rt = b.alloc("Rt", [4, 128], F32)
        Lt = b.alloc("Lt", [4, 128], F32)
        MT = b.alloc("MT", [16, 128], BF16)
        XDT = b.alloc("XDT", [1024], BF16)
        XDE = b.alloc("XDE", [1024], BF16)
        mT = b.mark()
        HIST = b.alloc("HIST", [2048], F32)
        b.release(mT)
        H0 = b.alloc("H0", [8, 128], F32)
        NS = b.alloc("NS", [8, 128], F32)
        b.release(mT)
        T1 = b.alloc("T1", [1024], F32)
        T2 = b.alloc("T2", [1024], F32)
        YAt = b.alloc("YAt", [1024], BF16)
        STf = b.alloc("STf", [1024], F32)
        STb = b.alloc("STb", [1024], BF16)
        STs = b.alloc("STs", [1024], BF16)
        CTm = b.alloc("CTm", [4, 128], BF16)
        Bm = b.alloc("Bm", [512], BF16)
        CDT = b.alloc("CDT", [8, 16], F32)
        c48 = b.alloc("c48", [48], F32)
        b.memset("pool", STf.ap, 0.0, [STf.name])
        b.memset("pool", STb.ap, 0.0, [STb.name])
        b.memset("pool", XPRE.ap, 0.0, [XPRE.name])
        (R_PRE, R_ABS, R_E, R_L, R_DT, R_DA, R_CUM, R_NCUM, R_DEND, R_ECUM, R_CD, R_DTD, R_MS, R_RS) = range(14)

        def smr(i, n=16):
            return sm.ap[:, i, 0:n]

        for (t0, ntile) in [(2 * i, 2) for i in range(8)] + [(16, 1)]:
            n = ntile * 128
            sample_g = (t0 == 16)
            for tt_ in range(ntile):
                t = t0 + tt_
                b.dma("sp", XT0.ap, x_src(t), [], [XT0.name])
                norm_tile(XT0.ap, [XT0.name], nmix0, XNg.ap[:, :, tt_ * 128:(tt_ + 1) * 128], XNg.name, scrN)
            if sample_g:
                b.dma("sp", HIST.ap[0:48, :], sa_conv, [], [HIST.name])
                xpv = XPRE.ap[:, :, 0:176].rearrange("p c (s r) -> p c s r", s=16)
                for c in range(16):
                    bk = bank[c % 2]
                    b.tr(bk.ap[:, 0:48], HIST.ap[0:48, c * 128:(c + 1) * 128], identf.ap[0:48, 0:48],
                         [HIST.name, identf.name], [bk.name])
                    b.cp("act", xpv[:, c, :, 0:3], bk.ap[:, 0:48].rearrange("p (s r) -> p s r", s=16), [bk.name], [XPRE.name])
            elif t0 > 0:
                b.cp("act", XPRE.ap[:, :, 0:3], XPRE.ap[:, :, 256:259], [XPRE.name], [XPRE.name])
            for c in range(16):
                bk = bank[c % 2]
                for k in range(8):
                    b.mm(bk.ap[:, 0:n], WA.ap[:, k, 1024 + c * 128:1024 + (c + 1) * 128], XNg.ap[:, k, 0:n], k == 0, k == 7,
                         [WA.name, XNg.name], [bk.name])
                if sample_g:
                    pre = xpv[:, c]
                    b.cp("act", pre[:, :, 3:11], bk.ap[:, 0:128].rearrange("p (s l) -> p s l", s=16), [bk.name], [XPRE.name])
                    sh = lambda w_: pre[:, :, w_:w_ + 8]
                    acc = XCa.ap[:, 0:128].rearrange("p (s l) -> p s l", s=16)
                    xco = XC.ap[:, c, 0:128].rearrange("p (s l) -> p s l", s=16)
                else:
                    pre = XPRE.ap[:, c, :]
                    b.cp("act", pre[:, 3:3 + n], bk.ap[:, 0:n], [bk.name], [XPRE.name])
                    sh = lambda w_: pre[:, w_:w_ + n]
                    acc = XCa.ap[:, 0:n]
                    xco = XC.ap[:, c, 0:n]
                b.ts("dve", acc, sh(3), cw.ap[:, c, 3:4], cbias.ap[:, c:c + 1], ALU.mult, ALU.add,
                     [XPRE.name, cw.name, cbias.name], [XCa.name])
                for w_ in range(3):
                    b.stt(acc, sh(w_), cw.ap[:, c, w_:w_ + 1], acc, ALU.mult, ALU.add, [XPRE.name, cw.name, XCa.name], [XCa.name])
                b.act(xco, acc, AF.Silu, [XCa.name], [XC.name])
            if t0 == 14 or sample_g:
                nr = 48 if sample_g else 3
                for c4 in range(4):
                    bk = bank[2 + c4 % 2]
                    for cc in range(4):
                        c = 4 * c4 + cc
                        if sample_g:
                            b.cp("dve", c48.ap.rearrange("p (s r) -> p s r", s=16), xpv[:, c, :, 8:11], [XPRE.name], [c48.name])
                            src_ap = c48.ap
                            rk = [c48.name]
                        else:
                            src_ap = XPRE.ap[:, c, 256:259]
                            rk = [XPRE.name]
                        b.tr(bk.ap[0:nr, cc * 128:(cc + 1) * 128], src_ap, identf.ap, rk + [identf.name], [bk.name])
                    b.cp("act", HIST.ap[0:nr, c4 * 512:(c4 + 1) * 512], bk.ap[0:nr, :], [bk.name], [HIST.name])
                if sample_g:
                    b.dma("sp", s_a_conv, HIST.ap[0:48, :], [HIST.name], ["s_a_conv"])
                else:
                    b.dma("sp", p_a_conv, HIST.ap[0:3, :], [HIST.name], ["p_a_conv"])
            for tt_ in range(ntile):
                t = t0 + tt_
                sample = (t == 16)
                cs = slice(tt_ * 128, (tt_ + 1) * 128)
                tok = slice(t * 128, (t + 1) * 128)
                tri = tri_s if sample else tri_p
                blk = blk_s if sample else ones
                neg4 = neg_s4 if sample else neg_p4
                tb = tbank[0]
                for c in range(12):
                    b.tr(tb.ap[:, (c % 8) * 128:(c % 8 + 1) * 128], XC.ap[:, c, cs], identb.ap, [XC.name, identb.name], [tb.name])
                    if c == 7:
                        b.cp("act", XTM.ap, tb.ap, [tb.name], [XTM.name])
                        tb = tbank[1]
                b.cp("act", BTM.ap, tb.ap[:, 0:512], [tb.name], [BTM.name])
                for hf in range(2):
                    bk = bank[3 + hf]
                    for k in range(8):
                        b.mm(bk.ap, XNg.ap[:, k, cs], WA.ap[:, k, hf * 512:(hf + 1) * 512], k == 0, k == 7,
                             [XNg.name, WA.name], [bk.name])
                    b.act(ZS.ap[:, hf * 512:(hf + 1) * 512], bk.ap, AF.Silu, [bk.name], [ZS.name])
                bk = bank[0]
                for k in range(8):
                    b.mm(bk.ap[:, 0:16], XNg.ap[:, k, cs], WA.ap[:, k, 3072:3088], k == 0, k == 7, [XNg.name, WA.name], [bk.name])
                smk = [sm.name]
                b.tt("dve", smr(R_PRE), bk.ap[:, 0:16], dtb.ap, ALU.add, [bk.name, dtb.name], smk)
                b.act(smr(R_ABS), smr(R_PRE), AF.Abs, smk, smk)
                b.act(smr(R_E), smr(R_ABS), AF.Exp, smk, smk, scale=-1.0)
                b.act(smr(R_L), smr(R_E), AF.Ln, smk, smk, bias=1.0)
                b.ts("dve", smr(R_DT), smr(R_PRE), 0.0, None, ALU.max, None, smk, smk)
                b.tt("dve", smr(R_DT), smr(R_DT), smr(R_L), ALU.add, smk, smk)
                b.tt("dve", smr(R_DA), smr(R_DT), abc.ap, ALU.mult, smk + [abc.name], smk)
                b.mm(bk.ap[:, 16:32], tri.ap, smr(R_DA), True, True, [tri.name, sm.name], [bk.name])
                b.mm(bk.ap[:, 32:48], blk.ap, smr(R_DA), True, True, [blk.name, sm.name], [bk.name])
                b.cp("dve", smr(R_CUM), bk.ap[:, 16:32], [bk.name], smk)
                b.ts("dve", smr(R_NCUM), smr(R_CUM), -1.0, None, ALU.mult, None, smk, smk)
                b.tt("dve", smr(R_DEND), bk.ap[:, 32:48], smr(R_CUM), ALU.subtract, [bk.name] + smk, smk)
                b.act(smr(R_DEND), smr(R_DEND), AF.Exp, smk, smk)
                b.act(smr(R_ECUM), smr(R_CUM), AF.Exp, smk, smk)
                b.act(smr(R_CD), bk.ap[:, 32:48], AF.Exp, [bk.name], smk)
                b.tt("dve", smr(R_DTD), smr(R_DT), smr(R_DEND), ALU.mult, smk, smk)
                xv = XTM.ap.rearrange("p (h q) -> p h q", h=16)
                b.tt("dve", XDT.ap.rearrange("p (h q) -> p h q", h=16), xv, smr(R_DT).unsqueeze(2).to_broadcast([128, 16, 64]),
                     ALU.mult, [XTM.name] + smk, [XDT.name])
                b.tt("dve", XDE.ap.rearrange("p (h q) -> p h q", h=16), xv, smr(R_DTD).unsqueeze(2).to_broadcast([128, 16, 64]),
                     ALU.mult, [XTM.name] + smk, [XDE.name])
                for g in range(4):
                    bk1 = bank[1]
                    b.tt("dve", Rt.ap, sm.ap[:, R_DA, 4 * g:4 * g + 4].unsqueeze(2).to_broadcast([128, 4, 128]),
                         tri.ap.unsqueeze(1).to_broadcast([128, 4, 128]), ALU.mult, smk + [tri.name], [Rt.name])
                    b.mm(bk1.ap, ones.ap, Rt.ap.rearrange("p e i -> p (e i)"), True, False,
                         [ones.name, Rt.name], [bk1.name])
                    b.mm(bk1.ap, identb.ap, neg4.ap.rearrange("p e i -> p (e i)"), False, True, [identb.name, neg4.name], [bk1.name])
                    for e_ in range(4):
                        h = 4 * g + e_
                        b.act(Lt.ap[:, e_, :], bk1.ap[:, e_ * 128:(e_ + 1) * 128], AF.Exp, [bk1.name] + smk, [Lt.name],
                              bias=sm.ap[:, R_NCUM, h:h + 1])
                    bk2 = bank[2]
                    b.mm(bk2.ap[:, 0:128], XC.ap[:, 8 + g, cs], XC.ap[:, 12 + g, cs], True, True, [XC.name], [bk2.name])
                    b.tt("dve", MT.ap[:, 4 * g:4 * g + 4, :], Lt.ap, bk2.ap[:, 0:128].unsqueeze(1).to_broadcast([128, 4, 128]),
                         ALU.mult, [Lt.name, bk2.name], [MT.name])
                for h in range(16):
                    bk = bank[3 + h // 8]
                    b.mm(bk.ap[:, (h % 8) * 64:(h % 8 + 1) * 64], MT.ap[:, h, :], XDT.ap[:, h * 64:(h + 1) * 64], True, True,
                         [MT.name, XDT.name], [bk.name])
                obk = [bank[5], bank[1]]
                if not sample:
                    for g in range(4):
                        ob = obk[g // 2]
                        b.mm(ob.ap[:, (g % 2) * 256:(g % 2 + 1) * 256], XC.ap[:, 12 + g, cs], STb.ap[:, g * 256:(g + 1) * 256],
                             True, True, [XC.name, STb.name], [ob.name])
                else:
                    b.cp("dve", T2.ap.rearrange("p (h q) -> p h q", h=16), smr(R_CD).unsqueeze(2).to_broadcast([128, 16, 64]),
                         smk, [T2.name])
                    for hf in range(2):
                        bkc = bank[3 + hf]
                    for k in range(8):
                        bkc = bank[0] if k % 2 == 0 else bank[2]
                        b.tr(bkc.ap[:, 0:128], T2.ap[:, k * 128:(k + 1) * 128], identf.ap, [T2.name, identf.name], [bkc.name])
                        b.cp("act", CDT.ap[:, k, :], bkc.ap[:, 0:128:8], [bkc.name], [CDT.name])
                    for s_ in range(16):
                        b.dma("sp", H0.ap, sa_ssm[s_].rearrange("(k p) n -> p k n", p=128), [], [H0.name])
                        for k in range(8):
                            bkc = bank[0] if k % 2 == 0 else bank[2]
                            b.tr(bkc.ap[:, 0:128], H0.ap[:, k, :], identf.ap, [H0.name, identf.name], [bkc.name])
                            b.cp("act" if k % 2 else "dve", STs.ap[:, k * 128:(k + 1) * 128], bkc.ap[:, 0:128], [bkc.name], [STs.name])
                        b.tt("dve", CTm.ap, XC.ap[:, 12:16, cs], SEL.ap[:, s_, :].unsqueeze(1).to_broadcast([128, 4, 128]), ALU.mult,
                             [XC.name, SEL.name], [CTm.name])
                        for g in range(4):
                            ob = obk[g // 2]
                            b.mm(ob.ap[:, (g % 2) * 256:(g % 2 + 1) * 256], CTm.ap[:, g, :], STs.ap[:, g * 256:(g + 1) * 256],
                                 s_ == 0 and g % 2 == 0, s_ == 15, [CTm.name, STs.name], [ob.name])
                        b.ts("dve", Bm.ap, BTM.ap, seqmask.ap[:, s_:s_ + 1], None, ALU.mult, None, [BTM.name, seqmask.name], [Bm.name])
                        for k in range(8):
                            bkc = bank[0] if k % 2 == 0 else bank[2]
                            g = k // 2
                            b.mm(bkc.ap[:, 128:256], XDE.ap[:, k * 128:(k + 1) * 128], Bm.ap[:, g * 128:(g + 1) * 128], True, True,
                                 [XDE.name, Bm.name], [bkc.name])
                            b.stt(NS.ap[:, k, :], H0.ap[:, k, :], CDT.ap[:, k, s_:s_ + 1], bkc.ap[:, 128:256], ALU.mult, ALU.add,
                                  [H0.name, CDT.name, bkc.name], [NS.name])
                        b.dma("sp", s_a_ssm[s_].rearrange("(k p) n -> p k n", p=128), NS.ap, [NS.name], ["s_a_ssm"])
                ecb = smr(R_ECUM)
                for hf in range(2):
                    cols = slice(hf * 512, (hf + 1) * 512)
                    v3 = lambda ap_: ap_.rearrange("p (h q) -> p h q", h=8)
                    b.tt("dve", v3(T1.ap[:, cols]), v3(obk[hf].ap), ecb[:, hf * 8:(hf + 1) * 8].unsqueeze(2).to_broadcast([128, 8, 64]),
                         ALU.mult, [obk[hf].name] + smk, [T1.name])
                    b.tt("dve", T1.ap[:, cols], T1.ap[:, cols], bank[3 + hf].ap, ALU.add, [T1.name, bank[3 + hf].name], [T1.name])
                    b.tt("pool", v3(T2.ap[:, cols]), v3(XTM.ap[:, cols]), dbc.ap[:, hf * 8:(hf + 1) * 8].unsqueeze(2).to_broadcast([128, 8, 64]),
                         ALU.mult, [XTM.name, dbc.name], [T2.name])
                    b.tt("dve", T1.ap[:, cols], T1.ap[:, cols], T2.ap[:, cols], ALU.add, [T1.name, T2.name], [T1.name])
                    b.tt("dve", T1.ap[:, cols], T1.ap[:, cols], ZS.ap[:, cols], ALU.mult, [T1.name, ZS.name], [T1.name])
                for g in range(4):
                    b.act(T2.ap[:, g * 256:(g + 1) * 256], T1.ap[:, g * 256:(g + 1) * 256], AF.Square, [T1.name], [T2.name, sm.name],
                          accum=sm.ap[:, R_MS, g:g + 1])
                b.act(smr(R_RS, 4), smr(R_MS, 4), AF.Sqrt, smk, smk, bias=EPS, scale=1.0 / 256)
                b.recip(smr(R_RS, 4), smr(R_RS, 4), smk, smk)
                b.tt("dve", T1.ap.rearrange("p (g q) -> p g q", g=4), T1.ap.rearrange("p (g q) -> p g q", g=4),
                     smr(R_RS, 4).unsqueeze(2).to_broadcast([128, 4, 256]), ALU.mult, [T1.name] + smk, [T1.name])
                b.tt("dve", YAt.ap, T1.ap, anorm.ap, ALU.mult, [T1.name, anorm.name], [YAt.name])
                tb = tbank[0]
                for k in range(8):
                    b.tr(tb.ap[:, k * 128:(k + 1) * 128], YAt.ap[:, k * 128:(k + 1) * 128], identb.ap, [YAt.name, identb.name], [tb.name])
                b.cp("act", YA.ap[:, :, tok], tb.ap.rearrange("p (k t) -> p k t", k=8), [tb.name], [YA.name])
                if not sample:
                    for g in range(4):
                        bk = bank[3 + g // 2]
                        b.mm(bk.ap[:, (g % 2) * 256:(g % 2 + 1) * 256], BTM.ap[:, g * 128:(g + 1) * 128], XDE.ap[:, g * 256:(g + 1) * 256],
                             True, True, [BTM.name, XDE.name], [bk.name])
                    b.tt("dve", STf.ap.rearrange("p (h q) -> p h q", h=16), STf.ap.rearrange("p (h q) -> p h q", h=16),
                         smr(R_CD).unsqueeze(2).to_broadcast([128, 16, 64]), ALU.mult, [STf.name] + smk, [STf.name])
                    for hf in range(2):
                        cols = slice(hf * 512, (hf + 1) * 512)
                        b.tt("dve", STf.ap[:, cols], STf.ap[:, cols], bank[3 + hf].ap, ALU.add, [STf.name, bank[3 + hf].name], [STf.name])
                    b.cp("act", STb.ap, STf.ap, [STf.name], [STb.name])
        for k in range(8):
            bkc = bank[0] if k % 2 == 0 else bank[2]
            b.tr(bkc.ap[:, 0:128], STf.ap[:, k * 128:(k + 1) * 128], identf.ap, [STf.name, identf.name], [bkc.name])
            b.cp("act", NS.ap[:, k, :], bkc.ap[:, 0:128], [bkc.name], [NS.name])
        b.dma("sp", p_a_ssm.rearrange("(k p) n -> p k n", p=128), NS.ap, [NS.name], ["p_a_ssm"])
        if debug:
            b.debug_dump("YA", YA, [128, 8, NTOK], BF16)
        b.release(mA)
    if stop_after == "A":
        P.finalize(final_keys=b.out_keys)
        return b


    BASE = b.top
    XOFF = B.ARENA - NT * 1024 * 4

    def alloc_at(name, free_shape, dt, off):
        keep = b.top
        b.top = off
        t_ = b.alloc(name, free_shape, dt)
        b.top = keep
        return t_

    X = alloc_at("X", [NT, 1024], F32, XOFF)
    TOKG = [(0, 512), (512, 512), (1024, 512), (1536, 512), (2048, 128)]
    mC = b.mark()
    WO = b.alloc("WO", [16, 1024], BF16)
    XT0 = b.alloc("XT0c", [1024], F32)
    assert b.top <= XOFF
    for k in range(16):
        b.dma("pool", WO.ap[:, k, :], w_out[k * 128:(k + 1) * 128, :], [], [WO.name])
    for t in range(NT):
        tok = slice(t * 128, (t + 1) * 128)
        b.dma("sp", XT0.ap, x_src(t), [], [XT0.name])
        for hf in range(2):
            bk = bank[2 + (2 * t + hf) % 4]
            for k in range(16):
                src = YA if k < 8 else YB
                b.mm(bk.ap, src.ap[:, k % 8, tok], WO.ap[:, k, hf * 512:(hf + 1) * 512], k == 0, k == 15,
                     [src.name, WO.name], [bk.name])
            b.tt("dve", X.ap[:, t, hf * 512:(hf + 1) * 512], bk.ap, XT0.ap[:, hf * 512:(hf + 1) * 512], ALU.add,
                 [bk.name, XT0.name], [X.name])
    b.release(mC)
    if debug and stop_after == "C":
        b.debug_dump("X", X, [128, NT, 1024])
    if stop_after == "C":
        P.finalize(final_keys=b.out_keys)
        return b
    LOW = BASE - 2 * 8 * NTOK * 2

    def norm_all(wcol, XN):
        scr = norm_scratch()
        for t in range(NT):
            norm_tile(X.ap[:, t, :], [X.name], wcol, XN.ap[:, :, t * 128:(t + 1) * 128], XN.name, scr)

    def ffn(layer, wcol):
        b.top = LOW
        XN = b.alloc("XNf", [8, NTOK], BF16)
        HT = [b.alloc("HT0", [4, NTOK], BF16), b.alloc("HT1", [4, NTOK], BF16)]
        W1 = [b.alloc("W1a", [8, 512], BF16), b.alloc("W1b", [8, 512], BF16)]
        W2 = [b.alloc("W2a", [4, 1024], BF16), b.alloc("W2b", [4, 1024], BF16)]
        R1 = [b.alloc("R1a", [512], F32), b.alloc("R1b", [512], F32)]
        norm_all(wcol, XN)
        assert b.top <= XOFF
        ri_ = 0
        for fg in range(8):
            w1, w2, ht = W1[fg % 2], W2[fg % 2], HT[fg % 2]
            for k in range(8):
                b.dma("pool", w1.ap[:, k, :], w_ff1[layer, k * 128:(k + 1) * 128, fg * 512:(fg + 1) * 512], [], [w1.name])
            for f in range(4):
                b.dma("pool", w2.ap[:, f, :], w_ff2[layer, fg * 512 + f * 128:fg * 512 + (f + 1) * 128, :], [], [w2.name])
            for f in range(4):
                for (c0, n) in TOKG:
                    bk = bank[ri_ % 2]
                    r1 = R1[ri_ % 2]
                    ri_ += 1
                    for k in range(8):
                        b.mm(bk.ap[:, 0:n], w1.ap[:, k, f * 128:(f + 1) * 128], XN.ap[:, k, c0:c0 + n], k == 0, k == 7,
                             [w1.name, XN.name], [bk.name])
                    b.act(r1.ap[:, 0:n], bk.ap[:, 0:n], AF.Relu, [bk.name], [r1.name])
                    b.tt("pool", ht.ap[:, f, c0:c0 + n], r1.ap[:, 0:n], r1.ap[:, 0:n], ALU.mult, [r1.name], [ht.name])
            for t in range(NT):
                tok = slice(t * 128, (t + 1) * 128)
                for hf in range(2):
                    bk = bank[2 + (2 * t + hf) % 4]
                    for f in range(4):
                        b.mm(bk.ap, ht.ap[:, f, tok], w2.ap[:, f, hf * 512:(hf + 1) * 512], f == 0, f == 3,
                             [ht.name, w2.name], [bk.name])
                    xs_ = X.ap[:, t, hf * 512:(hf + 1) * 512]
                    b.tt("dve", xs_, xs_, bk.ap, ALU.add, [X.name, bk.name], [X.name])

    if "F0" not in skip:
        ffn(0, nff0)
    if debug and stop_after == "F0":
        b.debug_dump("X", X, [128, NT, 1024])
    if stop_after == "F0":
        P.finalize(final_keys=b.out_keys)
        return b

    b.top = LOW
    HP = 30 + 2048
    H = b.alloc("H", [8, HP + 608], BF16)
    HL = b.alloc("HL", [8, 256], F32)
    bpw1 = load_cols("bpw1", c_b_pw1, 16)
    bdw = load_cols("bdw", c_b_dw, 8)
    lnw = load_cols("lnw", c_ln_w, 8)
    lnb = load_cols("lnb", c_ln_b, 8)
    bpw2 = load_bcast("bpw2", c_b_pw2, 1024)
    wdw = b.alloc("wdw", [8, 31], F32)
    for w_ in range(31):
        b.dma("sp" if w_ % 2 else "act", wdw.ap[:, :, w_], c_w_dw[w_:w_ + 1, :].rearrange("o (c p) -> p (o c)", p=128),
              [], [wdw.name], **NCD)
    mX = b.mark()
    XN = b.alloc("XNc", [8, NTOK], BF16)
    WP = [b.alloc("WPa", [8, 256], BF16), b.alloc("WPb", [8, 256], BF16)]
    SGt = b.alloc("SGt", [512], F32)
    HS = b.alloc("HS", [1024], F32)
    norm_all(nmix1, XN)
    assert b.top <= XOFF
    b.memset("pool", H.ap[:, :, 0:30], 0.0, [H.name])
    for q4 in range(4):
        b.dma("sp", HS.ap[0:120, :], sc_conv[4 * q4:4 * q4 + 4].rearrange("s w c -> (s w) c"), [], [HS.name])
        for c in range(8):
            bk = bank[c % 2]
            b.tr(bk.ap[:, 0:120], HS.ap[0:120, c * 128:(c + 1) * 128], identf.ap[0:120, 0:120], [HS.name, identf.name], [bk.name])
            dst = H.ap[:, c, HP:HP + 480].rearrange("p (w s) -> p s w", s=16)[:, 4 * q4:4 * q4 + 4, :]
            b.cp("act", dst, bk.ap[:, 0:120].rearrange("p (s w) -> p s w", s=4), [bk.name], [H.name])
    for c in range(8):
        wp = WP[c % 2]
        for k in range(8):
            b.dma("pool", wp.ap[:, k, 0:128], c_w_pw1[k * 128:(k + 1) * 128, c * 128:(c + 1) * 128], [], [wp.name])
            b.dma("pool", wp.ap[:, k, 128:256], c_w_pw1[k * 128:(k + 1) * 128, 1024 + c * 128:1024 + (c + 1) * 128], [], [wp.name])
        for (c0, n) in TOKG:
            bv, bg = bank[0], bank[1]
            for k in range(8):
                b.mm(bv.ap[:, 0:n], wp.ap[:, k, 0:128], XN.ap[:, k, c0:c0 + n], k == 0, k == 7, [wp.name, XN.name], [bv.name])
            for k in range(8):
                b.mm(bg.ap[:, 0:n], wp.ap[:, k, 128:256], XN.ap[:, k, c0:c0 + n], k == 0, k == 7, [wp.name, XN.name], [bg.name])
            b.act(SGt.ap[:, 0:n], bg.ap[:, 0:n], AF.Sigmoid, [bg.name, bpw1.name], [SGt.name], bias=bpw1.ap[:, 8 + c:9 + c])
            if c0 < 2048:
                b.stt(H.ap[:, c, 30 + c0:30 + c0 + n], bv.ap[:, 0:n], bpw1.ap[:, c:c + 1], SGt.ap[:, 0:n], ALU.add, ALU.mult,
                      [bv.name, bpw1.name, SGt.name], [H.name])
                if c0 == 1536:
                    b.stt(HL.ap[:, c, 0:128], bv.ap[:, 384:512], bpw1.ap[:, c:c + 1], SGt.ap[:, 384:512], ALU.add, ALU.mult,
                          [bv.name, bpw1.name, SGt.name], [HL.name])
            else:
                dst = H.ap[:, c, HP + 480:HP + 608].rearrange("p (l s) -> p s l", s=16)
                b.stt(dst, bv.ap[:, 0:128].rearrange("p (s l) -> p s l", s=16), bpw1.ap[:, c:c + 1],
                      SGt.ap[:, 0:128].rearrange("p (s l) -> p s l", s=16), ALU.add, ALU.mult,
                      [bv.name, bpw1.name, SGt.name], [H.name])
                b.stt(HL.ap[:, c, 128:256], bv.ap[:, 0:128], bpw1.ap[:, c:c + 1], SGt.ap[:, 0:128], ALU.add, ALU.mult,
                      [bv.name, bpw1.name, SGt.name], [HL.name])
    b.release(mX)
    OUTS = b.alloc("OUTS", [1024], F32)
    for c in range(8):
        bk = bank[2 + c // 4]
        b.tr(bk.ap[0:30, (c % 4) * 128:(c % 4 + 1) * 128], HL.ap[:, c, 98:128], identf.ap, [HL.name, identf.name], [bk.name])
        if c % 4 == 3:
            b.cp("act", OUTS.ap[0:30, (c // 4) * 512:(c // 4 + 1) * 512], bk.ap[0:30, :], [bk.name], [OUTS.name])
    b.dma("sp", p_c_conv, OUTS.ap[0:30, :], [OUTS.name], ["p_c_conv"])
    for c in range(8):
        bk = bank[4 + c // 4]
        b.tr(bk.ap[:, (c % 4) * 128:(c % 4 + 1) * 128], HL.ap[:, c, 128:256], identf.ap, [HL.name, identf.name], [bk.name])
        if c % 4 == 3:
            b.cp("act", OUTS.ap[:, (c // 4) * 512:(c // 4 + 1) * 512], bk.ap, [bk.name], [OUTS.name])
    for s_ in range(16):
        b.dma("sp" if s_ % 2 else "act", s_c_conv[s_, 22:30, :], OUTS.ap[8 * s_:8 * s_ + 8, :], [OUTS.name], ["s_c_conv"])
    b.dma("sp", s_c_conv[:, 0:22, :], sc_conv[:, 8:30, :], [], ["s_c_conv"])
    WP2 = b.alloc("WP2", [8, 1024], BF16)
    for k in range(8):
        b.dma("pool", WP2.ap[:, k, :], c_w_pw2[k * 128:(k + 1) * 128, :], [], [WP2.name])
    DM = [b.alloc("DMa", [31, 128], BF16), b.alloc("DMb", [31, 128], BF16)]
    CV = b.alloc("CV", [8, 512], F32)
    SQ = [b.alloc("SQa", [512], F32), b.alloc("SQb", [512], F32)]
    MEAN = b.alloc("MEAN", [512], F32)
    RSTD = b.alloc("RSTD", [512], F32)
    M2 = b.alloc("M2", [512], F32)
    TMP = b.alloc("TMPc", [512], F32)
    AV = b.alloc("AV", [8, 512], BF16)
    assert b.top <= XOFF, b.top
    for t in range(NT):
        b.tt("pool", X.ap[:, t, :], X.ap[:, t, :], bpw2.ap, ALU.add, [X.name, bpw2.name], [X.name])
    di = 0
    for (c0, n) in TOKG:
        sample_g = (c0 == 2048)
        s1, s2 = bank[2], bank[3]
        for c in range(8):
            dm = DM[di % 2]
            di += 1
            for w_ in range(31):
                b.ts("pool", dm.ap[:, w_, :], identb.ap, wdw.ap[:, c, w_:w_ + 1], 1.0, ALU.mult, ALU.mult,
                     [identb.name, wdw.name], [dm.name])
            bk = bank[c % 2]
            for w_ in range(31):
                if sample_g:
                    rhs = H.ap[:, c, HP + 16 * w_:HP + 16 * w_ + 128]
                else:
                    rhs = H.ap[:, c, c0 + w_:c0 + w_ + n]
                b.mm(bk.ap[:, 0:n], dm.ap[:, w_, :], rhs, w_ == 0, w_ == 30, [dm.name, H.name], [bk.name])
            b.act(CV.ap[:, c, 0:n], bk.ap[:, 0:n], AF.Identity, [bk.name, bdw.name], [CV.name], bias=bdw.ap[:, c:c + 1])
            sq = SQ[c % 2]
            b.act(sq.ap[:, 0:n], CV.ap[:, c, 0:n], AF.Square, [CV.name], [sq.name])
            b.mm(s1.ap[:, 0:n], ones.ap, CV.ap[:, c, 0:n], c == 0, c == 7, [ones.name, CV.name], [s1.name])
            b.mm(s2.ap[:, 0:n], ones.ap, sq.ap[:, 0:n], c == 0, c == 7, [ones.name, sq.name], [s2.name])
        b.act(MEAN.ap[:, 0:n], s1.ap[:, 0:n], AF.Copy, [s1.name], [MEAN.name], scale=1.0 / 1024)
        b.tt("dve", M2.ap[:, 0:n], MEAN.ap[:, 0:n], MEAN.ap[:, 0:n], ALU.mult, [MEAN.name], [M2.name])
        b.stt(RSTD.ap[:, 0:n], s2.ap[:, 0:n], 1.0 / 1024, M2.ap[:, 0:n], ALU.mult, ALU.subtract, [s2.name, M2.name], [RSTD.name])
        b.act(RSTD.ap[:, 0:n], RSTD.ap[:, 0:n], AF.Sqrt, [RSTD.name], [RSTD.name], bias=EPS)
        b.recip(RSTD.ap[:, 0:n], RSTD.ap[:, 0:n], [RSTD.name], [RSTD.name])
        for c in range(8):
            b.tt("dve", TMP.ap[:, 0:n], CV.ap[:, c, 0:n], MEAN.ap[:, 0:n], ALU.subtract, [CV.name, MEAN.name], [TMP.name])
            b.tt("dve", TMP.ap[:, 0:n], TMP.ap[:, 0:n], RSTD.ap[:, 0:n], ALU.mult, [TMP.name, RSTD.name], [TMP.name])
            if sample_g:
                dst = AV.ap[:, c, 0:128].rearrange("p (s l) -> p s l", s=16)
                src = TMP.ap[:, 0:128].rearrange("p (l s) -> p s l", s=16)
            else:
                dst, src = AV.ap[:, c, 0:n], TMP.ap[:, 0:n]
            b.act(dst, src, AF.Silu, [TMP.name, lnw.name, lnb.name], [AV.name], bias=lnb.ap[:, c:c + 1], scale=lnw.ap[:, c:c + 1])
        for tt_ in range(n // 128):
            t = c0 // 128 + tt_
            cs = slice(tt_ * 128, (tt_ + 1) * 128)
            for hf in range(2):
                bk = bank[4 + hf]
                for c in range(8):
                    b.mm(bk.ap, AV.ap[:, c, cs], WP2.ap[:, c, hf * 512:(hf + 1) * 512], c == 0, c == 7, [AV.name, WP2.name], [bk.name])
                xs_ = X.ap[:, t, hf * 512:(hf + 1) * 512]
                b.tt("dve", xs_, xs_, bk.ap, ALU.add, [X.name, bk.name], [X.name])
    if debug and stop_after == "L1":
        b.debug_dump("X", X, [128, NT, 1024])
    if stop_after == "L1":
        P.finalize(final_keys=b.out_keys)
        return b

    if "F1" not in skip:
        ffn(1, nff1)

    b.top = LOW
    nfin = load_bcast("nfin", norm_final, 1024)
    junk = b.alloc("junkf", [1024], BF16)
    ssf = b.alloc("ssf", [4], F32)
    YO = [b.alloc("YOa", [1024], F32), b.alloc("YOb", [1024], F32)]
    for t in range(NT):
        yo = YO[t % 2]
        b.act(junk.ap, X.ap[:, t, :], AF.Square, [X.name], [junk.name, ssf.name], accum=ssf.ap[:, 0:1])
        b.act(ssf.ap[:, 1:2], ssf.ap[:, 0:1], AF.Sqrt, [ssf.name], [ssf.name], bias=EPS, scale=1.0 / D)
        b.recip(ssf.ap[:, 2:3], ssf.ap[:, 1:2], [ssf.name], [ssf.name])
        b.stt(yo.ap, X.ap[:, t, :], ssf.ap[:, 2:3], nfin.ap, ALU.mult, ALU.mult, [X.name, ssf.name, nfin.name], [yo.name])
        if t < 16:
            b.dma("sp", y_p[t * 128:(t + 1) * 128, :], yo.ap, [yo.name], ["y_p"])
        else:
            b.dma("sp", y_s, yo.ap, [yo.name], ["y_s"])
    P.finalize(final_keys=b.out_keys)
    return b


def make_in_maps(inp):
    f = lambda a: np.ascontiguousarray(np.asarray(a, dtype=np.float32))
    shared = {
        "norm_mix": f(inp["norm_mix"]), "norm_ff": f(inp["norm_ff"]), "norm_final": f(inp["norm_final"]).reshape(1, D),
        "w_in": f(inp["w_in_ab"][0]), "a_conv_w": f(inp["a_conv_w"][0]), "a_conv_b": f(inp["a_conv_b"]),
        "a_dt_bias": f(inp["a_dt_bias"]), "a_log": f(inp["a_log"]), "a_d": f(inp["a_d"]), "a_norm": f(inp["a_norm"]),
        "lam_re": f(inp["s5_lam_re"][0]), "lam_im": f(inp["s5_lam_im"][0]), "log_step": f(inp["s5_log_step"]),
        "s5_b_re": f(inp["s5_b_re"][0]), "s5_b_im": f(inp["s5_b_im"][0]), "s5_c_re": f(inp["s5_c_re"][0]),
        "s5_c_im": f(inp["s5_c_im"][0]), "s5_d": f(inp["s5_d"]).reshape(1, 1024), "w_glu": f(inp["s5_w_glu"][0]),
        "b_glu": f(inp["s5_b_glu"]), "w_out": f(inp["w_out_ab"][0]), "c_w_pw1": f(inp["c_w_pw1"][0]),
        "c_b_pw1": f(inp["c_b_pw1"]), "c_w_dw": f(inp["c_w_dw"][0]), "c_b_dw": f(inp["c_b_dw"]),
        "c_ln_w": f(inp["c_ln_w"]), "c_ln_b": f(inp["c_ln_b"]), "c_w_pw2": f(inp["c_w_pw2"][0]),
        "c_b_pw2": f(inp["c_b_pw2"]), "w_ff1": f(inp["w_ff1"]), "w_ff2": f(inp["w_ff2"]),
    }
    maps = []
    for c in range(8):
        s = slice(16 * c, 16 * c + 16)
        m = dict(shared)
        m["xp"] = f(inp["x_prompt"][c])
        m["xs"] = f(inp["x_sample"][s]).reshape(128, D)
        m["sa_conv"] = f(inp["state_a_conv"][0, s]).reshape(48, 2048)
        m["sa_ssm"] = f(inp["state_a_ssm"][0, s]).reshape(16, 1024, 128)
        m["sb_re"] = f(inp["state_b_re"][0, s]).reshape(16, 4096)
        m["sb_im"] = f(inp["state_b_im"][0, s]).reshape(16, 4096)
        m["sc_conv"] = f(inp["state_c_conv"][0, s])
        maps.append(m)
    return maps


_CACHE = {}


def kernel(**inputs):
    if "nc" not in _CACHE:
        _CACHE["nc"] = build().nc
    nc = _CACHE["nc"]
    maps = make_in_maps(inputs)
    res = run_bass_kernel_spmd(nc, maps, core_ids=list(range(8)))
    R = res.results
    cat = lambda k: np.stack([np.asarray(r[k]) for r in R], axis=0)
    y_prompt = cat("y_p").reshape(8, 2048, D)
    y_sample = cat("y_s").reshape(128, 8, D)
    p_a_conv = cat("p_a_conv").reshape(1, 8, 3, 2048)
    p_a_ssm = cat("p_a_ssm").reshape(1, 8, 16, 64, 128)
    p_b_re = cat("p_b_re").reshape(1, 8, 64, 64)
    p_b_im = cat("p_b_im").reshape(1, 8, 64, 64)
    p_c_conv = cat("p_c_conv").reshape(1, 8, 30, D)
    s_a_conv = cat("s_a_conv").reshape(1, 128, 3, 2048)
    s_a_ssm = cat("s_a_ssm").reshape(1, 128, 16, 64, 128)
    s_b_re = cat("s_b_re").reshape(1, 128, 64, 64)
    s_b_im = cat("s_b_im").reshape(1, 128, 64, 64)
    s_c_conv = cat("s_c_conv").reshape(1, 128, 30, D)
    return tuple(np.ascontiguousarray(a, dtype=np.float32) for a in
                 (y_prompt, y_sample, p_a_conv, p_a_ssm, p_b_re, p_b_im, p_c_conv,
                  s_a_conv, s_a_ssm, s_b_re, s_b_im, s_c_conv))
```

```python
import math
import numpy as np
import concourse.bass as bass
import concourse.mybir as mybir
from concourse.bass_utils import run_bass_kernel_spmd

F32 = mybir.dt.float32
BF16 = mybir.dt.bfloat16
I32 = mybir.dt.int32
U8 = mybir.dt.uint8
AF = mybir.ActivationFunctionType
ALU = mybir.AluOpType
DTSIZE = {F32: 4, BF16: 2, I32: 4, U8: 1}

ENGS = ("pe", "act", "dve", "pool", "sp")
ROLL = 3000
N_DMA_SEMS = 24
PAR_DMA = True

D = 1024
NT = 17
NTOK = NT * 128
A_PROJ = 3088
IN_COLS = 4112
EPS = 1e-6
TWO_PI = 2.0 * math.pi


class _Op:
    __slots__ = ("eng", "emit", "deps", "is_dma", "signal", "sig", "waits", "idx", "dma_prev")

    def __init__(self, eng, emit, is_dma):
        self.eng = eng
        self.emit = emit
        self.is_dma = is_dma
        self.deps = []
        self.signal = False
        self.sig = None
        self.waits = []
        self.dma_prev = None


class Prog:
    def __init__(self, nc):
        self.nc = nc
        self.ops = []
        self.last_w = {}
        self.readers = {}
        self._sem_cms = []
        self.alias = {}

    def _writers(self, k):
        w = self.last_w.get(k)
        if w is None:
            return ()
        return w if isinstance(w, list) else (w,)

    def add(self, eng, emit, reads=(), writes=(), dma=False, par=False):
        par = par and PAR_DMA and eng != "pool"
        op = _Op(eng, emit, dma)
        op.idx = len(self.ops)
        deps = set()
        for k in list(reads) + list(writes):
            for a in self.alias.get(k, ()):
                deps.update(self._writers(a))
                for r in self.readers.get(a, ()):
                    deps.add(r)
        for k in reads:
            deps.update(self._writers(k))
            if k.startswith(("bank", "tbank")):
                for r in self.readers.get(k, ()):
                    if r.eng != eng:
                        deps.add(r)
        for k in writes:
            prev = self.last_w.get(k)
            same_batch = par and isinstance(prev, list) and not self.readers.get(k)
            if not same_batch:
                deps.update(self._writers(k))
            for r in self.readers.get(k, ()):
                deps.add(r)
        for k in reads:
            self.readers.setdefault(k, []).append(op)
        for k in writes:
            prev = self.last_w.get(k)
            if par and isinstance(prev, list) and not self.readers.get(k):
                prev.append(op)
            else:
                self.last_w[k] = [op] if par else op
            self.readers[k] = []
        deps.discard(op)
        op.deps = [d for d in deps if not (d.eng == "pe" and eng == "pe" and not d.is_dma and not dma)]
        if dma:
            op.signal = True
        for d in op.deps:
            d.signal = True
        self.ops.append(op)
        return op

    def _new_sem(self, name):
        cm = self.nc.semaphore(name)
        s = cm.__enter__()
        self._sem_cms.append(cm)
        return s

    def finalize(self, final_keys=()):
        nc = self.nc
        fin_deps = set()
        for k in final_keys:
            fin_deps.update(self._writers(k))
        for d in fin_deps:
            d.signal = True
        eng_sem, eng_cnt = {}, {}
        dma_sems = [self._new_sem(f"dq{i}") for i in range(N_DMA_SEMS)]
        dma_cnt = [0] * N_DMA_SEMS
        dma_last = [None] * N_DMA_SEMS
        nd = 0
        for op in self.ops:
            if not op.signal:
                continue
            if op.is_dma:
                i = nd % N_DMA_SEMS
                nd += 1
                op.dma_prev = dma_last[i]
                dma_cnt[i] += 16
                op.sig = (dma_sems[i], dma_cnt[i])
                dma_last[i] = op
            else:
                e = op.eng
                if e not in eng_sem or eng_cnt[e] >= ROLL:
                    eng_sem[e] = self._new_sem(f"s_{e}_{len(self._sem_cms)}")
                    eng_cnt[e] = 0
                eng_cnt[e] += 1
                op.sig = (eng_sem[e], eng_cnt[e])
        waited = {e: {} for e in ENGS}
        per_eng = {e: [] for e in ENGS}
        for op in self.ops:
            need = {}
            deps = list(op.deps)
            if op.is_dma and op.dma_prev is not None:
                deps.append(op.dma_prev)
            for d in deps:
                sem, val = d.sig
                key = id(sem)
                if waited[op.eng].get(key, (None, 0))[1] >= val:
                    continue
                if key not in need or need[key][1] < val:
                    need[key] = (sem, val)
            for key, sv in need.items():
                waited[op.eng][key] = sv
            op.waits = list(need.values())
            per_eng[op.eng].append(op)
        fin_waits = {}
        for d in fin_deps:
            sem, val = d.sig
            if id(sem) not in fin_waits or fin_waits[id(sem)][1] < val:
                fin_waits[id(sem)] = (sem, val)

        def run(engine_obj, lst, final=False):
            for op in lst:
                for sem, val in op.waits:
                    engine_obj.wait_ge(sem, val)
                ins = op.emit(engine_obj)
                if op.sig is not None:
                    ins.then_inc(op.sig[0], 16 if op.is_dma else 1)
            if final:
                for sem, val in fin_waits.values():
                    engine_obj.wait_ge(sem, val)

        with nc.Block() as block:
            @block.sync
            def _(e):
                run(e, per_eng["sp"], final=True)

            @block.tensor
            def _(e):
                run(e, per_eng["pe"])

            @block.scalar
            def _(e):
                run(e, per_eng["act"])

            @block.vector
            def _(e):
                run(e, per_eng["dve"])

            @block.gpsimd
            def _(e):
                run(e, per_eng["pool"])
        for cm in reversed(self._sem_cms):
            cm.__exit__(None, None, None)
        self.stats = {e: len(per_eng[e]) for e in ENGS}
        self.stats["waits"] = sum(len(o.waits) for o in self.ops)


class T:
    def __init__(self, name, ap):
        self.name = name
        self.ap = ap

    def __getitem__(self, k):
        return self.ap[k]


def _prod(s):
    r = 1
    for v in s:
        r *= v
    return r


class B:
    ARENA = 206 * 1024

    def __init__(self):
        self.nc = bass.Bass("TRN2", target_bir_lowering=False)
        nc = self.nc
        self.P = Prog(nc)
        self.arena = nc.alloc_sbuf_tensor("arena", [128, self.ARENA], U8)
        self.top = 0
        self.uid = 0
        self.banks = [T(f"bank{i}", nc.alloc_psum_tensor(f"bank{i}", [128, 512], F32)[:, :]) for i in range(6)]
        self.tbanks = [T(f"tbank{i}", nc.alloc_psum_tensor(f"tbank{i}", [128, 1024], BF16)[:, :]) for i in range(2)]
        self.dram = {}
        self.out_keys = []
        self.dbg = []
        self.regions = []

    def alloc(self, name, free_shape, dt):
        free_shape = tuple(free_shape)
        n = _prod(free_shape) * DTSIZE[dt]
        off = (self.top + 63) // 64 * 64
        self.top = off + n
        assert self.top <= self.ARENA, f"SBUF arena overflow at {name}: {self.top}"
        ap = self.arena[:, off:off + n].bitcast(dt)
        if len(free_shape) == 2:
            ap = ap.rearrange("p (a b) -> p a b", a=free_shape[0])
        elif len(free_shape) == 3:
            ap = ap.rearrange("p (a b c) -> p a b c", a=free_shape[0], b=free_shape[1])
        elif len(free_shape) == 4:
            ap = ap.rearrange("p (a b c d) -> p a b c d", a=free_shape[0], b=free_shape[1], c=free_shape[2])
        self.uid += 1
        nm = f"{name}#{self.uid}"
        al = [r[0] for r in self.regions if r[1] < off + n and off < r[2]]
        if al:
            self.P.alias[nm] = list(al)
            for o in al:
                self.P.alias.setdefault(o, []).append(nm)
        self.regions.append((nm, off, off + n))
        return T(nm, ap)

    def mark(self):
        return self.top

    def release(self, m):
        self.top = m

    def din(self, name, shape):
        t = self.nc.dram_tensor(name, list(shape), F32, kind="ExternalInput").ap()
        self.dram[name] = t
        return t

    def dout(self, name, shape):
        t = self.nc.dram_tensor(name, list(shape), F32, kind="ExternalOutput").ap()
        self.dram[name] = t
        self.out_keys.append(name)
        return t

    def mm(self, out, lhsT, rhs, start, stop, r, w):
        self.P.add("pe", lambda e: e.matmul(out, lhsT, rhs, start=start, stop=stop), r, w)

    def tr(self, out, in_, ident, r, w):
        self.P.add("pe", lambda e: e.transpose(out, in_, ident), r, w)

    def act(self, out, in_, func, r, w, bias=None, scale=None, accum=None):
        kw = {}
        if bias is not None:
            kw["bias"] = bias
        if scale is not None:
            kw["scale"] = scale
        if accum is not None:
            kw["accum_out"] = accum
        self.P.add("act", lambda e: e.activation(out, in_, func, **kw), r, w)

    def ts(self, eng, out, in0, s1, s2, op0, op1, r, w):
        if op1 is None:
            self.P.add(eng, lambda e: e.tensor_scalar(out, in0, s1, None, op0), r, w)
        else:
            self.P.add(eng, lambda e: e.tensor_scalar(out, in0, s1, s2, op0, op1), r, w)

    def tt(self, eng, out, in0, in1, op, r, w):
        self.P.add(eng, lambda e: e.tensor_tensor(out, in0, in1, op), r, w)

    def stt(self, out, in0, scalar, in1, op0, op1, r, w):
        self.P.add("dve", lambda e: e.scalar_tensor_tensor(out, in0, scalar, in1, op0, op1), r, w)

    def cp(self, eng, out, in_, r, w):
        if eng == "act":
            self.P.add("act", lambda e: e.activation(out, in_, AF.Copy), r, w)
        else:
            self.P.add(eng, lambda e: e.tensor_copy(out, in_), r, w)

    def memset(self, eng, out, val, w):
        self.P.add(eng, lambda e: e.memset(out, val), (), w)

    def scan(self, out, d0, d1, init, r, w):
        self.P.add("dve", lambda e: e.tensor_tensor_scan(out, d0, d1, init, ALU.mult, ALU.add), r, w)

    def dma(self, eng, out, in_, r, w, par=False, **kw):
        self.P.add(eng, lambda e: e.dma_start(out=out, in_=in_, **kw), r, w, dma=True, par=par)

    def recip(self, out, in_, r, w):
        self.P.add("dve", lambda e: e.reciprocal(out, in_), r, w)

    def debug_dump(self, name, t, shape, dt=F32):
        d = self.nc.dram_tensor("dbg_" + name, list(shape), dt, kind="ExternalOutput").ap()
        self.out_keys.append("dbg_" + name)
        self.dma("sp", d, t.ap if isinstance(t, T) else t, [t.name] if isinstance(t, T) else [], ["dbg_" + name])


def build(stop_after=None, debug=False, cut=None, skip=()):
    b = B()
    nc, P = b.nc, b.P
    NCD = dict(allow_slow_non_contiguous=True)

    xp = b.din("xp", [2048, D])
    xs = b.din("xs", [128, D])
    sa_conv = b.din("sa_conv", [48, 2048])
    sa_ssm = b.din("sa_ssm", [16, 1024, 128])
    sb_re = b.din("sb_re", [16, 4096])
    sb_im = b.din("sb_im", [16, 4096])
    sc_conv = b.din("sc_conv", [16, 30, D])
    norm_mix = b.din("norm_mix", [2, D])
    norm_ff = b.din("norm_ff", [2, D])
    norm_final = b.din("norm_final", [1, D])
    w_in = b.din("w_in", [D, IN_COLS])
    a_conv_w = b.din("a_conv_w", [4, 2048])
    a_conv_b = b.din("a_conv_b", [1, 2048])
    a_dt_bias = b.din("a_dt_bias", [1, 16])
    a_log = b.din("a_log", [1, 16])
    a_d = b.din("a_d", [1, 16])
    a_norm = b.din("a_norm", [1, 1024])
    lam_re = b.din("lam_re", [64, 64])
    lam_im = b.din("lam_im", [64, 64])
    log_step = b.din("log_step", [1, 64])
    s5_b_re = b.din("s5_b_re", [64, 64, 16])
    s5_b_im = b.din("s5_b_im", [64, 64, 16])
    s5_c_re = b.din("s5_c_re", [64, 16, 64])
    s5_c_im = b.din("s5_c_im", [64, 16, 64])
    s5_d = b.din("s5_d", [1, 1024])
    w_glu = b.din("w_glu", [1024, 1024])
    b_glu = b.din("b_glu", [1, 1024])
    w_out = b.din("w_out", [2048, D])
    c_w_pw1 = b.din("c_w_pw1", [D, 2048])
    c_b_pw1 = b.din("c_b_pw1", [1, 2048])
    c_w_dw = b.din("c_w_dw", [31, D])
    c_b_dw = b.din("c_b_dw", [1, D])
    c_ln_w = b.din("c_ln_w", [1, D])
    c_ln_b = b.din("c_ln_b", [1, D])
    c_w_pw2 = b.din("c_w_pw2", [D, D])
    c_b_pw2 = b.din("c_b_pw2", [1, D])
    w_ff1 = b.din("w_ff1", [2, D, 4096])
    w_ff2 = b.din("w_ff2", [2, 4096, D])

    y_p = b.dout("y_p", [2048, D])
    y_s = b.dout("y_s", [128, D])
    p_a_conv = b.dout("p_a_conv", [3, 2048])
    p_a_ssm = b.dout("p_a_ssm", [1024, 128])
    p_b_re = b.dout("p_b_re", [32, 128])
    p_b_im = b.dout("p_b_im", [32, 128])
    p_c_conv = b.dout("p_c_conv", [30, D])
    s_a_conv = b.dout("s_a_conv", [48, 2048])
    s_a_ssm = b.dout("s_a_ssm", [16, 1024, 128])
    s_b_re = b.dout("s_b_re", [16, 4096])
    s_b_im = b.dout("s_b_im", [16, 4096])
    s_c_conv = b.dout("s_c_conv", [16, 30, D])

    def x_src(t):
        return xp[t * 128:(t + 1) * 128, :] if t < 16 else xs

    bank = b.banks
    tbank = b.tbanks

    io = b.alloc("io", [128], I32)
    identf = b.alloc("identf", [128], F32)
    identb = b.alloc("identb", [128], BF16)
    tri_p = b.alloc("tri_p", [128], F32)
    tri_s = b.alloc("tri_s", [128], F32)
    blk_s = b.alloc("blk_s", [128], F32)
    ones = b.alloc("ones", [128], F32)
    neg_p4 = b.alloc("neg_p4", [4, 128], BF16)
    neg_s4 = b.alloc("neg_s4", [4, 128], BF16)
    seqmask = b.alloc("seqmask", [16], F32)
    lmask = b.alloc("lmask", [16, 8], F32)
    tau1 = b.alloc("tau1", [128], F32)
    cm = b.mark()
    tmpi = b.alloc("tmpi", [128], I32)
    tmpf = b.alloc("tmpf", [128], F32)
    tmpg = b.alloc("tmpg", [128], F32)
    P.add("pool", lambda e: e.iota(io.ap, [[1, 128]], base=0, channel_multiplier=-1), (), [io.name])
    b.ts("dve", identf.ap, io.ap, 0, None, ALU.is_equal, None, [io.name], [identf.name])
    b.cp("dve", identb.ap, identf.ap, [identf.name], [identb.name])
    b.ts("dve", tri_p.ap, io.ap, 0, None, ALU.is_ge, None, [io.name], [tri_p.name])
    b.memset("pool", ones.ap, 1.0, [ones.name])
    b.ts("dve", tmpf.ap, io.ap, 0, -32768.0, ALU.is_lt, ALU.mult, [io.name], [tmpf.name])
    b.cp("dve", neg_p4.ap, tmpf.ap.unsqueeze(1).to_broadcast([128, 4, 128]), [tmpf.name], [neg_p4.name])
    P.add("pool", lambda e: e.iota(tmpi.ap[:, 0:16], [[8, 16]], base=0, channel_multiplier=-1), (), [tmpi.name])
    b.ts("dve", tmpf.ap[:, 0:16], tmpi.ap[:, 0:16], 0, None, ALU.is_le, None, [tmpi.name], [tmpf.name])
    b.ts("dve", tmpg.ap[:, 0:16], tmpi.ap[:, 0:16], -7, None, ALU.is_ge, None, [tmpi.name], [tmpg.name])
    b.tt("dve", seqmask.ap, tmpf.ap[:, 0:16], tmpg.ap[:, 0:16], ALU.mult, [tmpf.name, tmpg.name], [seqmask.name])
    b.cp("dve", blk_s.ap.rearrange("p (s l) -> p s l", s=16), seqmask.ap.unsqueeze(2).to_broadcast([128, 16, 8]),
         [seqmask.name], [blk_s.name])
    b.tt("dve", tri_s.ap, tri_p.ap, blk_s.ap, ALU.mult, [tri_p.name, blk_s.name], [tri_s.name])
    b.ts("dve", tmpf.ap, tri_s.ap, -1.0, 32768.0, ALU.add, ALU.mult, [tri_s.name], [tmpf.name])
    b.cp("dve", neg_s4.ap, tmpf.ap.unsqueeze(1).to_broadcast([128, 4, 128]), [tmpf.name], [neg_s4.name])
    P.add("pool", lambda e: e.iota(tmpi.ap, [[0, 16], [1, 8]], base=0, channel_multiplier=0), (), [tmpi.name])
    b.ts("dve", lmask.ap.rearrange("p s l -> p (s l)"), tmpi.ap, 0, None, ALU.is_gt, None, [tmpi.name], [lmask.name])
    P.add("pool", lambda e: e.iota(tmpi.ap, [[1, 128]], base=1, channel_multiplier=0), (), [tmpi.name])
    b.cp("dve", tau1.ap, tmpi.ap, [tmpi.name], [tau1.name])
    b.release(cm)

    def load_cols(name, src_row, ncols):
        t = b.alloc(name, [ncols], F32)
        b.dma("sp", t.ap, src_row.rearrange("o (k p) -> p (o k)", p=128), [], [t.name], **NCD)
        return t

    def load_bcast(name, src_row, n):
        t = b.alloc(name, [n], F32)
        b.dma("sp", t.ap, src_row.partition_broadcast(128), [], [t.name])
        return t

    nmix0 = load_cols("nmix0", norm_mix[0:1, :], 8)
    nmix1 = load_cols("nmix1", norm_mix[1:2, :], 8)
    nff0 = load_cols("nff0", norm_ff[0:1, :], 8)
    nff1 = load_cols("nff1", norm_ff[1:2, :], 8)

    def norm_tile(xt_ap, xt_keys, wcol, dst_ap, dst_key, scr):
        junk, ss, xsb = scr
        b.act(junk.ap, xt_ap, AF.Square, xt_keys, [junk.name, ss.name], accum=ss.ap[:, 0:1])
        b.act(ss.ap[:, 1:2], ss.ap[:, 0:1], AF.Sqrt, [ss.name], [ss.name], bias=EPS, scale=1.0 / D)
        b.recip(ss.ap[:, 2:3], ss.ap[:, 1:2], [ss.name], [ss.name])
        b.ts("dve", xsb.ap, xt_ap, ss.ap[:, 2:3], None, ALU.mult, None, xt_keys + [ss.name], [xsb.name])
        if cut == 81:
            return
        tb = tbank[norm_tile.i % 2]
        norm_tile.i += 1
        for k in range(8):
            b.tr(tb.ap[:, k * 128:(k + 1) * 128], xsb.ap[:, k * 128:(k + 1) * 128], identb.ap,
                 [xsb.name, identb.name], [tb.name])
        if cut == 82:
            return
        b.tt("dve", dst_ap, tb.ap.rearrange("p (k t) -> p k t", k=8),
             wcol.ap.unsqueeze(2).to_broadcast([128, 8, 128]), ALU.mult, [tb.name, wcol.name], [dst_key])
    norm_tile.i = 0

    def norm_scratch():
        return (b.alloc("junk", [1024], BF16), b.alloc("ss", [4], F32), b.alloc("xsb", [1024], BF16))

    GROUPS = [(0, 4), (4, 4), (8, 4), (12, 4), (16, 1)]

    YB = b.alloc("YB", [8, NTOK], BF16)

    if "B" not in skip:
        mB = b.mark()
        LR = b.alloc("LR", [32], F32)
        LI = b.alloc("LI", [32], F32)
        ST = b.alloc("ST", [32], F32)
        MG = b.alloc("MG", [32], F32)
        TH = b.alloc("TH", [32], F32)
        KR = b.alloc("KR", [32], F32)
        KI = b.alloc("KI", [32], F32)
        COS = b.alloc("COS", [32, 128], F32)
        SIN = b.alloc("SIN", [32, 128], F32)
        BTz = b.alloc("BTz", [32, 2, 128], BF16)
        CTz = b.alloc("CTz", [32, 2, 128], BF16)
        dskip = load_cols("dskip", s5_d, 8)
        bglu = load_cols("bglu", b_glu, 8)
        for gl in range(2):
            rows = slice(64 * gl, 64 * gl + 64)
            b.dma("sp", LR.ap[rows, :], lam_re.rearrange("(gp gl) p -> gl p gp", gl=2)[gl], [], [LR.name], **NCD, par=True)
            b.dma("sp", LI.ap[rows, :], lam_im.rearrange("(gp gl) p -> gl p gp", gl=2)[gl], [], [LI.name], **NCD, par=True)
            b.dma("sp", ST.ap[rows, :], log_step.rearrange("o (gp gl) -> gl o gp", gl=2)[gl].partition_broadcast(64),
                  [], [ST.name], **NCD, par=True)

        def sin_rr(out_ap, in_ap, shift, t1, t2, keys_r, key_w, shape_kw=None):
            rk = keys_r
            b.ts("dve", out_ap, in_ap, 1.0 / TWO_PI, shift / TWO_PI, ALU.mult, ALU.add, rk, [key_w])
            b.cp("dve", t1.ap, out_ap, [key_w], [t1.name])
            b.cp("dve", t2.ap, t1.ap, [t1.name], [t2.name])
            b.ts("dve", out_ap, in_ap, shift, None, ALU.add, None, rk + [t1.name], [key_w])
            b.stt(out_ap, t2.ap, -TWO_PI, out_ap, ALU.mult, ALU.add, [t2.name, key_w], [key_w])
            b.ts("dve", t2.ap, out_ap, math.pi, -TWO_PI, ALU.is_gt, ALU.mult, [key_w], [t2.name])
            b.tt("dve", out_ap, out_ap, t2.ap, ALU.add, [key_w, t2.name], [key_w])
            b.ts("dve", t2.ap, out_ap, -math.pi, TWO_PI, ALU.is_lt, ALU.mult, [key_w], [t2.name])
            b.tt("dve", out_ap, out_ap, t2.ap, ALU.add, [key_w, t2.name], [key_w])
            b.act(out_ap, out_ap, AF.Sin, [key_w], [key_w])

        if cut == 1:
            P.finalize(final_keys=b.out_keys)
            return b
        mT = b.mark()
        ABR = b.alloc("ABR", [32], F32)
        ABI = b.alloc("ABI", [32], F32)
        w1 = b.alloc("w1", [32], F32)
        w2 = b.alloc("w2", [32], F32)
        w3 = b.alloc("w3", [32], F32)
        wi = b.alloc("wi", [32], I32)
        b.act(ST.ap, ST.ap, AF.Exp, [ST.name], [ST.name])
        b.tt("dve", w1.ap, LR.ap, ST.ap, ALU.mult, [LR.name, ST.name], [w1.name])
        b.act(MG.ap, w1.ap, AF.Exp, [w1.name], [MG.name])
        b.tt("dve", TH.ap, LI.ap, ST.ap, ALU.mult, [LI.name, ST.name], [TH.name])
        sin_rr(ABR.ap, TH.ap, math.pi / 2, wi, w2, [TH.name], ABR.name)
        sin_rr(ABI.ap, TH.ap, 0.0, wi, w2, [TH.name], ABI.name)
        b.tt("dve", ABR.ap, ABR.ap, MG.ap, ALU.mult, [ABR.name, MG.name], [ABR.name])
        b.tt("dve", ABI.ap, ABI.ap, MG.ap, ALU.mult, [ABI.name, MG.name], [ABI.name])
        b.tt("dve", w1.ap, LR.ap, LR.ap, ALU.mult, [LR.name], [w1.name])
        b.tt("dve", w2.ap, LI.ap, LI.ap, ALU.mult, [LI.name], [w2.name])
        b.tt("dve", w1.ap, w1.ap, w2.ap, ALU.add, [w1.name, w2.name], [w1.name])
        b.recip(w3.ap, w1.ap, [w1.name], [w3.name])
        b.ts("dve", w1.ap, ABR.ap, -1.0, None, ALU.add, None, [ABR.name], [w1.name])
        b.tt("dve", KR.ap, w1.ap, LR.ap, ALU.mult, [w1.name, LR.name], [KR.name])
        b.tt("dve", w2.ap, ABI.ap, LI.ap, ALU.mult, [ABI.name, LI.name], [w2.name])
        b.tt("dve", KR.ap, KR.ap, w2.ap, ALU.add, [KR.name, w2.name], [KR.name])
        b.tt("dve", KR.ap, KR.ap, w3.ap, ALU.mult, [KR.name, w3.name], [KR.name])
        b.tt("dve", KI.ap, ABI.ap, LR.ap, ALU.mult, [ABI.name, LR.name], [KI.name])
        b.tt("dve", w2.ap, w1.ap, LI.ap, ALU.mult, [w1.name, LI.name], [w2.name])
        b.tt("dve", KI.ap, KI.ap, w2.ap, ALU.subtract, [KI.name, w2.name], [KI.name])
        b.tt("dve", KI.ap, KI.ap, w3.ap, ALU.mult, [KI.name, w3.name], [KI.name])
        b.release(mT)
        if cut == 2:
            P.finalize(final_keys=b.out_keys)
            return b
        mT = b.mark()
        ti = b.alloc("ti", [8, 128], I32)
        tf = b.alloc("tf", [8, 128], F32)
        ang = b.alloc("ang", [8, 128], F32)
        for s8 in range(4):
            sl = slice(8 * s8, 8 * s8 + 8)
            b.tt("dve", ang.ap, TH.ap[:, sl].unsqueeze(2).to_broadcast([128, 8, 128]),
                 tau1.ap.unsqueeze(1).to_broadcast([128, 8, 128]), ALU.mult, [TH.name, tau1.name], [ang.name])
            sin_rr(COS.ap[:, sl, :], ang.ap, math.pi / 2, ti, tf, [ang.name], COS.name)
            sin_rr(SIN.ap[:, sl, :], ang.ap, 0.0, ti, tf, [ang.name], SIN.name)
        b.release(mT)
        if cut == 3:
            P.finalize(final_keys=b.out_keys)
            return b
        mT = b.mark()
        BR = b.alloc("BR", [32, 16], F32)
        BI = b.alloc("BI", [32, 16], F32)
        BBR = b.alloc("BBR", [32, 16], F32)
        BBI = b.alloc("BBI", [32, 16], F32)
        BX = b.alloc("BX", [32, 128], F32)
        for gl in range(2):
            rows = slice(64 * gl, 64 * gl + 64)
            b.dma("sp", BR.ap[rows], s5_b_re.rearrange("(gp gl) p c -> gl p gp c", gl=2)[gl], [], [BR.name], par=True)
            b.dma("sp", BI.ap[rows], s5_b_im.rearrange("(gp gl) p c -> gl p gp c", gl=2)[gl], [], [BI.name], par=True)
        krb = KR.ap.unsqueeze(2).to_broadcast([128, 32, 16])
        kib = KI.ap.unsqueeze(2).to_broadcast([128, 32, 16])
        b.tt("dve", BBR.ap, BR.ap, krb, ALU.mult, [BR.name, KR.name], [BBR.name])
        b.tt("dve", BX.ap[:, :, 0:16], BI.ap, kib, ALU.mult, [BI.name, KI.name], [BX.name])
        b.tt("dve", BBR.ap, BBR.ap, BX.ap[:, :, 0:16], ALU.subtract, [BBR.name, BX.name], [BBR.name])
        b.tt("dve", BBI.ap, BI.ap, krb, ALU.mult, [BI.name, KR.name], [BBI.name])
        b.tt("dve", BX.ap[:, :, 0:16], BR.ap, kib, ALU.mult, [BR.name, KI.name], [BX.name])
        b.tt("dve", BBI.ap, BBI.ap, BX.ap[:, :, 0:16], ALU.add, [BBI.name, BX.name], [BBI.name])
        for ri, src in enumerate((BBR, BBI)):
            b.memset("pool", BX.ap, 0.0, [BX.name])
            bxv = BX.ap.rearrange("p (q j) c -> p q j c", j=4)
            srcv = src.ap.rearrange("p (q j) c -> p q j c", j=4)
            for gl in range(2):
                rows = slice(64 * gl, 64 * gl + 64)
                for j in range(4):
                    c0 = 32 * j + 16 * gl
                    b.cp("dve", bxv[rows, :, j, c0:c0 + 16], srcv[rows, :, j, :], [src.name, BX.name], [BX.name])
            for gp in range(32):
                bk = bank[gp % 2]
                b.tr(bk.ap[:, 0:128], BX.ap[:, gp, :], identf.ap, [BX.name, identf.name], [bk.name])
                b.cp("act", BTz.ap[:, gp, ri, :], bk.ap[:, 0:128], [bk.name], [BTz.name])
        b.release(mT)
        if cut == 4:
            P.finalize(final_keys=b.out_keys)
            return b
        mT = b.mark()
        CRn = b.alloc("CRn", [8, 2, 64], F32)
        b.memset("pool", CTz.ap, 0.0, [CTz.name])
        ci_ = 0
        for ri, src in enumerate((s5_c_re, s5_c_im)):
            for dup in range(2):
                b.dma("sp", CRn.ap[:, :, dup, :], src.rearrange("(k g8) c p -> (g8 c) k p", g8=8), [], [CRn.name], par=True)
            for k in range(8):
                bk = bank[k % 2]
                b.mm(bk.ap[:, 0:128], CRn.ap[:, k, :, :].rearrange("p d q -> p (d q)"), identf.ap, True, True,
                     [CRn.name, identf.name], [bk.name])
                for gl in range(2):
                    rows = slice(64 * gl, 64 * gl + 64)
                    for j in range(4):
                        c0 = 32 * j + 16 * gl
                        eng_ = "dve" if ci_ % 2 else "act"
                        ci_ += 1
                        if eng_ == "dve":
                            b.ts("dve", CTz.ap[rows, 4 * k + j, ri, c0:c0 + 16], bk.ap[rows, c0:c0 + 16],
                                 (1.0 if ri == 0 else -1.0), None, ALU.mult, None, [bk.name, CTz.name], [CTz.name])
                        else:
                            b.act(CTz.ap[rows, 4 * k + j, ri, c0:c0 + 16], bk.ap[rows, c0:c0 + 16], AF.Copy,
                                  [bk.name, CTz.name], [CTz.name], scale=(1.0 if ri == 0 else -1.0))
        b.release(mT)
        if cut == 5:
            P.finalize(final_keys=b.out_keys)
            return b
        WU = b.alloc("WU", [8, 1024], BF16)
        WG = b.alloc("WG", [8, 1024], BF16)
        for k in range(8):
            b.dma("pool", WU.ap[:, k, :], w_in[k * 128:(k + 1) * 128, A_PROJ:IN_COLS], [], [WU.name], par=True)
            b.dma("pool", WG.ap[:, k, :], w_glu[k * 128:(k + 1) * 128, :], [], [WG.name], par=True)

        if cut == 6:
            P.finalize(final_keys=b.out_keys)
            return b
        CAR = [b.alloc("CARr", [32], F32), b.alloc("CARi", [32], F32)]
        H0M = [b.alloc("H0Mr", [32, 16], F32), b.alloc("H0Mi", [32, 16], F32)]
        SOUT = H0M
        b.memset("pool", CAR[0].ap, 0.0, [CAR[0].name])
        b.memset("pool", CAR[1].ap, 0.0, [CAR[1].name])
        mT = b.mark()
        h0n = b.alloc("h0n", [4096], F32)
        for ri, src in enumerate((sb_re, sb_im)):
            b.dma("sp", h0n.ap[0:16, :], src, [], [h0n.name])
            for gp in range(32):
                bk = bank[gp % 2]
                b.tr(bk.ap[:, 0:16], h0n.ap[0:16, gp * 128:(gp + 1) * 128], identf.ap[0:16, 0:16],
                     [h0n.name, identf.name], [bk.name])
                b.ts("dve", H0M[ri].ap[:, gp, :], bk.ap[:, 0:16], MG.ap[:, gp:gp + 1], None, ALU.mult, None,
                     [bk.name, MG.name], [H0M[ri].name])
        b.release(mT)

        if cut == 7:
            P.finalize(final_keys=b.out_keys)
            return b
        if stop_after == "B0":
            for nm, t_, shp, *dt_ in (("COS", COS, [128, 32, 128]), ("SIN", SIN, [128, 32, 128]), ("BTz", BTz, [128, 32, 2, 128], BF16),
                                ("CTz", CTz, [128, 32, 2, 128], BF16), ("KR", KR, [128, 32]), ("KI", KI, [128, 32]),
                                ("MG", MG, [128, 32]), ("H0Mr", H0M[0], [128, 32, 16])):
                b.debug_dump(nm, t_, shp, *dt_)
            P.finalize(final_keys=b.out_keys)
            return b
        mS = b.mark()
        scrN = norm_scratch()
        MSq = b.alloc("MSq", [4, 128], F32)
        XT = [b.alloc("XT0", [1024], F32)]
        XNg = b.alloc("XNg", [8, 256], BF16)
        U16 = b.alloc("U16", [8, 256], BF16)
        SETS = []
        Q1s = b.alloc("Q1", [512], F32)
        for si in range(2):
            SETS.append(dict(RR=b.alloc("RR", [512], F32), RI=b.alloc("RI", [512], F32), Q1=Q1s,
                             HR=b.alloc("HR", [512], F32),
                             HI=b.alloc("HI", [512], F32), HRb=b.alloc("HRb", [512], BF16), HIb=b.alloc("HIb", [512], BF16),
                             bu=(bank[2 * si], bank[2 * si + 1])))
        HL = [b.alloc("HLr", [32], F32), b.alloc("HLi", [32], F32)]
        HSm = [b.alloc("HSr", [32, 16], F32), b.alloc("HSi", [32, 16], F32)]
        cq = [b.alloc("cq1", [32], F32), b.alloc("cq2", [32], F32)]
        Y32s = [b.alloc("Y32a", [8, 128], F32), b.alloc("Y32b", [8, 128], F32)]
        CARM = [b.alloc("CARMr", [32], F32), b.alloc("CARMi", [32], F32)]
        b.memset("pool", CARM[0].ap, 0.0, [CARM[0].name])
        b.memset("pool", CARM[1].ap, 0.0, [CARM[1].name])
        mask0 = b.alloc("mask0", [128], F32)
        b.ts("dve", mask0.ap, tau1.ap, 1.5, None, ALU.is_gt, None, [tau1.name], [mask0.name])
        G1 = b.alloc("G1", [8, 128], F32)
        YGb = b.alloc("YGb", [8, 128], BF16)
        SG = b.alloc("SG", [128], F32)
        U16s = [U16, b.alloc("U16b", [8, 256], BF16)]
        ck = [COS.name, SIN.name]
        grp = [(2 * i, 2) for i in range(8)] + [(16, 1)]

        def views(sample, pr):
            if sample:
                cosq = COS.ap[:, pr, 0:8].unsqueeze(2).to_broadcast([128, 4, 16, 8])
                sinq = SIN.ap[:, pr, 0:8].unsqueeze(2).to_broadcast([128, 4, 16, 8])
                v = lambda tile_: tile_.ap.rearrange("p (j s l) -> p j s l", j=4, s=16)
            else:
                cosq = COS.ap[:, pr, :]
                sinq = SIN.ap[:, pr, :]
                v = lambda tile_: tile_.ap.rearrange("p (j t) -> p j t", j=4)
            return cosq, sinq, v

        SSg = [b.alloc("ssg0", [4], F32), b.alloc("ssg1", [4], F32)]
        junk_, _ss_unused, xsb_ = scrN

        def gf_stats(gi, tt_):
            t = grp[gi][0] + tt_
            ss = SSg[tt_]
            xt = XT[0]
            b.dma("sp", xt.ap, x_src(t), [], [xt.name])
            b.act(junk_.ap, xt.ap, AF.Square, [xt.name], [junk_.name, ss.name], accum=ss.ap[:, 0:1])
            b.act(ss.ap[:, 1:2], ss.ap[:, 0:1], AF.Sqrt, [ss.name], [ss.name], bias=EPS, scale=1.0 / D)

        def gf_scale(gi, tt_):
            ss = SSg[tt_]
            xt = XT[0]
            b.recip(ss.ap[:, 2:3], ss.ap[:, 1:2], [ss.name], [ss.name])
            b.ts("dve", xsb_.ap, xt.ap, ss.ap[:, 2:3], None, ALU.mult, None, [xt.name, ss.name], [xsb_.name])
            tb = tbank[tt_]
            for k in range(8):
                b.tr(tb.ap[:, k * 128:(k + 1) * 128], xsb_.ap[:, k * 128:(k + 1) * 128], identb.ap,
                     [xsb_.name, identb.name], [tb.name])

        def gf_evac(gi, tt_):
            tb = tbank[tt_]
            b.tt("dve", XNg.ap[:, :, tt_ * 128:(tt_ + 1) * 128], tb.ap.rearrange("p (k t) -> p k t", k=8),
                 nmix0.ap.unsqueeze(2).to_broadcast([128, 8, 128]), ALU.mult, [tb.name, nmix0.name], [XNg.name])

        def gf_proj(gi):
            n = grp[gi][1] * 128
            U = U16s[gi % 2]
            for q in range(8):
                bk = bank[4 + q % 2]
                for k in range(8):
                    b.mm(bk.ap[:, 0:n], WU.ap[:, k, q * 128:(q + 1) * 128], XNg.ap[:, k, 0:n], k == 0, k == 7,
                         [WU.name, XNg.name], [bk.name])
                b.cp("act", U.ap[:, q, 0:n], bk.ap[:, 0:n], [bk.name], [U.name])

        def gf_stages(gi):
            nt_ = grp[gi][1]
            st = [[lambda: gf_stats(gi, 0)], [lambda: gf_scale(gi, 0)], [lambda: gf_evac(gi, 0)], []]
            if nt_ == 2:
                st[1].append(lambda: gf_stats(gi, 1))
                st[2].append(lambda: gf_scale(gi, 1))
                st[3].append(lambda: gf_evac(gi, 1))
            st[3].append(lambda: gf_proj(gi))
            return st

        def S1(it):
            gi, t, tt_, q, idx = it
            S_ = SETS[idx % 2]
            U = U16s[gi % 2]
            sample = (t == 16)
            cs = slice(tt_ * 128, (tt_ + 1) * 128)
            RR, RI, Q1, HR, HI, HRb, HIb = (S_[k_] for k_ in ("RR", "RI", "Q1", "HR", "HI", "HRb", "HIb"))
            bA, bB = S_["bu"]
            pr = slice(4 * q, 4 * q + 4)
            for ri, bk in enumerate((bA, bB)):
                for j in range(4):
                    b.mm(bk.ap[:, j * 128:(j + 1) * 128], BTz.ap[:, 4 * q + j, ri, :], U.ap[:, q, cs], True, True,
                         [BTz.name, U.name], [bk.name])
            cosq, sinq, v = views(sample, pr)
            mrow = (lmask.ap.rearrange("p s l -> p (s l)") if sample else mask0.ap)
            b.tt("dve", MSq.ap, MG.ap[:, pr].unsqueeze(2).to_broadcast([128, 4, 128]),
                 mrow.unsqueeze(1).to_broadcast([128, 4, 128]), ALU.mult, [MG.name, lmask.name, mask0.name], [MSq.name])
            A_, B_ = v(bA), v(bB)
            b.tt("dve", v(RR), A_, cosq, ALU.mult, [bA.name] + ck, [RR.name])
            b.tt("dve", v(Q1), B_, sinq, ALU.mult, [bB.name] + ck, [Q1.name])
            b.tt("dve", v(RR), v(RR), v(Q1), ALU.add, [RR.name, Q1.name], [RR.name])
            b.tt("dve", v(RI), B_, cosq, ALU.mult, [bB.name] + ck, [RI.name])
            b.tt("dve", v(Q1), A_, sinq, ALU.mult, [bA.name] + ck, [Q1.name])
            b.tt("dve", v(RI), v(RI), v(Q1), ALU.subtract, [RI.name, Q1.name], [RI.name])
            for ri, (rt, ht) in enumerate(((RR, HR), (RI, HI))):
                if sample:
                    b.tt("dve", v(rt)[:, :, :, 0], v(rt)[:, :, :, 0], H0M[ri].ap[:, pr, :], ALU.add,
                         [rt.name, H0M[ri].name], [rt.name])
                else:
                    b.tt("dve", v(rt)[:, :, 0], v(rt)[:, :, 0], CARM[ri].ap[:, pr], ALU.add,
                         [rt.name, CARM[ri].name], [rt.name])
                b.scan(ht.ap, MSq.ap.rearrange("p j t -> p (j t)"), rt.ap, 0.0, [rt.name, MSq.name], [ht.name])
                if sample:
                    b.cp("dve", HSm[ri].ap[:, pr, :], v(ht)[:, :, :, 7], [ht.name], [HSm[ri].name])
                else:
                    b.cp("dve", HL[ri].ap[:, pr], v(ht)[:, :, 127], [ht.name], [HL[ri].name])
            if q == 7 and not sample:
                c127, s127 = COS.ap[:, :, 127], SIN.ap[:, :, 127]
                b.tt("dve", cq[0].ap, HL[0].ap, c127, ALU.mult, [HL[0].name, COS.name], [cq[0].name])
                b.tt("dve", cq[1].ap, HL[1].ap, s127, ALU.mult, [HL[1].name, SIN.name], [cq[1].name])
                b.tt("dve", CAR[0].ap, cq[0].ap, cq[1].ap, ALU.subtract, [cq[0].name, cq[1].name], [CAR[0].name])
                b.tt("dve", cq[0].ap, HL[0].ap, s127, ALU.mult, [HL[0].name, SIN.name], [cq[0].name])
                b.tt("dve", cq[1].ap, HL[1].ap, c127, ALU.mult, [HL[1].name, COS.name], [cq[1].name])
                b.tt("dve", CAR[1].ap, cq[0].ap, cq[1].ap, ALU.add, [cq[0].name, cq[1].name], [CAR[1].name])
                for ri in range(2):
                    b.tt("dve", CARM[ri].ap, CAR[ri].ap, MG.ap, ALU.mult, [CAR[ri].name, MG.name], [CARM[ri].name])
            b.tt("dve", v(Q1), v(HR), cosq, ALU.mult, [HR.name] + ck, [Q1.name])
            b.tt("dve", v(RR), v(HI), sinq, ALU.mult, [HI.name] + ck, [RR.name])
            b.tt("dve", v(HRb), v(Q1), v(RR), ALU.subtract, [Q1.name, RR.name], [HRb.name])
            b.tt("dve", v(Q1), v(HR), sinq, ALU.mult, [HR.name] + ck, [Q1.name])
            b.tt("dve", v(RR), v(HI), cosq, ALU.mult, [HI.name] + ck, [RR.name])
            b.tt("dve", v(HIb), v(Q1), v(RR), ALU.add, [Q1.name, RR.name], [HIb.name])

        def S3(it):
            gi, t, tt_, q, idx = it
            S_ = SETS[idx % 2]
            U = U16s[gi % 2]
            Y32 = Y32s[t % 2]
            cs = slice(tt_ * 128, (tt_ + 1) * 128)
            bk = bank[4 + q % 2]
            i = 0
            for j in range(4):
                for ri, hb in enumerate((S_["HRb"], S_["HIb"])):
                    b.mm(bk.ap[:, 0:128], CTz.ap[:, 4 * q + j, ri, :], hb.ap[:, j * 128:(j + 1) * 128], i == 0, i == 7,
                         [CTz.name, hb.name], [bk.name])
                    i += 1
            b.stt(Y32.ap[:, q, :], U.ap[:, q, cs], dskip.ap[:, q:q + 1], bk.ap[:, 0:128], ALU.mult, ALU.add,
                  [U.name, dskip.name, bk.name], [Y32.name])

        def TL0(t):
            b.act(G1.ap, Y32s[t % 2].ap, AF.Square, [Y32s[t % 2].name], [G1.name])

        def TL1(t):
            Y32 = Y32s[t % 2]
            b.ts("dve", G1.ap, G1.ap, 0.044715, 1.0, ALU.mult, ALU.add, [G1.name], [G1.name])
            b.tt("dve", G1.ap, G1.ap, Y32.ap, ALU.mult, [G1.name, Y32.name], [G1.name])
            b.act(G1.ap, G1.ap, AF.Sigmoid, [G1.name], [G1.name], scale=1.5957691216057308)

        def TL2(t):
            Y32 = Y32s[t % 2]
            b.tt("dve", Y32.ap, Y32.ap, G1.ap, ALU.mult, [Y32.name, G1.name], [Y32.name])
            b.cp("act", YGb.ap, Y32.ap, [Y32.name], [YGb.name])

        def TL3(t):
            for o in range(8):
                bk = bank[4 + o % 2]
                for k in range(8):
                    b.mm(bk.ap[:, 0:128], WG.ap[:, k, o * 128:(o + 1) * 128], YGb.ap[:, k, :], k == 0, k == 7,
                         [WG.name, YGb.name], [bk.name])
                b.act(G1.ap[:, o, :], bk.ap[:, 0:128], AF.Sigmoid, [bk.name, bglu.name], [G1.name], bias=bglu.ap[:, o:o + 1])

        def TL4(t):
            tok = slice(t * 128, (t + 1) * 128)
            b.tt("dve", YB.ap[:, :, tok], Y32s[t % 2].ap, G1.ap, ALU.mult, [Y32s[t % 2].name, G1.name], [YB.name])

        items = []
        first_item = {}
        for gi, (t0, ntile) in enumerate(grp):
            for tt_ in range(ntile):
                for q in range(8):
                    if gi not in first_item:
                        first_item[gi] = len(items)
                    items.append((gi, t0 + tt_, tt_, q, len(items)))
        NI = len(items)
        prelude = {}
        for gi in range(len(grp)):
            for k_, fl in enumerate(gf_stages(gi)):
                prelude.setdefault(max(0, first_item[gi] - 5 + k_), []).extend(fl)
        tails = []
        TLS = (TL0, TL1, TL2, TL3, TL4)
        for step in range(NI + 8):
            for f_ in prelude.get(step, ()):
                f_()
            if step < NI:
                S1(items[step])
            nt_ = []
            for (tc, ms) in tails:
                TLS[ms](tc)
                if ms < 4:
                    nt_.append((tc, ms + 1))
            tails = nt_
            if 1 <= step <= NI:
                it = items[step - 1]
                S3(it)
                if it[3] == 7:
                    tails.append((it[1], 0))
        assert not tails
        mQ = b.mark()
        sq_ = [T(Q1s.name, Q1s.ap.rearrange("p (a c) -> p a c", a=32)), T(SETS[0]["RR"].name, SETS[0]["RR"].ap.rearrange("p (a c) -> p a c", a=32))]
        c7 = COS.ap[:, :, 7].unsqueeze(2).to_broadcast([128, 32, 16])
        s7 = SIN.ap[:, :, 7].unsqueeze(2).to_broadcast([128, 32, 16])
        b.tt("dve", sq_[0].ap, HSm[0].ap, c7, ALU.mult, [HSm[0].name, COS.name], [sq_[0].name])
        b.tt("dve", sq_[1].ap, HSm[1].ap, s7, ALU.mult, [HSm[1].name, SIN.name], [sq_[1].name])
        b.tt("dve", SOUT[0].ap, sq_[0].ap, sq_[1].ap, ALU.subtract, [sq_[0].name, sq_[1].name], [SOUT[0].name])
        b.tt("dve", sq_[0].ap, HSm[0].ap, s7, ALU.mult, [HSm[0].name, SIN.name], [sq_[0].name])
        b.tt("dve", sq_[1].ap, HSm[1].ap, c7, ALU.mult, [HSm[1].name, COS.name], [sq_[1].name])
        b.tt("dve", SOUT[1].ap, sq_[0].ap, sq_[1].ap, ALU.add, [sq_[0].name, sq_[1].name], [SOUT[1].name])
        b.release(mQ)
        b.release(mS)
        mT = b.mark()
        so = b.alloc("so", [4096], F32)
        for ri, (dst_s, dst_p) in enumerate(((s_b_re, p_b_re), (s_b_im, p_b_im))):
            for gp in range(32):
                bk = bank[gp % 2]
                b.tr(bk.ap[0:16, 0:128], SOUT[ri].ap[:, gp, :], identf.ap, [SOUT[ri].name, identf.name], [bk.name])
                b.cp("act", so.ap[0:16, gp * 128:(gp + 1) * 128], bk.ap[0:16, 0:128], [bk.name], [so.name])
            b.dma("sp", dst_s, so.ap[0:16, :], [so.name], ["s_b_re" if ri == 0 else "s_b_im"])
            bk = bank[2]
            b.tr(bk.ap[0:32, 0:128], CAR[ri].ap, identf.ap, [CAR[ri].name, identf.name], [bk.name])
            b.cp("act", so.ap[0:32, 0:128], bk.ap[0:32, 0:128], [bk.name], [so.name])
            b.dma("sp", dst_p, so.ap[0:32, 0:128], [so.name], ["p_b_re" if ri == 0 else "p_b_im"])
        b.release(mT)
        if debug:
            b.debug_dump("YB", YB, [128, 8, NTOK], BF16)
        b.release(mB)
    if stop_after == "B":
        P.finalize(final_keys=b.out_keys)
        return b


    YA = b.alloc("YA", [8, NTOK], BF16)
    if "A" not in skip:
        mA = b.mark()
        WA = b.alloc("WA", [8, A_PROJ], BF16)
        for k in range(8):
            b.dma("pool", WA.ap[:, k, :], w_in[k * 128:(k + 1) * 128, 0:A_PROJ], [], [WA.name], par=True)
        cw = b.alloc("cw", [16, 4], F32)
        for w_ in range(4):
            b.dma("sp", cw.ap[:, :, w_], a_conv_w[w_:w_ + 1, :].rearrange("o (c p) -> p (o c)", p=128), [], [cw.name], **NCD)
        cbias = load_cols("cbias", a_conv_b, 16)
        dtb = load_bcast("dtb", a_dt_bias, 16)
        abc = load_bcast("abc", a_log, 16)
        dbc = load_bcast("dbc", a_d, 16)
        anorm = load_bcast("anorm", a_norm, 1024)
        b.act(abc.ap, abc.ap, AF.Exp, [abc.name], [abc.name])
        b.ts("dve", abc.ap, abc.ap, -1.0, None, ALU.mult, None, [abc.name], [abc.name])
        SEL = b.alloc("SEL", [16, 128], BF16)
        mT = b.mark()
        seli = b.alloc("seli", [16, 128], I32)
        P.add("pool", lambda e: e.iota(seli.ap, [[-1, 16], [1, 16], [0, 8]], base=0, channel_multiplier=0), (), [seli.name])
        b.ts("dve", SEL.ap, seli.ap, 0, None, ALU.is_equal, None, [seli.name], [SEL.name])
        b.release(mT)

        scrN = norm_scratch()
        mT = b.mark()
        XT0 = b.alloc("XT0a", [1024], F32)
        b.release(mT)
        ZS = b.alloc("ZS", [1024], F32)
        XNg = b.alloc("XNgA", [8, 256], BF16)
        XPRE = b.alloc("XPRE", [16, 259], F32)
        XC = b.alloc("XCv", [16, 256], BF16)
        XCa = b.alloc("XCa", [256], F32)
        XTM = b.alloc("XTM", [1024], BF16)
        BTM = b.alloc("BTM", [512], BF16)
        sm = b.alloc("sm", [16, 16], F32)
        Rt = b.alloc("Rt", [4, 128], F32)
        Lt = b.alloc("Lt", [4, 128], F32)
        MT = b.alloc("MT", [16, 128], BF16)
        XDT = b.alloc("XDT", [1024], BF16)
        XDE = b.alloc("XDE", [1024], BF16)
        mT = b.mark()
        HIST = b.alloc("HIST", [2048], F32)
        b.release(mT)
        H0 = b.alloc("H0", [8, 128], F32)
        NS = b.alloc("NS", [8, 128], F32)
        b.release(mT)
        T1 = b.alloc("T1", [1024], F32)
        T2 = b.alloc("T2", [1024], F32)
        YAt = b.alloc("YAt", [1024], BF16)
        STf = b.alloc("STf", [1024], F32)
        STb = b.alloc("STb", [1024], BF16)
        STs = b.alloc("STs", [1024], BF16)
        CTm = b.alloc("CTm", [4, 128], BF16)
        Bm = b.alloc("Bm", [512], BF16)
        CDT = b.alloc("CDT", [8, 16], F32)
        c48 = b.alloc("c48", [48], F32)
        b.memset("pool", STf.ap, 0.0, [STf.name])
        b.memset("pool", STb.ap, 0.0, [STb.name])
        b.memset("pool", XPRE.ap, 0.0, [XPRE.name])
        (R_PRE, R_ABS, R_E, R_L, R_DT, R_DA, R_CUM, R_NCUM, R_DEND, R_ECUM, R_CD, R_DTD, R_MS, R_RS) = range(14)

        def smr(i, n=16):
            return sm.ap[:, i, 0:n]

        for (t0, ntile) in [(2 * i, 2) for i in range(8)] + [(16, 1)]:
            n = ntile * 128
            sample_g = (t0 == 16)
            for tt_ in range(ntile):
                t = t0 + tt_
                b.dma("sp", XT0.ap, x_src(t), [], [XT0.name])
                norm_tile(XT0.ap, [XT0.name], nmix0, XNg.ap[:, :, tt_ * 128:(tt_ + 1) * 128], XNg.name, scrN)
            if sample_g:
                b.dma("sp", HIST.ap[0:48, :], sa_conv, [], [HIST.name])
                xpv = XPRE.ap[:, :, 0:176].rearrange("p c (s r) -> p c s r", s=16)
                for c in range(16):
                    bk = bank[c % 2]
                    b.tr(bk.ap[:, 0:48], HIST.ap[0:48, c * 128:(c + 1) * 128], identf.ap[0:48, 0:48],
                         [HIST.name, identf.name], [bk.name])
                    b.cp("act", xpv[:, c, :, 0:3], bk.ap[:, 0:48].rearrange("p (s r) -> p s r", s=16), [bk.name], [XPRE.name])
            elif t0 > 0:
                b.cp("act", XPRE.ap[:, :, 0:3], XPRE.ap[:, :, 256:259], [XPRE.name], [XPRE.name])
            for c in range(16):
                bk = bank[c % 2]
                for k in range(8):
                    b.mm(bk.ap[:, 0:n], WA.ap[:, k, 1024 + c * 128:1024 + (c + 1) * 128], XNg.ap[:, k, 0:n], k == 0, k == 7,
                         [WA.name, XNg.name], [bk.name])
                if sample_g:
                    pre = xpv[:, c]
                    b.cp("act", pre[:, :, 3:11], bk.ap[:, 0:128].rearrange("p (s l) -> p s l", s=16), [bk.name], [XPRE.name])
                    sh = lambda w_: pre[:, :, w_:w_ + 8]
                    acc = XCa.ap[:, 0:128].rearrange("p (s l) -> p s l", s=16)
                    xco = XC.ap[:, c, 0:128].rearrange("p (s l) -> p s l", s=16)
                else:
                    pre = XPRE.ap[:, c, :]
                    b.cp("act", pre[:, 3:3 + n], bk.ap[:, 0:n], [bk.name], [XPRE.name])
                    sh = lambda w_: pre[:, w_:w_ + n]
                    acc = XCa.ap[:, 0:n]
                    xco = XC.ap[:, c, 0:n]
                b.ts("dve", acc, sh(3), cw.ap[:, c, 3:4], cbias.ap[:, c:c + 1], ALU.mult, ALU.add,
                     [XPRE.name, cw.name, cbias.name], [XCa.name])
                for w_ in range(3):
                    b.stt(acc, sh(w_), cw.ap[:, c, w_:w_ + 1], acc, ALU.mult, ALU.add, [XPRE.name, cw.name, XCa.name], [XCa.name])
                b.act(xco, acc, AF.Silu, [XCa.name], [XC.name])
            if t0 == 14 or sample_g:
                nr = 48 if sample_g else 3
                for c4 in range(4):
                    bk = bank[2 + c4 % 2]
                    for cc in range(4):
                        c = 4 * c4 + cc
                        if sample_g:
                            b.cp("dve", c48.ap.rearrange("p (s r) -> p s r", s=16), xpv[:, c, :, 8:11], [XPRE.name], [c48.name])
                            src_ap = c48.ap
                            rk = [c48.name]
                        else:
                            src_ap = XPRE.ap[:, c, 256:259]
                            rk = [XPRE.name]
                        b.tr(bk.ap[0:nr, cc * 128:(cc + 1) * 128], src_ap, identf.ap, rk + [identf.name], [bk.name])
                    b.cp("act", HIST.ap[0:nr, c4 * 512:(c4 + 1) * 512], bk.ap[0:nr, :], [bk.name], [HIST.name])
                if sample_g:
                    b.dma("sp", s_a_conv, HIST.ap[0:48, :], [HIST.name], ["s_a_conv"])
                else:
                    b.dma("sp", p_a_conv, HIST.ap[0:3, :], [HIST.name], ["p_a_conv"])
            for tt_ in range(ntile):
                t = t0 + tt_
                sample = (t == 16)
                cs = slice(tt_ * 128, (tt_ + 1) * 128)
                tok = slice(t * 128, (t + 1) * 128)
                tri = tri_s if sample else tri_p
                blk = blk_s if sample else ones
                neg4 = neg_s4 if sample else neg_p4
                tb = tbank[0]
                for c in range(12):
                    b.tr(tb.ap[:, (c % 8) * 128:(c % 8 + 1) * 128], XC.ap[:, c, cs], identb.ap, [XC.name, identb.name], [tb.name])
                    if c == 7:
                        b.cp("act", XTM.ap, tb.ap, [tb.name], [XTM.name])
                        tb = tbank[1]
                b.cp("act", BTM.ap, tb.ap[:, 0:512], [tb.name], [BTM.name])
                for hf in range(2):
                    bk = bank[3 + hf]
                    for k in range(8):
                        b.mm(bk.ap, XNg.ap[:, k, cs], WA.ap[:, k, hf * 512:(hf + 1) * 512], k == 0, k == 7,
                             [XNg.name, WA.name], [bk.name])
                    b.act(ZS.ap[:, hf * 512:(hf + 1) * 512], bk.ap, AF.Silu, [bk.name], [ZS.name])
                bk = bank[0]
                for k in range(8):
                    b.mm(bk.ap[:, 0:16], XNg.ap[:, k, cs], WA.ap[:, k, 3072:3088], k == 0, k == 7, [XNg.name, WA.name], [bk.name])
                smk = [sm.name]
                b.tt("dve", smr(R_PRE), bk.ap[:, 0:16], dtb.ap, ALU.add, [bk.name, dtb.name], smk)
                b.act(smr(R_ABS), smr(R_PRE), AF.Abs, smk, smk)
                b.act(smr(R_E), smr(R_ABS), AF.Exp, smk, smk, scale=-1.0)
                b.act(smr(R_L), smr(R_E), AF.Ln, smk, smk, bias=1.0)
                b.ts("dve", smr(R_DT), smr(R_PRE), 0.0, None, ALU.max, None, smk, smk)
                b.tt("dve", smr(R_DT), smr(R_DT), smr(R_L), ALU.add, smk, smk)
                b.tt("dve", smr(R_DA), smr(R_DT), abc.ap, ALU.mult, smk + [abc.name], smk)
                b.mm(bk.ap[:, 16:32], tri.ap, smr(R_DA), True, True, [tri.name, sm.name], [bk.name])
                b.mm(bk.ap[:, 32:48], blk.ap, smr(R_DA), True, True, [blk.name, sm.name], [bk.name])
                b.cp("dve", smr(R_CUM), bk.ap[:, 16:32], [bk.name], smk)
                b.ts("dve", smr(R_NCUM), smr(R_CUM), -1.0, None, ALU.mult, None, smk, smk)
                b.tt("dve", smr(R_DEND), bk.ap[:, 32:48], smr(R_CUM), ALU.subtract, [bk.name] + smk, smk)
                b.act(smr(R_DEND), smr(R_DEND), AF.Exp, smk, smk)
                b.act(smr(R_ECUM), smr(R_CUM), AF.Exp, smk, smk)
                b.act(smr(R_CD), bk.ap[:, 32:48], AF.Exp, [bk.name], smk)
                b.tt("dve", smr(R_DTD), smr(R_DT), smr(R_DEND), ALU.mult, smk, smk)
                xv = XTM.ap.rearrange("p (h q) -> p h q", h=16)
                b.tt("dve", XDT.ap.rearrange("p (h q) -> p h q", h=16), xv, smr(R_DT).unsqueeze(2).to_broadcast([128, 16, 64]),
                     ALU.mult, [XTM.name] + smk, [XDT.name])
                b.tt("dve", XDE.ap.rearrange("p (h q) -> p h q", h=16), xv, smr(R_DTD).unsqueeze(2).to_broadcast([128, 16, 64]),
                     ALU.mult, [XTM.name] + smk, [XDE.name])
                for g in range(4):
                    bk1 = bank[1]
                    b.tt("dve", Rt.ap, sm.ap[:, R_DA, 4 * g:4 * g + 4].unsqueeze(2).to_broadcast([128, 4, 128]),
                         tri.ap.unsqueeze(1).to_broadcast([128, 4, 128]), ALU.mult, smk + [tri.name], [Rt.name])
                    b.mm(bk1.ap, ones.ap, Rt.ap.rearrange("p e i -> p (e i)"), True, False,
                         [ones.name, Rt.name], [bk1.name])
                    b.mm(bk1.ap, identb.ap, neg4.ap.rearrange("p e i -> p (e i)"), False, True, [identb.name, neg4.name], [bk1.name])
                    for e_ in range(4):
                        h = 4 * g + e_
                        b.act(Lt.ap[:, e_, :], bk1.ap[:, e_ * 128:(e_ + 1) * 128], AF.Exp, [bk1.name] + smk, [Lt.name],
                              bias=sm.ap[:, R_NCUM, h:h + 1])
                    bk2 = bank[2]
                    b.mm(bk2.ap[:, 0:128], XC.ap[:, 8 + g, cs], XC.ap[:, 12 + g, cs], True, True, [XC.name], [bk2.name])
                    b.tt("dve", MT.ap[:, 4 * g:4 * g + 4, :], Lt.ap, bk2.ap[:, 0:128].unsqueeze(1).to_broadcast([128, 4, 128]),
                         ALU.mult, [Lt.name, bk2.name], [MT.name])
                for h in range(16):
                    bk = bank[3 + h // 8]
                    b.mm(bk.ap[:, (h % 8) * 64:(h % 8 + 1) * 64], MT.ap[:, h, :], XDT.ap[:, h * 64:(h + 1) * 64], True, True,
                         [MT.name, XDT.name], [bk.name])
                obk = [bank[5], bank[1]]
                if not sample:
                    for g in range(4):
                        ob = obk[g // 2]
                        b.mm(ob.ap[:, (g % 2) * 256:(g % 2 + 1) * 256], XC.ap[:, 12 + g, cs], STb.ap[:, g * 256:(g + 1) * 256],
                             True, True, [XC.name, STb.name], [ob.name])
                else:
                    b.cp("dve", T2.ap.rearrange("p (h q) -> p h q", h=16), smr(R_CD).unsqueeze(2).to_broadcast([128, 16, 64]),
                         smk, [T2.name])
                    for hf in range(2):
                        bkc = bank[3 + hf]
                    for k in range(8):
                        bkc = bank[0] if k % 2 == 0 else bank[2]
                        b.tr(bkc.ap[:, 0:128], T2.ap[:, k * 128:(k + 1) * 128], identf.ap, [T2.name, identf.name], [bkc.name])
                        b.cp("act", CDT.ap[:, k, :], bkc.ap[:, 0:128:8], [bkc.name], [CDT.name])
                    for s_ in range(16):
                        b.dma("sp", H0.ap, sa_ssm[s_].rearrange("(k p) n -> p k n", p=128), [], [H0.name])
                        for k in range(8):
                            bkc = bank[0] if k % 2 == 0 else bank[2]
                            b.tr(bkc.ap[:, 0:128], H0.ap[:, k, :], identf.ap, [H0.name, identf.name], [bkc.name])
                            b.cp("act" if k % 2 else "dve", STs.ap[:, k * 128:(k + 1) * 128], bkc.ap[:, 0:128], [bkc.name], [STs.name])
                        b.tt("dve", CTm.ap, XC.ap[:, 12:16, cs], SEL.ap[:, s_, :].unsqueeze(1).to_broadcast([128, 4, 128]), ALU.mult,
                             [XC.name, SEL.name], [CTm.name])
                        for g in range(4):
                            ob = obk[g // 2]
                            b.mm(ob.ap[:, (g % 2) * 256:(g % 2 + 1) * 256], CTm.ap[:, g, :], STs.ap[:, g * 256:(g + 1) * 256],
                                 s_ == 0 and g % 2 == 0, s_ == 15, [CTm.name, STs.name], [ob.name])
                        b.ts("dve", Bm.ap, BTM.ap, seqmask.ap[:, s_:s_ + 1], None, ALU.mult, None, [BTM.name, seqmask.name], [Bm.name])
                        for k in range(8):
                            bkc = bank[0] if k % 2 == 0 else bank[2]
                            g = k // 2
                            b.mm(bkc.ap[:, 128:256], XDE.ap[:, k * 128:(k + 1) * 128], Bm.ap[:, g * 128:(g + 1) * 128], True, True,
                                 [XDE.name, Bm.name], [bkc.name])
                            b.stt(NS.ap[:, k, :], H0.ap[:, k, :], CDT.ap[:, k, s_:s_ + 1], bkc.ap[:, 128:256], ALU.mult, ALU.add,
                                  [H0.name, CDT.name, bkc.name], [NS.name])
                        b.dma("sp", s_a_ssm[s_].rearrange("(k p) n -> p k n", p=128), NS.ap, [NS.name], ["s_a_ssm"], par=True)
                ecb = smr(R_ECUM)
                for hf in range(2):
                    cols = slice(hf * 512, (hf + 1) * 512)
                    v3 = lambda ap_: ap_.rearrange("p (h q) -> p h q", h=8)
                    b.tt("dve", v3(T1.ap[:, cols]), v3(obk[hf].ap), ecb[:, hf * 8:(hf + 1) * 8].unsqueeze(2).to_broadcast([128, 8, 64]),
                         ALU.mult, [obk[hf].name] + smk, [T1.name])
                    b.tt("dve", T1.ap[:, cols], T1.ap[:, cols], bank[3 + hf].ap, ALU.add, [T1.name, bank[3 + hf].name], [T1.name])
                    b.tt("pool", v3(T2.ap[:, cols]), v3(XTM.ap[:, cols]), dbc.ap[:, hf * 8:(hf + 1) * 8].unsqueeze(2).to_broadcast([128, 8, 64]),
                         ALU.mult, [XTM.name, dbc.name], [T2.name])
                    b.tt("dve", T1.ap[:, cols], T1.ap[:, cols], T2.ap[:, cols], ALU.add, [T1.name, T2.name], [T1.name])
                    b.tt("dve", T1.ap[:, cols], T1.ap[:, cols], ZS.ap[:, cols], ALU.mult, [T1.name, ZS.name], [T1.name])
                for g in range(4):
                    b.act(T2.ap[:, g * 256:(g + 1) * 256], T1.ap[:, g * 256:(g + 1) * 256], AF.Square, [T1.name], [T2.name, sm.name],
                          accum=sm.ap[:, R_MS, g:g + 1])
                b.act(smr(R_RS, 4), smr(R_MS, 4), AF.Sqrt, smk, smk, bias=EPS, scale=1.0 / 256)
                b.recip(smr(R_RS, 4), smr(R_RS, 4), smk, smk)
                b.tt("dve", T1.ap.rearrange("p (g q) -> p g q", g=4), T1.ap.rearrange("p (g q) -> p g q", g=4),
                     smr(R_RS, 4).unsqueeze(2).to_broadcast([128, 4, 256]), ALU.mult, [T1.name] + smk, [T1.name])
                b.tt("dve", YAt.ap, T1.ap, anorm.ap, ALU.mult, [T1.name, anorm.name], [YAt.name])
                tb = tbank[0]
                for k in range(8):
                    b.tr(tb.ap[:, k * 128:(k + 1) * 128], YAt.ap[:, k * 128:(k + 1) * 128], identb.ap, [YAt.name, identb.name], [tb.name])
                b.cp("act", YA.ap[:, :, tok], tb.ap.rearrange("p (k t) -> p k t", k=8), [tb.name], [YA.name])
                if not sample:
                    for g in range(4):
                        bk = bank[3 + g // 2]
                        b.mm(bk.ap[:, (g % 2) * 256:(g % 2 + 1) * 256], BTM.ap[:, g * 128:(g + 1) * 128], XDE.ap[:, g * 256:(g + 1) * 256],
                             True, True, [BTM.name, XDE.name], [bk.name])
                    b.tt("dve", STf.ap.rearrange("p (h q) -> p h q", h=16), STf.ap.rearrange("p (h q) -> p h q", h=16),
                         smr(R_CD).unsqueeze(2).to_broadcast([128, 16, 64]), ALU.mult, [STf.name] + smk, [STf.name])
                    for hf in range(2):
                        cols = slice(hf * 512, (hf + 1) * 512)
                        b.tt("dve", STf.ap[:, cols], STf.ap[:, cols], bank[3 + hf].ap, ALU.add, [STf.name, bank[3 + hf].name], [STf.name])
                    b.cp("act", STb.ap, STf.ap, [STf.name], [STb.name])
        for k in range(8):
            bkc = bank[0] if k % 2 == 0 else bank[2]
            b.tr(bkc.ap[:, 0:128], STf.ap[:, k * 128:(k + 1) * 128], identf.ap, [STf.name, identf.name], [bkc.name])
            b.cp("act", NS.ap[:, k, :], bkc.ap[:, 0:128], [bkc.name], [NS.name])
        b.dma("sp", p_a_ssm.rearrange("(k p) n -> p k n", p=128), NS.ap, [NS.name], ["p_a_ssm"])
        if debug:
            b.debug_dump("YA", YA, [128, 8, NTOK], BF16)
        b.release(mA)
    if stop_after == "A":
        P.finalize(final_keys=b.out_keys)
        return b


    BASE = b.top
    XOFF = B.ARENA - NT * 1024 * 4

    def alloc_at(name, free_shape, dt, off):
        keep = b.top
        b.top = off
        t_ = b.alloc(name, free_shape, dt)
        b.top = keep
        return t_

    X = alloc_at("X", [NT, 1024], F32, XOFF)
    TOKG = [(0, 512), (512, 512), (1024, 512), (1536, 512), (2048, 128)]
    mC = b.mark()
    WO = b.alloc("WO", [16, 1024], BF16)
    XT0 = b.alloc("XT0c", [1024], F32)
    assert b.top <= XOFF
    for k in range(16):
        b.dma("pool", WO.ap[:, k, :], w_out[k * 128:(k + 1) * 128, :], [], [WO.name], par=True)
    for t in range(NT):
        tok = slice(t * 128, (t + 1) * 128)
        b.dma("sp", XT0.ap, x_src(t), [], [XT0.name])
        for hf in range(2):
            bk = bank[2 + (2 * t + hf) % 4]
            for k in range(16):
                src = YA if k < 8 else YB
                b.mm(bk.ap, src.ap[:, k % 8, tok], WO.ap[:, k, hf * 512:(hf + 1) * 512], k == 0, k == 15,
                     [src.name, WO.name], [bk.name])
            b.tt("dve", X.ap[:, t, hf * 512:(hf + 1) * 512], bk.ap, XT0.ap[:, hf * 512:(hf + 1) * 512], ALU.add,
                 [bk.name, XT0.name], [X.name])
    b.release(mC)
    if debug and stop_after == "C":
        b.debug_dump("X", X, [128, NT, 1024])
    if stop_after == "C":
        P.finalize(final_keys=b.out_keys)
        return b
    LOW = BASE - 2 * 8 * NTOK * 2

    def norm_all(wcol, XN):
        scr = norm_scratch()
        for t in range(NT):
            norm_tile(X.ap[:, t, :], [X.name], wcol, XN.ap[:, :, t * 128:(t + 1) * 128], XN.name, scr)

    def ffn(layer, wcol):
        b.top = LOW
        XN = b.alloc("XNf", [8, NTOK], BF16)
        HT = [b.alloc("HT0", [4, NTOK], BF16), b.alloc("HT1", [4, NTOK], BF16)]
        W1 = [b.alloc("W1a", [8, 512], BF16), b.alloc("W1b", [8, 512], BF16)]
        W2 = [b.alloc("W2a", [4, 1024], BF16), b.alloc("W2b", [4, 1024], BF16)]
        R1 = [b.alloc("R1a", [512], F32), b.alloc("R1b", [512], F32)]
        norm_all(wcol, XN)
        assert b.top <= XOFF
        ri_ = 0
        for fg in range(8):
            w1, w2, ht = W1[fg % 2], W2[fg % 2], HT[fg % 2]
            for k in range(8):
                b.dma("pool", w1.ap[:, k, :], w_ff1[layer, k * 128:(k + 1) * 128, fg * 512:(fg + 1) * 512], [], [w1.name], par=True)
            for f in range(4):
                b.dma("pool", w2.ap[:, f, :], w_ff2[layer, fg * 512 + f * 128:fg * 512 + (f + 1) * 128, :], [], [w2.name], par=True)
            for f in range(4):
                for (c0, n) in TOKG:
                    bk = bank[ri_ % 2]
                    r1 = R1[ri_ % 2]
                    ri_ += 1
                    for k in range(8):
                        b.mm(bk.ap[:, 0:n], w1.ap[:, k, f * 128:(f + 1) * 128], XN.ap[:, k, c0:c0 + n], k == 0, k == 7,
                             [w1.name, XN.name], [bk.name])
                    b.act(r1.ap[:, 0:n], bk.ap[:, 0:n], AF.Relu, [bk.name], [r1.name])
                    b.tt("pool", ht.ap[:, f, c0:c0 + n], r1.ap[:, 0:n], r1.ap[:, 0:n], ALU.mult, [r1.name], [ht.name])
            for t in range(NT):
                tok = slice(t * 128, (t + 1) * 128)
                for hf in range(2):
                    bk = bank[2 + (2 * t + hf) % 4]
                    for f in range(4):
                        b.mm(bk.ap, ht.ap[:, f, tok], w2.ap[:, f, hf * 512:(hf + 1) * 512], f == 0, f == 3,
                             [ht.name, w2.name], [bk.name])
                    xs_ = X.ap[:, t, hf * 512:(hf + 1) * 512]
                    b.tt("dve", xs_, xs_, bk.ap, ALU.add, [X.name, bk.name], [X.name])

    if "F0" not in skip:
        ffn(0, nff0)
    if debug and stop_after == "F0":
        b.debug_dump("X", X, [128, NT, 1024])
    if stop_after == "F0":
        P.finalize(final_keys=b.out_keys)
        return b

    b.top = LOW
    HP = 30 + 2048
    H = b.alloc("H", [8, HP + 608], BF16)
    HL = b.alloc("HL", [8, 256], F32)
    bpw1 = load_cols("bpw1", c_b_pw1, 16)
    bdw = load_cols("bdw", c_b_dw, 8)
    lnw = load_cols("lnw", c_ln_w, 8)
    lnb = load_cols("lnb", c_ln_b, 8)
    bpw2 = load_bcast("bpw2", c_b_pw2, 1024)
    wdw = b.alloc("wdw", [8, 31], F32)
    for w_ in range(31):
        b.dma("sp" if w_ % 2 else "act", wdw.ap[:, :, w_], c_w_dw[w_:w_ + 1, :].rearrange("o (c p) -> p (o c)", p=128),
              [], [wdw.name], **NCD)
    mX = b.mark()
    XN = b.alloc("XNc", [8, NTOK], BF16)
    WP = [b.alloc("WPa", [8, 256], BF16), b.alloc("WPb", [8, 256], BF16)]
    SGt = b.alloc("SGt", [512], F32)
    HS = b.alloc("HS", [1024], F32)
    norm_all(nmix1, XN)
    assert b.top <= XOFF
    b.memset("pool", H.ap[:, :, 0:30], 0.0, [H.name])
    for q4 in range(4):
        b.dma("sp", HS.ap[0:120, :], sc_conv[4 * q4:4 * q4 + 4].rearrange("s w c -> (s w) c"), [], [HS.name])
        for c in range(8):
            bk = bank[c % 2]
            b.tr(bk.ap[:, 0:120], HS.ap[0:120, c * 128:(c + 1) * 128], identf.ap[0:120, 0:120], [HS.name, identf.name], [bk.name])
            dst = H.ap[:, c, HP:HP + 480].rearrange("p (w s) -> p s w", s=16)[:, 4 * q4:4 * q4 + 4, :]
            b.cp("act", dst, bk.ap[:, 0:120].rearrange("p (s w) -> p s w", s=4), [bk.name], [H.name])
    for c in range(8):
        wp = WP[c % 2]
        for k in range(8):
            b.dma("pool", wp.ap[:, k, 0:128], c_w_pw1[k * 128:(k + 1) * 128, c * 128:(c + 1) * 128], [], [wp.name], par=True)
            b.dma("pool", wp.ap[:, k, 128:256], c_w_pw1[k * 128:(k + 1) * 128, 1024 + c * 128:1024 + (c + 1) * 128], [], [wp.name], par=True)
        for (c0, n) in TOKG:
            bv, bg = bank[0], bank[1]
            for k in range(8):
                b.mm(bv.ap[:, 0:n], wp.ap[:, k, 0:128], XN.ap[:, k, c0:c0 + n], k == 0, k == 7, [wp.name, XN.name], [bv.name])
            for k in range(8):
                b.mm(bg.ap[:, 0:n], wp.ap[:, k, 128:256], XN.ap[:, k, c0:c0 + n], k == 0, k == 7, [wp.name, XN.name], [bg.name])
            b.act(SGt.ap[:, 0:n], bg.ap[:, 0:n], AF.Sigmoid, [bg.name, bpw1.name], [SGt.name], bias=bpw1.ap[:, 8 + c:9 + c])
            if c0 < 2048:
                b.stt(H.ap[:, c, 30 + c0:30 + c0 + n], bv.ap[:, 0:n], bpw1.ap[:, c:c + 1], SGt.ap[:, 0:n], ALU.add, ALU.mult,
                      [bv.name, bpw1.name, SGt.name], [H.name])
                if c0 == 1536:
                    b.stt(HL.ap[:, c, 0:128], bv.ap[:, 384:512], bpw1.ap[:, c:c + 1], SGt.ap[:, 384:512], ALU.add, ALU.mult,
                          [bv.name, bpw1.name, SGt.name], [HL.name])
            else:
                dst = H.ap[:, c, HP + 480:HP + 608].rearrange("p (l s) -> p s l", s=16)
                b.stt(dst, bv.ap[:, 0:128].rearrange("p (s l) -> p s l", s=16), bpw1.ap[:, c:c + 1],
                      SGt.ap[:, 0:128].rearrange("p (s l) -> p s l", s=16), ALU.add, ALU.mult,
                      [bv.name, bpw1.name, SGt.name], [H.name])
                b.stt(HL.ap[:, c, 128:256], bv.ap[:, 0:128], bpw1.ap[:, c:c + 1], SGt.ap[:, 0:128], ALU.add, ALU.mult,
                      [bv.name, bpw1.name, SGt.name], [HL.name])
    b.release(mX)
    OUTS = b.alloc("OUTS", [1024], F32)
    for c in range(8):
        bk = bank[2 + c // 4]
        b.tr(bk.ap[0:30, (c % 4) * 128:(c % 4 + 1) * 128], HL.ap[:, c, 98:128], identf.ap, [HL.name, identf.name], [bk.name])
        if c % 4 == 3:
            b.cp("act", OUTS.ap[0:30, (c // 4) * 512:(c // 4 + 1) * 512], bk.ap[0:30, :], [bk.name], [OUTS.name])
    b.dma("sp", p_c_conv, OUTS.ap[0:30, :], [OUTS.name], ["p_c_conv"])
    for c in range(8):
        bk = bank[4 + c // 4]
        b.tr(bk.ap[:, (c % 4) * 128:(c % 4 + 1) * 128], HL.ap[:, c, 128:256], identf.ap, [HL.name, identf.name], [bk.name])
        if c % 4 == 3:
            b.cp("act", OUTS.ap[:, (c // 4) * 512:(c // 4 + 1) * 512], bk.ap, [bk.name], [OUTS.name])
    for s_ in range(16):
        b.dma("sp" if s_ % 2 else "act", s_c_conv[s_, 22:30, :], OUTS.ap[8 * s_:8 * s_ + 8, :], [OUTS.name], ["s_c_conv"], par=True)
    b.dma("sp", s_c_conv[:, 0:22, :], sc_conv[:, 8:30, :], [], ["s_c_conv"], par=True)
    WP2 = b.alloc("WP2", [8, 1024], BF16)
    for k in range(8):
        b.dma("pool", WP2.ap[:, k, :], c_w_pw2[k * 128:(k + 1) * 128, :], [], [WP2.name], par=True)
    DM = [b.alloc("DMa", [31, 128], BF16), b.alloc("DMb", [31, 128], BF16)]
    CV = b.alloc("CV", [8, 512], F32)
    SQ = [b.alloc("SQa", [512], F32), b.alloc("SQb", [512], F32)]
    MEAN = b.alloc("MEAN", [512], F32)
    RSTD = b.alloc("RSTD", [512], F32)
    M2 = b.alloc("M2", [512], F32)
    TMP = b.alloc("TMPc", [512], F32)
    AV = b.alloc("AV", [8, 512], BF16)
    assert b.top <= XOFF, b.top
    for t in range(NT):
        b.tt("pool", X.ap[:, t, :], X.ap[:, t, :], bpw2.ap, ALU.add, [X.name, bpw2.name], [X.name])
    di = 0
    for (c0, n) in TOKG:
        sample_g = (c0 == 2048)
        s1, s2 = bank[2], bank[3]
        for c in range(8):
            dm = DM[di % 2]
            di += 1
            for w_ in range(31):
                b.ts("pool", dm.ap[:, w_, :], identb.ap, wdw.ap[:, c, w_:w_ + 1], 1.0, ALU.mult, ALU.mult,
                     [identb.name, wdw.name], [dm.name])
            bk = bank[c % 2]
            for w_ in range(31):
                if sample_g:
                    rhs = H.ap[:, c, HP + 16 * w_:HP + 16 * w_ + 128]
                else:
                    rhs = H.ap[:, c, c0 + w_:c0 + w_ + n]
                b.mm(bk.ap[:, 0:n], dm.ap[:, w_, :], rhs, w_ == 0, w_ == 30, [dm.name, H.name], [bk.name])
            b.act(CV.ap[:, c, 0:n], bk.ap[:, 0:n], AF.Identity, [bk.name, bdw.name], [CV.name], bias=bdw.ap[:, c:c + 1])
            sq = SQ[c % 2]
            b.act(sq.ap[:, 0:n], CV.ap[:, c, 0:n], AF.Square, [CV.name], [sq.name])
            b.mm(s1.ap[:, 0:n], ones.ap, CV.ap[:, c, 0:n], c == 0, c == 7, [ones.name, CV.name], [s1.name])
            b.mm(s2.ap[:, 0:n], ones.ap, sq.ap[:, 0:n], c == 0, c == 7, [ones.name, sq.name], [s2.name])
        b.act(MEAN.ap[:, 0:n], s1.ap[:, 0:n], AF.Copy, [s1.name], [MEAN.name], scale=1.0 / 1024)
        b.tt("dve", M2.ap[:, 0:n], MEAN.ap[:, 0:n], MEAN.ap[:, 0:n], ALU.mult, [MEAN.name], [M2.name])
        b.stt(RSTD.ap[:, 0:n], s2.ap[:, 0:n], 1.0 / 1024, M2.ap[:, 0:n], ALU.mult, ALU.subtract, [s2.name, M2.name], [RSTD.name])
        b.act(RSTD.ap[:, 0:n], RSTD.ap[:, 0:n], AF.Sqrt, [RSTD.name], [RSTD.name], bias=EPS)
        b.recip(RSTD.ap[:, 0:n], RSTD.ap[:, 0:n], [RSTD.name], [RSTD.name])
        for c in range(8):
            b.tt("dve", TMP.ap[:, 0:n], CV.ap[:, c, 0:n], MEAN.ap[:, 0:n], ALU.subtract, [CV.name, MEAN.name], [TMP.name])
            b.tt("dve", TMP.ap[:, 0:n], TMP.ap[:, 0:n], RSTD.ap[:, 0:n], ALU.mult, [TMP.name, RSTD.name], [TMP.name])
            if sample_g:
                dst = AV.ap[:, c, 0:128].rearrange("p (s l) -> p s l", s=16)
                src = TMP.ap[:, 0:128].rearrange("p (l s) -> p s l", s=16)
            else:
                dst, src = AV.ap[:, c, 0:n], TMP.ap[:, 0:n]
            b.act(dst, src, AF.Silu, [TMP.name, lnw.name, lnb.name], [AV.name], bias=lnb.ap[:, c:c + 1], scale=lnw.ap[:, c:c + 1])
        for tt_ in range(n // 128):
            t = c0 // 128 + tt_
            cs = slice(tt_ * 128, (tt_ + 1) * 128)
            for hf in range(2):
                bk = bank[4 + hf]
                for c in range(8):
                    b.mm(bk.ap, AV.ap[:, c, cs], WP2.ap[:, c, hf * 512:(hf + 1) * 512], c == 0, c == 7, [AV.name, WP2.name], [bk.name])
                xs_ = X.ap[:, t, hf * 512:(hf + 1) * 512]
                b.tt("dve", xs_, xs_, bk.ap, ALU.add, [X.name, bk.name], [X.name])
    if debug and stop_after == "L1":
        b.debug_dump("X", X, [128, NT, 1024])
    if stop_after == "L1":
        P.finalize(final_keys=b.out_keys)
        return b

    if "F1" not in skip:
        ffn(1, nff1)

    b.top = LOW
    nfin = load_bcast("nfin", norm_final, 1024)
    junk = b.alloc("junkf", [1024], BF16)
    ssf = b.alloc("ssf", [4], F32)
    YO = [b.alloc("YOa", [1024], F32), b.alloc("YOb", [1024], F32)]
    for t in range(NT):
        yo = YO[t % 2]
        b.act(junk.ap, X.ap[:, t, :], AF.Square, [X.name], [junk.name, ssf.name], accum=ssf.ap[:, 0:1])
        b.act(ssf.ap[:, 1:2], ssf.ap[:, 0:1], AF.Sqrt, [ssf.name], [ssf.name], bias=EPS, scale=1.0 / D)
        b.recip(ssf.ap[:, 2:3], ssf.ap[:, 1:2], [ssf.name], [ssf.name])
        b.stt(yo.ap, X.ap[:, t, :], ssf.ap[:, 2:3], nfin.ap, ALU.mult, ALU.mult, [X.name, ssf.name, nfin.name], [yo.name])
        if t < 16:
            b.dma("sp", y_p[t * 128:(t + 1) * 128, :], yo.ap, [yo.name], ["y_p"], par=True)
        else:
            b.dma("sp", y_s, yo.ap, [yo.name], ["y_s"])
    P.finalize(final_keys=b.out_keys)
    return b


def make_in_maps(inp):
    f = lambda a: np.ascontiguousarray(np.asarray(a, dtype=np.float32))
    shared = {
        "norm_mix": f(inp["norm_mix"]), "norm_ff": f(inp["norm_ff"]), "norm_final": f(inp["norm_final"]).reshape(1, D),
        "w_in": f(inp["w_in_ab"][0]), "a_conv_w": f(inp["a_conv_w"][0]), "a_conv_b": f(inp["a_conv_b"]),
        "a_dt_bias": f(inp["a_dt_bias"]), "a_log": f(inp["a_log"]), "a_d": f(inp["a_d"]), "a_norm": f(inp["a_norm"]),
        "lam_re": f(inp["s5_lam_re"][0]), "lam_im": f(inp["s5_lam_im"][0]), "log_step": f(inp["s5_log_step"]),
        "s5_b_re": f(inp["s5_b_re"][0]), "s5_b_im": f(inp["s5_b_im"][0]), "s5_c_re": f(inp["s5_c_re"][0]),
        "s5_c_im": f(inp["s5_c_im"][0]), "s5_d": f(inp["s5_d"]).reshape(1, 1024), "w_glu": f(inp["s5_w_glu"][0]),
        "b_glu": f(inp["s5_b_glu"]), "w_out": f(inp["w_out_ab"][0]), "c_w_pw1": f(inp["c_w_pw1"][0]),
        "c_b_pw1": f(inp["c_b_pw1"]), "c_w_dw": f(inp["c_w_dw"][0]), "c_b_dw": f(inp["c_b_dw"]),
        "c_ln_w": f(inp["c_ln_w"]), "c_ln_b": f(inp["c_ln_b"]), "c_w_pw2": f(inp["c_w_pw2"][0]),
        "c_b_pw2": f(inp["c_b_pw2"]), "w_ff1": f(inp["w_ff1"]), "w_ff2": f(inp["w_ff2"]),
    }
    maps = []
    for c in range(8):
        s = slice(16 * c, 16 * c + 16)
        m = dict(shared)
        m["xp"] = f(inp["x_prompt"][c])
        m["xs"] = f(inp["x_sample"][s]).reshape(128, D)
        m["sa_conv"] = f(inp["state_a_conv"][0, s]).reshape(48, 2048)
        m["sa_ssm"] = f(inp["state_a_ssm"][0, s]).reshape(16, 1024, 128)
        m["sb_re"] = f(inp["state_b_re"][0, s]).reshape(16, 4096)
        m["sb_im"] = f(inp["state_b_im"][0, s]).reshape(16, 4096)
        m["sc_conv"] = f(inp["state_c_conv"][0, s])
        maps.append(m)
    return maps


_CACHE = {}


def kernel(**inputs):
    if "nc" not in _CACHE:
        _CACHE["nc"] = build().nc
    nc = _CACHE["nc"]
    maps = make_in_maps(inputs)
    res = run_bass_kernel_spmd(nc, maps, core_ids=list(range(8)))
    R = res.results
    cat = lambda k: np.stack([np.asarray(r[k]) for r in R], axis=0)
    y_prompt = cat("y_p").reshape(8, 2048, D)
    y_sample = cat("y_s").reshape(128, 8, D)
    p_a_conv = cat("p_a_conv").reshape(1, 8, 3, 2048)
    p_a_ssm = cat("p_a_ssm").reshape(1, 8, 16, 64, 128)
    p_b_re = cat("p_b_re").reshape(1, 8, 64, 64)
    p_b_im = cat("p_b_im").reshape(1, 8, 64, 64)
    p_c_conv = cat("p_c_conv").reshape(1, 8, 30, D)
    s_a_conv = cat("s_a_conv").reshape(1, 128, 3, 2048)
    s_a_ssm = cat("s_a_ssm").reshape(1, 128, 16, 64, 128)
    s_b_re = cat("s_b_re").reshape(1, 128, 64, 64)
    s_b_im = cat("s_b_im").reshape(1, 128, 64, 64)
    s_c_conv = cat("s_c_conv").reshape(1, 128, 30, D)
    return tuple(np.ascontiguousarray(a, dtype=np.float32) for a in
                 (y_prompt, y_sample, p_a_conv, p_a_ssm, p_b_re, p_b_im, p_c_conv,
                  s_a_conv, s_a_ssm, s_b_re, s_b_im, s_c_conv))
```

```python
import math
import numpy as np
import concourse.bass as bass
import concourse.mybir as mybir
from concourse.bass_utils import run_bass_kernel_spmd

F32 = mybir.dt.float32
BF16 = mybir.dt.bfloat16
I32 = mybir.dt.int32
U8 = mybir.dt.uint8
AF = mybir.ActivationFunctionType
ALU = mybir.AluOpType
DTSIZE = {F32: 4, BF16: 2, I32: 4, U8: 1}

ENGS = ("pe", "act", "dve", "pool", "sp")
ROLL = 3000
N_DMA_SEMS = 24
PAR_DMA = True
SCHEDULE = True
DEBUG_TAGS = False

D = 1024
NT = 17
NTOK = NT * 128
A_PROJ = 3088
IN_COLS = 4112
EPS = 1e-6
TWO_PI = 2.0 * math.pi


class _Op:
    __slots__ = ("eng", "emit", "deps", "odeps", "is_dma", "signal", "sig", "waits", "idx", "dma_prev", "cost",
                 "t0", "t1", "tag", "crit")

    def __init__(self, eng, emit, is_dma):
        self.eng = eng
        self.emit = emit
        self.is_dma = is_dma
        self.deps = []
        self.signal = False
        self.sig = None
        self.waits = []
        self.dma_prev = None
        self.odeps = []
        self.cost = 0.3
        self.t0 = 0.0
        self.t1 = 0.0
        self.tag = 0
        self.crit = None


class Prog:
    def __init__(self, nc):
        self.nc = nc
        self.ops = []
        self.last_w = {}
        self.readers = {}
        self._sem_cms = []
        self.alias = {}
        self.batch_deps = {}

    def _writers(self, k):
        w = self.last_w.get(k)
        if w is None:
            return ()
        return w if isinstance(w, list) else (w,)

    def add(self, eng, emit, reads=(), writes=(), dma=False, par=False, cost=None):
        par = par and PAR_DMA and eng != "pool"
        op = _Op(eng, emit, dma)
        if cost is not None:
            op.cost = cost
        if DEBUG_TAGS:
            import sys as _sys
            f = _sys._getframe(1)
            while f.f_code.co_name in ("add", "mm", "tr", "act", "ts", "tt", "stt", "cp", "memset", "scan", "dma", "recip", "<lambda>"):
                f = f.f_back
            op.tag = f.f_lineno
        op.idx = len(self.ops)
        deps = set()
        for k in list(reads) + list(writes):
            for a in self.alias.get(k, ()):
                deps.update(self._writers(a))
                for r in self.readers.get(a, ()):
                    deps.add(r)
        for k in reads:
            deps.update(self._writers(k))
            if k.startswith(("bank", "tbank")):
                for r in self.readers.get(k, ()):
                    if r.eng != eng:
                        deps.add(r)
        for k in writes:
            prev = self.last_w.get(k)
            same_batch = par and isinstance(prev, list) and not self.readers.get(k)
            if not same_batch:
                base = set(self._writers(k))
                base.update(self.readers.get(k, ()))
                if par:
                    self.batch_deps[k] = base
                deps.update(base)
            else:
                deps.update(self.batch_deps.get(k, ()))
        for k in reads:
            self.readers.setdefault(k, []).append(op)
        for k in writes:
            prev = self.last_w.get(k)
            if par and isinstance(prev, list) and not self.readers.get(k):
                prev.append(op)
            else:
                self.last_w[k] = [op] if par else op
            self.readers[k] = []
        deps.discard(op)
        op.odeps = list(deps)
        op.deps = [d for d in deps if not (d.eng == "pe" and eng == "pe" and not d.is_dma and not dma)]
        if dma:
            op.signal = True
        for d in op.deps:
            d.signal = True
        self.ops.append(op)
        return op

    def _schedule(self):
        LAT_X, LAT_S, WIN = 0.9, 0.08, 48
        ops = self.ops
        n = len(ops)
        done = [False] * n
        queues = {e: [] for e in ENGS}
        for op in ops:
            queues[op.eng].append(op)
        head = {e: 0 for e in ENGS}
        free = {e: 0.0 for e in ENGS}
        last_on = {e: None for e in ENGS}
        out = []
        remaining = n
        while remaining:
            best = None
            for e in ENGS:
                q = queues[e]
                h = head[e]
                while h < len(q) and done[q[h].idx]:
                    h += 1
                head[e] = h
                cnt = 0
                i = h
                while i < len(q) and cnt < WIN:
                    op = q[i]
                    i += 1
                    if done[op.idx]:
                        continue
                    cnt += 1
                    ok = True
                    rdy = free[e]
                    cr = last_on[e]
                    for d in op.odeps:
                        if not done[d.idx]:
                            ok = False
                            break
                        lat = LAT_S if (d.eng == e and not d.is_dma) else LAT_X
                        if d.eng == "pe" and e == "pe" and not d.is_dma:
                            lat = 0.0
                        if d.t1 + lat > rdy:
                            rdy = d.t1 + lat
                            cr = d
                    if not ok:
                        continue
                    key = (rdy, op.idx)
                    if best is None or key < best[0]:
                        best = (key, op, rdy, cr)
                    if rdy <= free[e] + 1e-9:
                        break
            assert best is not None, "scheduler deadlock"
            _, op, rdy, cr = best
            op.crit = cr
            last_on[op.eng] = op
            op.t0 = rdy
            if op.is_dma:
                op.t1 = rdy + op.cost
                free[op.eng] = rdy + 0.06
            else:
                op.t1 = rdy + op.cost
                free[op.eng] = op.t1
            done[op.idx] = True
            out.append(op)
            remaining -= 1
        out.sort(key=lambda o: (o.t0, o.idx))
        self.sim_span = max(o.t1 for o in out)
        return out

    def _new_sem(self, name):
        cm = self.nc.semaphore(name)
        s = cm.__enter__()
        self._sem_cms.append(cm)
        return s

    def finalize(self, final_keys=()):
        nc = self.nc
        fin_deps = set()
        for k in final_keys:
            fin_deps.update(self._writers(k))
        for d in fin_deps:
            d.signal = True
        if SCHEDULE:
            self.ops = self._schedule()
        eng_sem, eng_cnt = {}, {}
        dma_sems = [self._new_sem(f"dq{i}") for i in range(N_DMA_SEMS)]
        dma_cnt = [0] * N_DMA_SEMS
        dma_last = [None] * N_DMA_SEMS
        nd = 0
        for op in self.ops:
            if not op.signal:
                continue
            if op.is_dma:
                i = nd % N_DMA_SEMS
                nd += 1
                op.dma_prev = dma_last[i]
                dma_cnt[i] += 16
                op.sig = (dma_sems[i], dma_cnt[i])
                dma_last[i] = op
            else:
                e = op.eng
                if e not in eng_sem or eng_cnt[e] >= ROLL:
                    eng_sem[e] = self._new_sem(f"s_{e}_{len(self._sem_cms)}")
                    eng_cnt[e] = 0
                eng_cnt[e] += 1
                op.sig = (eng_sem[e], eng_cnt[e])
        waited = {e: {} for e in ENGS}
        per_eng = {e: [] for e in ENGS}
        for op in self.ops:
            need = {}
            deps = list(op.deps)
            if op.is_dma and op.dma_prev is not None:
                deps.append(op.dma_prev)
            for d in deps:
                sem, val = d.sig
                key = id(sem)
                if waited[op.eng].get(key, (None, 0))[1] >= val:
                    continue
                if key not in need or need[key][1] < val:
                    need[key] = (sem, val)
            for key, sv in need.items():
                waited[op.eng][key] = sv
            op.waits = list(need.values())
            per_eng[op.eng].append(op)
        fin_waits = {}
        for d in fin_deps:
            sem, val = d.sig
            if id(sem) not in fin_waits or fin_waits[id(sem)][1] < val:
                fin_waits[id(sem)] = (sem, val)

        def run(engine_obj, lst, final=False):
            for op in lst:
                for sem, val in op.waits:
                    engine_obj.wait_ge(sem, val)
                ins = op.emit(engine_obj)
                if op.sig is not None:
                    ins.then_inc(op.sig[0], 16 if op.is_dma else 1)
            if final:
                for sem, val in fin_waits.values():
                    engine_obj.wait_ge(sem, val)

        with nc.Block() as block:
            @block.sync
            def _(e):
                run(e, per_eng["sp"], final=True)

            @block.tensor
            def _(e):
                run(e, per_eng["pe"])

            @block.scalar
            def _(e):
                run(e, per_eng["act"])

            @block.vector
            def _(e):
                run(e, per_eng["dve"])

            @block.gpsimd
            def _(e):
                run(e, per_eng["pool"])
        for cm in reversed(self._sem_cms):
            cm.__exit__(None, None, None)
        self.stats = {e: len(per_eng[e]) for e in ENGS}
        self.stats["waits"] = sum(len(o.waits) for o in self.ops)


class T:
    def __init__(self, name, ap):
        self.name = name
        self.ap = ap

    def __getitem__(self, k):
        return self.ap[k]


def _prod(s):
    r = 1
    for v in s:
        r *= v
    return r


class B:
    ARENA = 212800

    def __init__(self):
        self.nc = bass.Bass("TRN2", target_bir_lowering=False)
        nc = self.nc
        self.P = Prog(nc)
        self.arena = nc.alloc_sbuf_tensor("arena", [128, self.ARENA], U8)
        self.top = 0
        self.uid = 0
        self.banks = [T(f"bank{i}", nc.alloc_psum_tensor(f"bank{i}", [128, 512], F32)[:, :]) for i in range(6)]
        self.tbanks = [T(f"tbank{i}", nc.alloc_psum_tensor(f"tbank{i}", [128, 1024], BF16)[:, :]) for i in range(2)]
        self.dram = {}
        self.out_keys = []
        self.dbg = []
        self.regions = []

    def alloc(self, name, free_shape, dt):
        free_shape = tuple(free_shape)
        n = _prod(free_shape) * DTSIZE[dt]
        off = (self.top + 63) // 64 * 64
        self.top = off + n
        assert self.top <= self.ARENA, f"SBUF arena overflow at {name}: {self.top}"
        ap = self.arena[:, off:off + n].bitcast(dt)
        if len(free_shape) == 2:
            ap = ap.rearrange("p (a b) -> p a b", a=free_shape[0])
        elif len(free_shape) == 3:
            ap = ap.rearrange("p (a b c) -> p a b c", a=free_shape[0], b=free_shape[1])
        elif len(free_shape) == 4:
            ap = ap.rearrange("p (a b c d) -> p a b c d", a=free_shape[0], b=free_shape[1], c=free_shape[2])
        self.uid += 1
        nm = f"{name}#{self.uid}"
        al = [r[0] for r in self.regions if r[1] < off + n and off < r[2]]
        if al:
            self.P.alias[nm] = list(al)
            for o in al:
                self.P.alias.setdefault(o, []).append(nm)
        self.regions.append((nm, off, off + n))
        return T(nm, ap)

    def mark(self):
        return self.top

    def release(self, m):
        self.top = m

    def din(self, name, shape):
        t = self.nc.dram_tensor(name, list(shape), F32, kind="ExternalInput").ap()
        self.dram[name] = t
        return t

    def dout(self, name, shape):
        t = self.nc.dram_tensor(name, list(shape), F32, kind="ExternalOutput").ap()
        self.dram[name] = t
        self.out_keys.append(name)
        return t

    @staticmethod
    def _n(ap):
        sh = ap.shape
        r = 1
        for v in sh[1:]:
            r *= v
        return r

    def mm(self, out, lhsT, rhs, start, stop, r, w):
        n = self._n(rhs)
        f32 = (rhs.dtype == F32)
        c = max(0.07, n / 2000.0 * (4.0 if f32 else 1.0)) + 0.04
        self.P.add("pe", lambda e: e.matmul(out, lhsT, rhs, start=start, stop=stop), r, w, cost=c)

    def tr(self, out, in_, ident, r, w):
        self.P.add("pe", lambda e: e.transpose(out, in_, ident), r, w, cost=0.15)

    def act(self, out, in_, func, r, w, bias=None, scale=None, accum=None):
        kw = {}
        if bias is not None:
            kw["bias"] = bias
        if scale is not None:
            kw["scale"] = scale
        if accum is not None:
            kw["accum_out"] = accum
        self.P.add("act", lambda e: e.activation(out, in_, func, **kw), r, w,
                   cost=0.22 + 0.00075 * self._n(in_) + (0.1 if accum is not None else 0.0))

    def _ec(self, eng, ap, k=1.0):
        n = self._n(ap)
        if eng == "pool":
            return 0.15 + 0.0026 * n * k
        if eng == "act":
            return 0.22 + 0.00075 * n
        return 0.07 + 0.00115 * n * k

    def ts(self, eng, out, in0, s1, s2, op0, op1, r, w):
        if op1 is None:
            self.P.add(eng, lambda e: e.tensor_scalar(out, in0, s1, None, op0), r, w, cost=self._ec(eng, in0))
        else:
            self.P.add(eng, lambda e: e.tensor_scalar(out, in0, s1, s2, op0, op1), r, w, cost=self._ec(eng, in0))

    def tt(self, eng, out, in0, in1, op, r, w):
        self.P.add(eng, lambda e: e.tensor_tensor(out, in0, in1, op), r, w, cost=self._ec(eng, out))

    def stt(self, out, in0, scalar, in1, op0, op1, r, w):
        self.P.add("dve", lambda e: e.scalar_tensor_tensor(out, in0, scalar, in1, op0, op1), r, w, cost=self._ec("dve", out))

    def cp(self, eng, out, in_, r, w):
        if eng == "act":
            self.P.add("act", lambda e: e.activation(out, in_, AF.Copy), r, w, cost=self._ec("act", out))
        else:
            self.P.add(eng, lambda e: e.tensor_copy(out, in_), r, w, cost=self._ec(eng, out))

    def memset(self, eng, out, val, w):
        self.P.add(eng, lambda e: e.memset(out, val), (), w, cost=self._ec(eng, out, 0.5))

    def scan(self, out, d0, d1, init, r, w):
        self.P.add("dve", lambda e: e.tensor_tensor_scan(out, d0, d1, init, ALU.mult, ALU.add), r, w,
                   cost=self._ec("dve", out, 2.0))

    def dma(self, eng, out, in_, r, w, par=False, **kw):
        nb = self._n(out) * 128 * 4
        self.P.add(eng, lambda e: e.dma_start(out=out, in_=in_, **kw), r, w, dma=True, par=par,
                   cost=2.5 + nb / 150e3)

    def recip(self, out, in_, r, w):
        self.P.add("dve", lambda e: e.reciprocal(out, in_), r, w, cost=self._ec("dve", out))

    def debug_dump(self, name, t, shape, dt=F32):
        d = self.nc.dram_tensor("dbg_" + name, list(shape), dt, kind="ExternalOutput").ap()
        self.out_keys.append("dbg_" + name)
        self.dma("sp", d, t.ap if isinstance(t, T) else t, [t.name] if isinstance(t, T) else [], ["dbg_" + name])


def build(stop_after=None, debug=False, cut=None, skip=()):
    b = B()
    nc, P = b.nc, b.P
    NCD = dict(allow_slow_non_contiguous=True)

    xp = b.din("xp", [2048, D])
    xs = b.din("xs", [128, D])
    sa_conv = b.din("sa_conv", [48, 2048])
    sa_ssm = b.din("sa_ssm", [16, 1024, 128])
    sb_re = b.din("sb_re", [16, 4096])
    sb_im = b.din("sb_im", [16, 4096])
    sc_conv = b.din("sc_conv", [16, 30, D])
    norm_mix = b.din("norm_mix", [2, D])
    norm_ff = b.din("norm_ff", [2, D])
    norm_final = b.din("norm_final", [1, D])
    w_in = b.din("w_in", [D, IN_COLS])
    a_conv_w = b.din("a_conv_w", [4, 2048])
    a_conv_b = b.din("a_conv_b", [1, 2048])
    a_dt_bias = b.din("a_dt_bias", [1, 16])
    a_log = b.din("a_log", [1, 16])
    a_d = b.din("a_d", [1, 16])
    a_norm = b.din("a_norm", [1, 1024])
    lam_re = b.din("lam_re", [64, 64])
    lam_im = b.din("lam_im", [64, 64])
    log_step = b.din("log_step", [1, 64])
    s5_b_re = b.din("s5_b_re", [64, 64, 16])
    s5_b_im = b.din("s5_b_im", [64, 64, 16])
    s5_c_re = b.din("s5_c_re", [64, 16, 64])
    s5_c_im = b.din("s5_c_im", [64, 16, 64])
    s5_d = b.din("s5_d", [1, 1024])
    w_glu = b.din("w_glu", [1024, 1024])
    b_glu = b.din("b_glu", [1, 1024])
    w_out = b.din("w_out", [2048, D])
    c_w_pw1 = b.din("c_w_pw1", [D, 2048])
    c_b_pw1 = b.din("c_b_pw1", [1, 2048])
    c_w_dw = b.din("c_w_dw", [31, D])
    c_b_dw = b.din("c_b_dw", [1, D])
    c_ln_w = b.din("c_ln_w", [1, D])
    c_ln_b = b.din("c_ln_b", [1, D])
    c_w_pw2 = b.din("c_w_pw2", [D, D])
    c_b_pw2 = b.din("c_b_pw2", [1, D])
    w_ff1 = b.din("w_ff1", [2, D, 4096])
    w_ff2 = b.din("w_ff2", [2, 4096, D])

    y_p = b.dout("y_p", [2048, D])
    y_s = b.dout("y_s", [128, D])
    p_a_conv = b.dout("p_a_conv", [3, 2048])
    p_a_ssm = b.dout("p_a_ssm", [1024, 128])
    p_b_re = b.dout("p_b_re", [32, 128])
    p_b_im = b.dout("p_b_im", [32, 128])
    p_c_conv = b.dout("p_c_conv", [30, D])
    s_a_conv = b.dout("s_a_conv", [48, 2048])
    s_a_ssm = b.dout("s_a_ssm", [16, 1024, 128])
    s_b_re = b.dout("s_b_re", [16, 4096])
    s_b_im = b.dout("s_b_im", [16, 4096])
    s_c_conv = b.dout("s_c_conv", [16, 30, D])

    def x_src(t):
        return xp[t * 128:(t + 1) * 128, :] if t < 16 else xs

    bank = b.banks
    tbank = b.tbanks

    io = b.alloc("io", [128], I32)
    identf = b.alloc("identf", [128], F32)
    identb = b.alloc("identb", [128], BF16)
    tri_p = b.alloc("tri_p", [128], F32)
    tri_s = b.alloc("tri_s", [128], F32)
    blk_s = b.alloc("blk_s", [128], F32)
    ones = b.alloc("ones", [128], F32)
    neg_p4 = b.alloc("neg_p4", [4, 128], BF16)
    neg_s4 = b.alloc("neg_s4", [4, 128], BF16)
    seqmask = b.alloc("seqmask", [16], F32)
    lmask = b.alloc("lmask", [16, 8], F32)
    tau1 = b.alloc("tau1", [128], F32)
    cm = b.mark()
    tmpi = b.alloc("tmpi", [128], I32)
    tmpf = b.alloc("tmpf", [128], F32)
    tmpg = b.alloc("tmpg", [128], F32)
    P.add("pool", lambda e: e.iota(io.ap, [[1, 128]], base=0, channel_multiplier=-1), (), [io.name])
    b.ts("dve", identf.ap, io.ap, 0, None, ALU.is_equal, None, [io.name], [identf.name])
    b.cp("dve", identb.ap, identf.ap, [identf.name], [identb.name])
    b.ts("dve", tri_p.ap, io.ap, 0, None, ALU.is_ge, None, [io.name], [tri_p.name])
    b.memset("pool", ones.ap, 1.0, [ones.name])
    b.ts("dve", tmpf.ap, io.ap, 0, -32768.0, ALU.is_lt, ALU.mult, [io.name], [tmpf.name])
    b.cp("dve", neg_p4.ap, tmpf.ap.unsqueeze(1).to_broadcast([128, 4, 128]), [tmpf.name], [neg_p4.name])
    P.add("pool", lambda e: e.iota(tmpi.ap[:, 0:16], [[8, 16]], base=0, channel_multiplier=-1), (), [tmpi.name])
    b.ts("dve", tmpf.ap[:, 0:16], tmpi.ap[:, 0:16], 0, None, ALU.is_le, None, [tmpi.name], [tmpf.name])
    b.ts("dve", tmpg.ap[:, 0:16], tmpi.ap[:, 0:16], -7, None, ALU.is_ge, None, [tmpi.name], [tmpg.name])
    b.tt("dve", seqmask.ap, tmpf.ap[:, 0:16], tmpg.ap[:, 0:16], ALU.mult, [tmpf.name, tmpg.name], [seqmask.name])
    b.cp("dve", blk_s.ap.rearrange("p (s l) -> p s l", s=16), seqmask.ap.unsqueeze(2).to_broadcast([128, 16, 8]),
         [seqmask.name], [blk_s.name])
    b.tt("dve", tri_s.ap, tri_p.ap, blk_s.ap, ALU.mult, [tri_p.name, blk_s.name], [tri_s.name])
    b.ts("dve", tmpf.ap, tri_s.ap, -1.0, 32768.0, ALU.add, ALU.mult, [tri_s.name], [tmpf.name])
    b.cp("dve", neg_s4.ap, tmpf.ap.unsqueeze(1).to_broadcast([128, 4, 128]), [tmpf.name], [neg_s4.name])
    P.add("pool", lambda e: e.iota(tmpi.ap, [[0, 16], [1, 8]], base=0, channel_multiplier=0), (), [tmpi.name])
    b.ts("dve", lmask.ap.rearrange("p s l -> p (s l)"), tmpi.ap, 0, None, ALU.is_gt, None, [tmpi.name], [lmask.name])
    P.add("pool", lambda e: e.iota(tmpi.ap, [[1, 128]], base=1, channel_multiplier=0), (), [tmpi.name])
    b.cp("dve", tau1.ap, tmpi.ap, [tmpi.name], [tau1.name])
    b.release(cm)

    def load_cols(name, src_row, ncols):
        t = b.alloc(name, [ncols], F32)
        b.dma("sp", t.ap, src_row.rearrange("o (k p) -> p (o k)", p=128), [], [t.name], **NCD)
        return t

    def load_bcast(name, src_row, n):
        t = b.alloc(name, [n], F32)
        b.dma("sp", t.ap, src_row.partition_broadcast(128), [], [t.name])
        return t

    nmix0 = load_cols("nmix0", norm_mix[0:1, :], 8)
    nmix1 = load_cols("nmix1", norm_mix[1:2, :], 8)
    nff0 = load_cols("nff0", norm_ff[0:1, :], 8)
    nff1 = load_cols("nff1", norm_ff[1:2, :], 8)

    def norm_tile(xt_ap, xt_keys, wcol, dst_ap, dst_key, scr):
        junk, ss, xsb = scr
        b.act(junk.ap, xt_ap, AF.Square, xt_keys, [junk.name, ss.name], accum=ss.ap[:, 0:1])
        b.act(ss.ap[:, 1:2], ss.ap[:, 0:1], AF.Sqrt, [ss.name], [ss.name], bias=EPS, scale=1.0 / D)
        b.recip(ss.ap[:, 2:3], ss.ap[:, 1:2], [ss.name], [ss.name])
        b.ts("dve", xsb.ap, xt_ap, ss.ap[:, 2:3], None, ALU.mult, None, xt_keys + [ss.name], [xsb.name])
        if cut == 81:
            return
        tb = tbank[norm_tile.i % 2]
        norm_tile.i += 1
        for k in range(8):
            b.tr(tb.ap[:, k * 128:(k + 1) * 128], xsb.ap[:, k * 128:(k + 1) * 128], identb.ap,
                 [xsb.name, identb.name], [tb.name])
        if cut == 82:
            return
        b.tt("dve", dst_ap, tb.ap.rearrange("p (k t) -> p k t", k=8),
             wcol.ap.unsqueeze(2).to_broadcast([128, 8, 128]), ALU.mult, [tb.name, wcol.name], [dst_key])
    norm_tile.i = 0

    def norm_scratch():
        return (b.alloc("junk", [1024], BF16), b.alloc("ss", [4], F32), b.alloc("xsb", [1024], BF16))

    GROUPS = [(0, 4), (4, 4), (8, 4), (12, 4), (16, 1)]

    YB = b.alloc("YB", [8, NTOK], BF16)

    if "B" not in skip:
        mB = b.mark()
        LR = b.alloc("LR", [32], F32)
        LI = b.alloc("LI", [32], F32)
        ST = b.alloc("ST", [32], F32)
        MG = b.alloc("MG", [32], F32)
        TH = b.alloc("TH", [32], F32)
        KR = b.alloc("KR", [32], F32)
        KI = b.alloc("KI", [32], F32)
        COS = b.alloc("COS", [32, 128], F32)
        SIN = b.alloc("SIN", [32, 128], F32)
        BTz = b.alloc("BTz", [32, 2, 128], BF16)
        CTz = b.alloc("CTz", [32, 2, 128], BF16)
        dskip = load_cols("dskip", s5_d, 8)
        bglu = load_cols("bglu", b_glu, 8)
        for gl in range(2):
            rows = slice(64 * gl, 64 * gl + 64)
            b.dma("sp", LR.ap[rows, :], lam_re.rearrange("(gp gl) p -> gl p gp", gl=2)[gl], [], [LR.name], **NCD, par=True)
            b.dma("sp", LI.ap[rows, :], lam_im.rearrange("(gp gl) p -> gl p gp", gl=2)[gl], [], [LI.name], **NCD, par=True)
            b.dma("sp", ST.ap[rows, :], log_step.rearrange("o (gp gl) -> gl o gp", gl=2)[gl].partition_broadcast(64),
                  [], [ST.name], **NCD, par=True)

        def sin_rr(out_ap, in_ap, shift, t1, t2, keys_r, key_w, shape_kw=None):
            rk = keys_r
            b.ts("dve", out_ap, in_ap, 1.0 / TWO_PI, shift / TWO_PI, ALU.mult, ALU.add, rk, [key_w])
            b.cp("dve", t1.ap, out_ap, [key_w], [t1.name])
            b.cp("dve", t2.ap, t1.ap, [t1.name], [t2.name])
            b.ts("dve", out_ap, in_ap, shift, None, ALU.add, None, rk + [t1.name], [key_w])
            b.stt(out_ap, t2.ap, -TWO_PI, out_ap, ALU.mult, ALU.add, [t2.name, key_w], [key_w])
            b.ts("dve", t2.ap, out_ap, math.pi, -TWO_PI, ALU.is_gt, ALU.mult, [key_w], [t2.name])
            b.tt("dve", out_ap, out_ap, t2.ap, ALU.add, [key_w, t2.name], [key_w])
            b.ts("dve", t2.ap, out_ap, -math.pi, TWO_PI, ALU.is_lt, ALU.mult, [key_w], [t2.name])
            b.tt("dve", out_ap, out_ap, t2.ap, ALU.add, [key_w, t2.name], [key_w])
            b.act(out_ap, out_ap, AF.Sin, [key_w], [key_w])

        if cut == 1:
            P.finalize(final_keys=b.out_keys)
            return b
        mT = b.mark()
        ABR = b.alloc("ABR", [32], F32)
        ABI = b.alloc("ABI", [32], F32)
        w1 = b.alloc("w1", [32], F32)
        w2 = b.alloc("w2", [32], F32)
        w3 = b.alloc("w3", [32], F32)
        wi = b.alloc("wi", [32], I32)
        b.act(ST.ap, ST.ap, AF.Exp, [ST.name], [ST.name])
        b.tt("dve", w1.ap, LR.ap, ST.ap, ALU.mult, [LR.name, ST.name], [w1.name])
        b.act(MG.ap, w1.ap, AF.Exp, [w1.name], [MG.name])
        b.tt("dve", TH.ap, LI.ap, ST.ap, ALU.mult, [LI.name, ST.name], [TH.name])
        sin_rr(ABR.ap, TH.ap, math.pi / 2, wi, w2, [TH.name], ABR.name)
        sin_rr(ABI.ap, TH.ap, 0.0, wi, w2, [TH.name], ABI.name)
        b.tt("dve", ABR.ap, ABR.ap, MG.ap, ALU.mult, [ABR.name, MG.name], [ABR.name])
        b.tt("dve", ABI.ap, ABI.ap, MG.ap, ALU.mult, [ABI.name, MG.name], [ABI.name])
        b.tt("dve", w1.ap, LR.ap, LR.ap, ALU.mult, [LR.name], [w1.name])
        b.tt("dve", w2.ap, LI.ap, LI.ap, ALU.mult, [LI.name], [w2.name])
        b.tt("dve", w1.ap, w1.ap, w2.ap, ALU.add, [w1.name, w2.name], [w1.name])
        b.recip(w3.ap, w1.ap, [w1.name], [w3.name])
        b.ts("dve", w1.ap, ABR.ap, -1.0, None, ALU.add, None, [ABR.name], [w1.name])
        b.tt("dve", KR.ap, w1.ap, LR.ap, ALU.mult, [w1.name, LR.name], [KR.name])
        b.tt("dve", w2.ap, ABI.ap, LI.ap, ALU.mult, [ABI.name, LI.name], [w2.name])
        b.tt("dve", KR.ap, KR.ap, w2.ap, ALU.add, [KR.name, w2.name], [KR.name])
        b.tt("dve", KR.ap, KR.ap, w3.ap, ALU.mult, [KR.name, w3.name], [KR.name])
        b.tt("dve", KI.ap, ABI.ap, LR.ap, ALU.mult, [ABI.name, LR.name], [KI.name])
        b.tt("dve", w2.ap, w1.ap, LI.ap, ALU.mult, [w1.name, LI.name], [w2.name])
        b.tt("dve", KI.ap, KI.ap, w2.ap, ALU.subtract, [KI.name, w2.name], [KI.name])
        b.tt("dve", KI.ap, KI.ap, w3.ap, ALU.mult, [KI.name, w3.name], [KI.name])
        b.release(mT)
        if cut == 2:
            P.finalize(final_keys=b.out_keys)
            return b
        mT = b.mark()
        ti = b.alloc("ti", [8, 128], I32)
        tf = b.alloc("tf", [8, 128], F32)
        ang = b.alloc("ang", [8, 128], F32)
        for s8 in range(4):
            sl = slice(8 * s8, 8 * s8 + 8)
            b.tt("dve", ang.ap, TH.ap[:, sl].unsqueeze(2).to_broadcast([128, 8, 128]),
                 tau1.ap.unsqueeze(1).to_broadcast([128, 8, 128]), ALU.mult, [TH.name, tau1.name], [ang.name])
            sin_rr(COS.ap[:, sl, :], ang.ap, math.pi / 2, ti, tf, [ang.name], COS.name)
            sin_rr(SIN.ap[:, sl, :], ang.ap, 0.0, ti, tf, [ang.name], SIN.name)
        b.release(mT)
        if cut == 3:
            P.finalize(final_keys=b.out_keys)
            return b
        mT = b.mark()
        BR = b.alloc("BR", [32, 16], F32)
        BI = b.alloc("BI", [32, 16], F32)
        BBR = b.alloc("BBR", [32, 16], F32)
        BBI = b.alloc("BBI", [32, 16], F32)
        BX = b.alloc("BX", [32, 128], F32)
        for gl in range(2):
            rows = slice(64 * gl, 64 * gl + 64)
            b.dma("sp", BR.ap[rows], s5_b_re.rearrange("(gp gl) p c -> gl p gp c", gl=2)[gl], [], [BR.name], par=True)
            b.dma("sp", BI.ap[rows], s5_b_im.rearrange("(gp gl) p c -> gl p gp c", gl=2)[gl], [], [BI.name], par=True)
        krb = KR.ap.unsqueeze(2).to_broadcast([128, 32, 16])
        kib = KI.ap.unsqueeze(2).to_broadcast([128, 32, 16])
        b.tt("dve", BBR.ap, BR.ap, krb, ALU.mult, [BR.name, KR.name], [BBR.name])
        b.tt("dve", BX.ap[:, :, 0:16], BI.ap, kib, ALU.mult, [BI.name, KI.name], [BX.name])
        b.tt("dve", BBR.ap, BBR.ap, BX.ap[:, :, 0:16], ALU.subtract, [BBR.name, BX.name], [BBR.name])
        b.tt("dve", BBI.ap, BI.ap, krb, ALU.mult, [BI.name, KR.name], [BBI.name])
        b.tt("dve", BX.ap[:, :, 0:16], BR.ap, kib, ALU.mult, [BR.name, KI.name], [BX.name])
        b.tt("dve", BBI.ap, BBI.ap, BX.ap[:, :, 0:16], ALU.add, [BBI.name, BX.name], [BBI.name])
        for ri, src in enumerate((BBR, BBI)):
            b.memset("pool", BX.ap, 0.0, [BX.name])
            bxv = BX.ap.rearrange("p (q j) c -> p q j c", j=4)
            srcv = src.ap.rearrange("p (q j) c -> p q j c", j=4)
            for gl in range(2):
                rows = slice(64 * gl, 64 * gl + 64)
                for j in range(4):
                    c0 = 32 * j + 16 * gl
                    b.cp("dve", bxv[rows, :, j, c0:c0 + 16], srcv[rows, :, j, :], [src.name, BX.name], [BX.name])
            for gp in range(32):
                bk = bank[gp % 2]
                b.tr(bk.ap[:, 0:128], BX.ap[:, gp, :], identf.ap, [BX.name, identf.name], [bk.name])
                b.cp("act", BTz.ap[:, gp, ri, :], bk.ap[:, 0:128], [bk.name], [BTz.name])
        b.release(mT)
        if cut == 4:
            P.finalize(final_keys=b.out_keys)
            return b
        mT = b.mark()
        CRn = b.alloc("CRn", [8, 2, 64], F32)
        b.memset("pool", CTz.ap, 0.0, [CTz.name])
        ci_ = 0
        for ri, src in enumerate((s5_c_re, s5_c_im)):
            for dup in range(2):
                b.dma("sp", CRn.ap[:, :, dup, :], src.rearrange("(k g8) c p -> (g8 c) k p", g8=8), [], [CRn.name], par=True)
            for k in range(8):
                bk = bank[k % 2]
                b.mm(bk.ap[:, 0:128], CRn.ap[:, k, :, :].rearrange("p d q -> p (d q)"), identf.ap, True, True,
                     [CRn.name, identf.name], [bk.name])
                for gl in range(2):
                    rows = slice(64 * gl, 64 * gl + 64)
                    for j in range(4):
                        c0 = 32 * j + 16 * gl
                        eng_ = "dve" if ci_ % 2 else "act"
                        ci_ += 1
                        if eng_ == "dve":
                            b.ts("dve", CTz.ap[rows, 4 * k + j, ri, c0:c0 + 16], bk.ap[rows, c0:c0 + 16],
                                 (1.0 if ri == 0 else -1.0), None, ALU.mult, None, [bk.name, CTz.name], [CTz.name])
                        else:
                            b.act(CTz.ap[rows, 4 * k + j, ri, c0:c0 + 16], bk.ap[rows, c0:c0 + 16], AF.Copy,
                                  [bk.name, CTz.name], [CTz.name], scale=(1.0 if ri == 0 else -1.0))
        b.release(mT)
        if cut == 5:
            P.finalize(final_keys=b.out_keys)
            return b
        WU = b.alloc("WU", [8, 1024], BF16)
        WG = b.alloc("WG", [8, 1024], BF16)
        for k in range(8):
            b.dma("pool", WU.ap[:, k, :], w_in[k * 128:(k + 1) * 128, A_PROJ:IN_COLS], [], [WU.name], par=True)
            b.dma("pool", WG.ap[:, k, :], w_glu[k * 128:(k + 1) * 128, :], [], [WG.name], par=True)

        if cut == 6:
            P.finalize(final_keys=b.out_keys)
            return b
        CAR = [b.alloc("CARr", [32], F32), b.alloc("CARi", [32], F32)]
        H0M = [b.alloc("H0Mr", [32, 16], F32), b.alloc("H0Mi", [32, 16], F32)]
        SOUT = H0M
        b.memset("pool", CAR[0].ap, 0.0, [CAR[0].name])
        b.memset("pool", CAR[1].ap, 0.0, [CAR[1].name])
        mT = b.mark()
        h0n = b.alloc("h0n", [4096], F32)
        for ri, src in enumerate((sb_re, sb_im)):
            b.dma("sp", h0n.ap[0:16, :], src, [], [h0n.name])
            for gp in range(32):
                bk = bank[gp % 2]
                b.tr(bk.ap[:, 0:16], h0n.ap[0:16, gp * 128:(gp + 1) * 128], identf.ap[0:16, 0:16],
                     [h0n.name, identf.name], [bk.name])
                b.ts("dve", H0M[ri].ap[:, gp, :], bk.ap[:, 0:16], MG.ap[:, gp:gp + 1], None, ALU.mult, None,
                     [bk.name, MG.name], [H0M[ri].name])
        b.release(mT)

        if cut == 7:
            P.finalize(final_keys=b.out_keys)
            return b
        if stop_after == "B0":
            for nm, t_, shp, *dt_ in (("COS", COS, [128, 32, 128]), ("SIN", SIN, [128, 32, 128]), ("BTz", BTz, [128, 32, 2, 128], BF16),
                                ("CTz", CTz, [128, 32, 2, 128], BF16), ("KR", KR, [128, 32]), ("KI", KI, [128, 32]),
                                ("MG", MG, [128, 32]), ("H0Mr", H0M[0], [128, 32, 16])):
                b.debug_dump(nm, t_, shp, *dt_)
            P.finalize(final_keys=b.out_keys)
            return b
        mS = b.mark()
        scrN = norm_scratch()
        MSq = b.alloc("MSq", [4, 128], F32)
        XT = [b.alloc("XT0", [1024], F32)]
        XNg = b.alloc("XNg", [8, 256], BF16)
        U16 = b.alloc("U16", [8, 256], BF16)
        SETS = []
        Q1s = b.alloc("Q1", [512], F32)
        for si in range(2):
            SETS.append(dict(RR=b.alloc("RR", [512], F32), RI=b.alloc("RI", [512], F32), Q1=Q1s,
                             HR=b.alloc("HR", [512], F32),
                             HI=b.alloc("HI", [512], F32), HRb=b.alloc("HRb", [512], BF16), HIb=b.alloc("HIb", [512], BF16),
                             bu=(bank[2 * si], bank[2 * si + 1])))
        HL = [b.alloc("HLr", [32], F32), b.alloc("HLi", [32], F32)]
        HSm = [b.alloc("HSr", [32, 16], F32), b.alloc("HSi", [32, 16], F32)]
        cq = [b.alloc("cq1", [32], F32), b.alloc("cq2", [32], F32)]
        Y32s = [b.alloc("Y32a", [8, 128], F32), b.alloc("Y32b", [8, 128], F32)]
        CARM = [b.alloc("CARMr", [32], F32), b.alloc("CARMi", [32], F32)]
        b.memset("pool", CARM[0].ap, 0.0, [CARM[0].name])
        b.memset("pool", CARM[1].ap, 0.0, [CARM[1].name])
        mask0 = b.alloc("mask0", [128], F32)
        b.ts("dve", mask0.ap, tau1.ap, 1.5, None, ALU.is_gt, None, [tau1.name], [mask0.name])
        G1 = b.alloc("G1", [8, 128], F32)
        DSK = b.alloc("DSK", [8, 128], BF16)
        for q_ in range(8):
            b.ts("dve", DSK.ap[:, q_, :], identf.ap, dskip.ap[:, q_:q_ + 1], None, ALU.mult, None,
                 [identf.name, dskip.name], [DSK.name])
        YGb = b.alloc("YGb", [8, 128], BF16)
        SG = b.alloc("SG", [128], F32)
        U16s = [U16, b.alloc("U16b", [8, 256], BF16)]
        ck = [COS.name, SIN.name]
        grp = [(2 * i, 2) for i in range(8)] + [(16, 1)]

        def views(sample, pr):
            if sample:
                cosq = COS.ap[:, pr, 0:8].unsqueeze(2).to_broadcast([128, 4, 16, 8])
                sinq = SIN.ap[:, pr, 0:8].unsqueeze(2).to_broadcast([128, 4, 16, 8])
                v = lambda tile_: tile_.ap.rearrange("p (j s l) -> p j s l", j=4, s=16)
            else:
                cosq = COS.ap[:, pr, :]
                sinq = SIN.ap[:, pr, :]
                v = lambda tile_: tile_.ap.rearrange("p (j t) -> p j t", j=4)
            return cosq, sinq, v

        SSg = [b.alloc("ssg0", [4], F32), b.alloc("ssg1", [4], F32)]
        junk_, _ss_unused, xsb_ = scrN

        def gf_stats(gi, tt_):
            t = grp[gi][0] + tt_
            ss = SSg[tt_]
            xt = XT[0]
            b.dma("sp", xt.ap, x_src(t), [], [xt.name])
            b.act(junk_.ap, xt.ap, AF.Square, [xt.name], [junk_.name, ss.name], accum=ss.ap[:, 0:1])
            b.act(ss.ap[:, 1:2], ss.ap[:, 0:1], AF.Sqrt, [ss.name], [ss.name], bias=EPS, scale=1.0 / D)

        def gf_scale(gi, tt_):
            ss = SSg[tt_]
            xt = XT[0]
            b.recip(ss.ap[:, 2:3], ss.ap[:, 1:2], [ss.name], [ss.name])
            b.ts("dve", xsb_.ap, xt.ap, ss.ap[:, 2:3], None, ALU.mult, None, [xt.name, ss.name], [xsb_.name])
            tb = tbank[tt_]
            for k in range(8):
                b.tr(tb.ap[:, k * 128:(k + 1) * 128], xsb_.ap[:, k * 128:(k + 1) * 128], identb.ap,
                     [xsb_.name, identb.name], [tb.name])

        def gf_evac(gi, tt_):
            tb = tbank[tt_]
            b.tt("dve", XNg.ap[:, :, tt_ * 128:(tt_ + 1) * 128], tb.ap.rearrange("p (k t) -> p k t", k=8),
                 nmix0.ap.unsqueeze(2).to_broadcast([128, 8, 128]), ALU.mult, [tb.name, nmix0.name], [XNg.name])

        def gf_proj(gi):
            n = grp[gi][1] * 128
            U = U16s[gi % 2]
            for q in range(8):
                bk = bank[4 + q % 2]
                for k in range(8):
                    b.mm(bk.ap[:, 0:n], WU.ap[:, k, q * 128:(q + 1) * 128], XNg.ap[:, k, 0:n], k == 0, k == 7,
                         [WU.name, XNg.name], [bk.name])
                b.cp("act", U.ap[:, q, 0:n], bk.ap[:, 0:n], [bk.name], [U.name])

        def gf_stages(gi):
            nt_ = grp[gi][1]
            st = [[lambda: gf_stats(gi, 0)], [lambda: gf_scale(gi, 0)], [lambda: gf_evac(gi, 0)], []]
            if nt_ == 2:
                st[1].append(lambda: gf_stats(gi, 1))
                st[2].append(lambda: gf_scale(gi, 1))
                st[3].append(lambda: gf_evac(gi, 1))
            st[3].append(lambda: gf_proj(gi))
            return st

        def S1(it):
            gi, t, tt_, q, idx = it
            S_ = SETS[idx % 2]
            U = U16s[gi % 2]
            sample = (t == 16)
            cs = slice(tt_ * 128, (tt_ + 1) * 128)
            RR, RI, Q1, HR, HI, HRb, HIb = (S_[k_] for k_ in ("RR", "RI", "Q1", "HR", "HI", "HRb", "HIb"))
            bA, bB = S_["bu"]
            pr = slice(4 * q, 4 * q + 4)
            for ri, bk in enumerate((bA, bB)):
                for j in range(4):
                    b.mm(bk.ap[:, j * 128:(j + 1) * 128], BTz.ap[:, 4 * q + j, ri, :], U.ap[:, q, cs], True, True,
                         [BTz.name, U.name], [bk.name])
            cosq, sinq, v = views(sample, pr)
            mrow = (lmask.ap.rearrange("p s l -> p (s l)") if sample else mask0.ap)
            for j in range(4):
                b.act(MSq.ap[:, j, :], mrow, AF.Copy, [MG.name, lmask.name, mask0.name], [MSq.name],
                      scale=MG.ap[:, 4 * q + j:4 * q + j + 1])
            A_, B_ = v(bA), v(bB)
            b.tt("dve", v(RR), A_, cosq, ALU.mult, [bA.name] + ck, [RR.name])
            b.tt("dve", v(Q1), B_, sinq, ALU.mult, [bB.name] + ck, [Q1.name])
            b.tt("dve", v(RR), v(RR), v(Q1), ALU.add, [RR.name, Q1.name], [RR.name])
            b.tt("dve", v(RI), B_, cosq, ALU.mult, [bB.name] + ck, [RI.name])
            b.tt("dve", v(Q1), A_, sinq, ALU.mult, [bA.name] + ck, [Q1.name])
            b.tt("dve", v(RI), v(RI), v(Q1), ALU.subtract, [RI.name, Q1.name], [RI.name])
            for ri, (rt, ht) in enumerate(((RR, HR), (RI, HI))):
                if sample:
                    b.tt("dve", v(rt)[:, :, :, 0], v(rt)[:, :, :, 0], H0M[ri].ap[:, pr, :], ALU.add,
                         [rt.name, H0M[ri].name], [rt.name])
                else:
                    b.tt("dve", v(rt)[:, :, 0], v(rt)[:, :, 0], CARM[ri].ap[:, pr], ALU.add,
                         [rt.name, CARM[ri].name], [rt.name])
                b.scan(ht.ap, MSq.ap.rearrange("p j t -> p (j t)"), rt.ap, 0.0, [rt.name, MSq.name], [ht.name])
                if sample:
                    b.cp("act", HSm[ri].ap[:, pr, :], v(ht)[:, :, :, 7], [ht.name], [HSm[ri].name])
                else:
                    b.cp("act", HL[ri].ap[:, pr], v(ht)[:, :, 127], [ht.name], [HL[ri].name])
            if q == 7 and not sample:
                c127, s127 = COS.ap[:, :, 127], SIN.ap[:, :, 127]
                b.tt("dve", cq[0].ap, HL[0].ap, c127, ALU.mult, [HL[0].name, COS.name], [cq[0].name])
                b.tt("dve", cq[1].ap, HL[1].ap, s127, ALU.mult, [HL[1].name, SIN.name], [cq[1].name])
                b.tt("dve", CAR[0].ap, cq[0].ap, cq[1].ap, ALU.subtract, [cq[0].name, cq[1].name], [CAR[0].name])
                b.tt("dve", cq[0].ap, HL[0].ap, s127, ALU.mult, [HL[0].name, SIN.name], [cq[0].name])
                b.tt("dve", cq[1].ap, HL[1].ap, c127, ALU.mult, [HL[1].name, COS.name], [cq[1].name])
                b.tt("dve", CAR[1].ap, cq[0].ap, cq[1].ap, ALU.add, [cq[0].name, cq[1].name], [CAR[1].name])
                for ri in range(2):
                    b.tt("dve", CARM[ri].ap, CAR[ri].ap, MG.ap, ALU.mult, [CAR[ri].name, MG.name], [CARM[ri].name])
            b.tt("dve", v(Q1), v(HR), cosq, ALU.mult, [HR.name] + ck, [Q1.name])
            b.tt("dve", v(RR), v(HI), sinq, ALU.mult, [HI.name] + ck, [RR.name])
            b.tt("dve", v(HRb), v(Q1), v(RR), ALU.subtract, [Q1.name, RR.name], [HRb.name])
            b.tt("dve", v(Q1), v(HR), sinq, ALU.mult, [HR.name] + ck, [Q1.name])
            b.tt("dve", v(RR), v(HI), cosq, ALU.mult, [HI.name] + ck, [RR.name])
            b.tt("dve", v(HIb), v(Q1), v(RR), ALU.add, [Q1.name, RR.name], [HIb.name])

        def S3(it):
            gi, t, tt_, q, idx = it
            S_ = SETS[idx % 2]
            U = U16s[gi % 2]
            Y32 = Y32s[t % 2]
            cs = slice(tt_ * 128, (tt_ + 1) * 128)
            bk = bank[4 + q % 2]
            i = 0
            for j in range(4):
                for ri, hb in enumerate((S_["HRb"], S_["HIb"])):
                    b.mm(bk.ap[:, 0:128], CTz.ap[:, 4 * q + j, ri, :], hb.ap[:, j * 128:(j + 1) * 128], i == 0, False,
                         [CTz.name, hb.name], [bk.name])
                    i += 1
            b.mm(bk.ap[:, 0:128], DSK.ap[:, q, :], U.ap[:, q, cs], False, True, [DSK.name, U.name], [bk.name])
            b.cp("act", Y32.ap[:, q, :], bk.ap[:, 0:128], [bk.name], [Y32.name])

        def TL0(t):
            b.act(G1.ap, Y32s[t % 2].ap, AF.Square, [Y32s[t % 2].name], [G1.name])

        def TL1(t):
            Y32 = Y32s[t % 2]
            b.ts("dve", G1.ap, G1.ap, 0.044715, 1.0, ALU.mult, ALU.add, [G1.name], [G1.name])
            b.tt("dve", G1.ap, G1.ap, Y32.ap, ALU.mult, [G1.name, Y32.name], [G1.name])
            b.act(G1.ap, G1.ap, AF.Sigmoid, [G1.name], [G1.name], scale=1.5957691216057308)

        def TL2(t):
            Y32 = Y32s[t % 2]
            b.tt("dve", Y32.ap, Y32.ap, G1.ap, ALU.mult, [Y32.name, G1.name], [Y32.name])
            b.cp("act", YGb.ap, Y32.ap, [Y32.name], [YGb.name])

        def TL3(t):
            for o in range(8):
                bk = bank[4 + o % 2]
                for k in range(8):
                    b.mm(bk.ap[:, 0:128], WG.ap[:, k, o * 128:(o + 1) * 128], YGb.ap[:, k, :], k == 0, k == 7,
                         [WG.name, YGb.name], [bk.name])
                b.act(G1.ap[:, o, :], bk.ap[:, 0:128], AF.Sigmoid, [bk.name, bglu.name], [G1.name], bias=bglu.ap[:, o:o + 1])

        def TL4(t):
            tok = slice(t * 128, (t + 1) * 128)
            b.tt("dve", YB.ap[:, :, tok], Y32s[t % 2].ap, G1.ap, ALU.mult, [Y32s[t % 2].name, G1.name], [YB.name])

        items = []
        first_item = {}
        for gi, (t0, ntile) in enumerate(grp):
            for tt_ in range(ntile):
                for q in range(8):
                    if gi not in first_item:
                        first_item[gi] = len(items)
                    items.append((gi, t0 + tt_, tt_, q, len(items)))
        NI = len(items)
        prelude = {}
        for gi in range(len(grp)):
            for k_, fl in enumerate(gf_stages(gi)):
                prelude.setdefault(max(0, first_item[gi] - 5 + k_), []).extend(fl)
        tails = []
        TLS = (TL0, TL1, TL2, TL3, TL4)
        for step in range(NI + 8):
            for f_ in prelude.get(step, ()):
                f_()
            if step < NI:
                S1(items[step])
            nt_ = []
            for (tc, ms) in tails:
                TLS[ms](tc)
                if ms < 4:
                    nt_.append((tc, ms + 1))
            tails = nt_
            if 1 <= step <= NI:
                it = items[step - 1]
                S3(it)
                if it[3] == 7:
                    tails.append((it[1], 0))
        assert not tails
        mQ = b.mark()
        sq_ = [T(Q1s.name, Q1s.ap.rearrange("p (a c) -> p a c", a=32)), T(SETS[0]["RR"].name, SETS[0]["RR"].ap.rearrange("p (a c) -> p a c", a=32))]
        c7 = COS.ap[:, :, 7].unsqueeze(2).to_broadcast([128, 32, 16])
        s7 = SIN.ap[:, :, 7].unsqueeze(2).to_broadcast([128, 32, 16])
        b.tt("dve", sq_[0].ap, HSm[0].ap, c7, ALU.mult, [HSm[0].name, COS.name], [sq_[0].name])
        b.tt("dve", sq_[1].ap, HSm[1].ap, s7, ALU.mult, [HSm[1].name, SIN.name], [sq_[1].name])
        b.tt("dve", SOUT[0].ap, sq_[0].ap, sq_[1].ap, ALU.subtract, [sq_[0].name, sq_[1].name], [SOUT[0].name])
        b.tt("dve", sq_[0].ap, HSm[0].ap, s7, ALU.mult, [HSm[0].name, SIN.name], [sq_[0].name])
        b.tt("dve", sq_[1].ap, HSm[1].ap, c7, ALU.mult, [HSm[1].name, COS.name], [sq_[1].name])
        b.tt("dve", SOUT[1].ap, sq_[0].ap, sq_[1].ap, ALU.add, [sq_[0].name, sq_[1].name], [SOUT[1].name])
        b.release(mQ)
        b.release(mS)
        mT = b.mark()
        so = b.alloc("so", [4096], F32)
        for ri, (dst_s, dst_p) in enumerate(((s_b_re, p_b_re), (s_b_im, p_b_im))):
            for gp in range(32):
                bk = bank[gp % 2]
                b.tr(bk.ap[0:16, 0:128], SOUT[ri].ap[:, gp, :], identf.ap, [SOUT[ri].name, identf.name], [bk.name])
                b.cp("act", so.ap[0:16, gp * 128:(gp + 1) * 128], bk.ap[0:16, 0:128], [bk.name], [so.name])
            b.dma("sp", dst_s, so.ap[0:16, :], [so.name], ["s_b_re" if ri == 0 else "s_b_im"])
            bk = bank[2]
            b.tr(bk.ap[0:32, 0:128], CAR[ri].ap, identf.ap, [CAR[ri].name, identf.name], [bk.name])
            b.cp("act", so.ap[0:32, 0:128], bk.ap[0:32, 0:128], [bk.name], [so.name])
            b.dma("sp", dst_p, so.ap[0:32, 0:128], [so.name], ["p_b_re" if ri == 0 else "p_b_im"])
        b.release(mT)
        if debug:
            b.debug_dump("YB", YB, [128, 8, NTOK], BF16)
        b.release(mB)
    if stop_after == "B":
        P.finalize(final_keys=b.out_keys)
        return b


    YA = b.alloc("YA", [8, NTOK], BF16)
    if "A" not in skip:
        mA = b.mark()
        WA = b.alloc("WA", [8, A_PROJ], BF16)
        for k in range(8):
            b.dma("pool", WA.ap[:, k, :], w_in[k * 128:(k + 1) * 128, 0:A_PROJ], [], [WA.name])
        cw = b.alloc("cw", [16, 4], F32)
        for w_ in range(4):
            b.dma("sp", cw.ap[:, :, w_], a_conv_w[w_:w_ + 1, :].rearrange("o (c p) -> p (o c)", p=128), [], [cw.name], **NCD)
        cbias = load_cols("cbias", a_conv_b, 16)
        dtb = load_bcast("dtb", a_dt_bias, 16)
        abc = load_bcast("abc", a_log, 16)
        dbc = load_bcast("dbc", a_d, 16)
        anorm = load_cols("anormc", a_norm, 8)
        b.act(abc.ap, abc.ap, AF.Exp, [abc.name], [abc.name])
        b.ts("dve", abc.ap, abc.ap, -1.0, None, ALU.mult, None, [abc.name], [abc.name])
        SEL = b.alloc("SEL", [16, 128], BF16)
        mT = b.mark()
        seli = b.alloc("seli", [16, 128], I32)
        P.add("pool", lambda e: e.iota(seli.ap, [[-1, 16], [1, 16], [0, 8]], base=0, channel_multiplier=0), (), [seli.name])
        b.ts("dve", SEL.ap, seli.ap, 0, None, ALU.is_equal, None, [seli.name], [SEL.name])
        b.release(mT)

        ssA = b.alloc("ssA", [4], F32)
        XNg = b.alloc("XNgA", [8, 256], BF16)
        XP = [b.alloc("XPa", [259], F32), b.alloc("XPb", [259], F32)]
        HALO = b.alloc("HALO", [16, 3], F32)
        XC = b.alloc("XCv", [16, 256], BF16)
        XCa = b.alloc("XCa", [256], F32)
        Rt = b.alloc("Rt", [4, 128], F32)
        Lt = b.alloc("Lt", [4, 128], F32)
        PB = []
        for pi in range(2):
            PB.append(dict(XTM=b.alloc("XTM", [1024], BF16), BTM=b.alloc("BTM", [512], BF16), ZS=b.alloc("ZS", [1024], F32),
                           sm=b.alloc("sm", [16, 16], F32), MT=b.alloc("MT", [16, 128], BF16), XDT=b.alloc("XDT", [1024], BF16),
                           XDE=b.alloc("XDE", [1024], BF16), CTk=b.alloc("CTk", [4, 128], BF16)))
        mT = b.mark()
        HIST = b.alloc("HIST", [2048], F32)
        b.release(mT)
        H0 = b.alloc("H0", [8, 128], F32)
        NS = b.alloc("NS", [8, 128], F32)
        b.release(mT)
        H0s = [H0, None]
        NSs = [NS, None]
        T1 = b.alloc("T1", [1024], F32)
        mT2 = b.mark()
        XT0 = b.alloc("XT0a", [1024], F32)
        b.release(mT2)
        T2 = b.alloc("T2", [1024], F32)
        mT2 = b.mark()
        xsbA = b.alloc("xsbA", [1024], BF16)
        b.release(mT2)
        YAt = b.alloc("YAt", [1024], BF16)
        STf = b.alloc("STf", [1024], F32)
        mT2 = b.mark()
        STs = b.alloc("STs", [1024], BF16)
        b.release(mT2)
        STb = b.alloc("STb", [1024], BF16)
        CTm = b.alloc("CTm", [4, 128], BF16)
        Bm = b.alloc("Bm", [512], BF16)
        CDT = b.alloc("CDT", [8, 16], F32)
        c48 = b.alloc("c48", [48], F32)
        H0s[1] = b.alloc("H0b", [8, 128], F32)
        NSs[1] = b.alloc("NSb", [8, 128], F32)
        b.memset("pool", STf.ap, 0.0, [STf.name])
        b.memset("pool", STb.ap, 0.0, [STb.name])
        b.memset("pool", HALO.ap, 0.0, [HALO.name])
        (R_PRE, R_ABS, R_E, R_L, R_DT, R_DA, R_CUM, R_NCUM, R_DEND, R_ECUM, R_CD, R_DTD, R_MS, R_RS) = range(14)
        grpA = [(2 * i, 2) for i in range(8)] + [(16, 1)]
        TOPA = b.top

        def GFA(gi):
            t0, ntile = grpA[gi]
            n = ntile * 128
            sample_g = (t0 == 16)
            for tt_ in range(ntile):
                t = t0 + tt_
                b.dma("sp", XT0.ap, x_src(t), [], [XT0.name])
                norm_tile(XT0.ap, [XT0.name], nmix0, XNg.ap[:, :, tt_ * 128:(tt_ + 1) * 128], XNg.name, (xsbA, ssA, xsbA))
            if sample_g:
                b.dma("sp", HIST.ap[0:48, :], sa_conv, [], [HIST.name])
            for c in range(16):
                xp = XP[c % 2]
                bk = bank[c % 2]
                accb = bank[2 + c % 2]
                if sample_g:
                    xpv = xp.ap[:, 0:176].rearrange("p (s r) -> p s r", s=16)
                    bk2 = bank[4 + c % 2]
                    b.tr(bk2.ap[:, 0:48], HIST.ap[0:48, c * 128:(c + 1) * 128], identf.ap[0:48, 0:48],
                         [HIST.name, identf.name], [bk2.name])
                    b.cp("act", xpv[:, :, 0:3], bk2.ap[:, 0:48].rearrange("p (s r) -> p s r", s=16), [bk2.name], [xp.name])
                else:
                    b.cp("act", xp.ap[:, 0:3], HALO.ap[:, c, :], [HALO.name], [xp.name])
                for k in range(8):
                    b.mm(bk.ap[:, 0:n], WA.ap[:, k, 1024 + c * 128:1024 + (c + 1) * 128], XNg.ap[:, k, 0:n], k == 0, k == 7,
                         [WA.name, XNg.name], [bk.name])
                if sample_g:
                    b.cp("act", xpv[:, :, 3:11], bk.ap[:, 0:128].rearrange("p (s l) -> p s l", s=16), [bk.name], [xp.name])
                    sh = lambda w_: xpv[:, :, w_:w_ + 8]
                    acc = accb.ap[:, 0:128].rearrange("p (s l) -> p s l", s=16)
                    xco = XC.ap[:, c, 0:128].rearrange("p (s l) -> p s l", s=16)
                else:
                    b.cp("act", xp.ap[:, 3:3 + n], bk.ap[:, 0:n], [bk.name], [xp.name])
                    sh = lambda w_: xp.ap[:, w_:w_ + n]
                    acc = accb.ap[:, 0:n]
                    xco = XC.ap[:, c, 0:n]
                b.ts("dve", acc, sh(3), cw.ap[:, c, 3:4], cbias.ap[:, c:c + 1], ALU.mult, ALU.add,
                     [xp.name, cw.name, cbias.name], [accb.name])
                for w_ in range(3):
                    b.stt(acc, sh(w_), cw.ap[:, c, w_:w_ + 1], acc, ALU.mult, ALU.add, [xp.name, cw.name, accb.name], [accb.name])
                b.act(xco, acc, AF.Silu, [accb.name], [XC.name])
                if sample_g:
                    bk3 = bank[4 + c % 2]
                    b.cp("dve", c48.ap.rearrange("p (s r) -> p s r", s=16), xpv[:, :, 8:11], [xp.name], [c48.name])
                    b.tr(bk3.ap[0:48, 128:256], c48.ap, identf.ap, [c48.name, identf.name], [bk3.name])
                    b.cp("act", HIST.ap[0:48, c * 128:(c + 1) * 128], bk3.ap[0:48, 128:256], [bk3.name], [HIST.name])
                else:
                    b.cp("act", HALO.ap[:, c, :], xp.ap[:, n:n + 3], [xp.name], [HALO.name])
            if sample_g:
                b.dma("sp", s_a_conv, HIST.ap[0:48, :], [HIST.name], ["s_a_conv"])
            if t0 == 14:
                for c4 in range(4):
                    bk3 = bank[4 + c4 % 2]
                    for cc in range(4):
                        c = 4 * c4 + cc
                        b.tr(bk3.ap[0:3, cc * 128:(cc + 1) * 128], HALO.ap[:, c, :], identf.ap, [HALO.name, identf.name], [bk3.name])
                    b.cp("act", HIST.ap[0:3, c4 * 512:(c4 + 1) * 512], bk3.ap[0:3, :], [bk3.name], [HIST.name])
                b.dma("sp", p_a_conv, HIST.ap[0:3, :], [HIST.name], ["p_a_conv"])

        def FA(it):
            gi, t, tt_, idx = it
            PBi = PB[idx % 2]
            XTM, BTM, ZS, sm, MT, XDT, XDE, CTk = (PBi[k_] for k_ in ("XTM", "BTM", "ZS", "sm", "MT", "XDT", "XDE", "CTk"))
            smr = lambda i, n=16: sm.ap[:, i, 0:n]
            smk = [sm.name]
            sample = (t == 16)
            cs = slice(tt_ * 128, (tt_ + 1) * 128)
            tri = tri_s if sample else tri_p
            blk = blk_s if sample else ones
            neg4 = neg_s4 if sample else neg_p4
            tb = tbank[0]
            for c in range(12):
                b.tr(tb.ap[:, (c % 8) * 128:(c % 8 + 1) * 128], XC.ap[:, c, cs], identb.ap, [XC.name, identb.name], [tb.name])
                if c == 7:
                    b.cp("act", XTM.ap, tb.ap, [tb.name], [XTM.name])
            b.cp("act", BTM.ap, tb.ap[:, 0:512], [tb.name], [BTM.name])
            b.cp("pool", CTk.ap, XC.ap[:, 12:16, cs], [XC.name], [CTk.name])
            for hf in range(2):
                bk = bank[2]
                for k in range(8):
                    b.mm(bk.ap, XNg.ap[:, k, cs], WA.ap[:, k, hf * 512:(hf + 1) * 512], k == 0, k == 7,
                         [XNg.name, WA.name], [bk.name])
                b.act(ZS.ap[:, hf * 512:(hf + 1) * 512], bk.ap, AF.Silu, [bk.name], [ZS.name])
            bk = bank[0]
            for k in range(8):
                b.mm(bk.ap[:, 0:16], XNg.ap[:, k, cs], WA.ap[:, k, 3072:3088], k == 0, k == 7, [XNg.name, WA.name], [bk.name])
            b.tt("dve", smr(R_PRE), bk.ap[:, 0:16], dtb.ap, ALU.add, [bk.name, dtb.name], smk)
            b.act(smr(R_ABS), smr(R_PRE), AF.Abs, smk, smk)
            b.act(smr(R_E), smr(R_ABS), AF.Exp, smk, smk, scale=-1.0)
            b.act(smr(R_L), smr(R_E), AF.Ln, smk, smk, bias=1.0)
            b.ts("dve", smr(R_DT), smr(R_PRE), 0.0, None, ALU.max, None, smk, smk)
            b.tt("dve", smr(R_DT), smr(R_DT), smr(R_L), ALU.add, smk, smk)
            b.tt("dve", smr(R_DA), smr(R_DT), abc.ap, ALU.mult, smk + [abc.name], smk)
            b.mm(bk.ap[:, 16:32], tri.ap, smr(R_DA), True, True, [tri.name, sm.name], [bk.name])
            b.mm(bk.ap[:, 32:48], blk.ap, smr(R_DA), True, True, [blk.name, sm.name], [bk.name])
            b.cp("dve", smr(R_CUM), bk.ap[:, 16:32], [bk.name], smk)
            b.ts("dve", smr(R_NCUM), smr(R_CUM), -1.0, None, ALU.mult, None, smk, smk)
            b.tt("dve", smr(R_DEND), bk.ap[:, 32:48], smr(R_CUM), ALU.subtract, [bk.name] + smk, smk)
            b.act(smr(R_DEND), smr(R_DEND), AF.Exp, smk, smk)
            b.act(smr(R_ECUM), smr(R_CUM), AF.Exp, smk, smk)
            b.act(smr(R_CD), bk.ap[:, 32:48], AF.Exp, [bk.name], smk)
            b.tt("dve", smr(R_DTD), smr(R_DT), smr(R_DEND), ALU.mult, smk, smk)
            xv = XTM.ap.rearrange("p (h q) -> p h q", h=16)
            b.tt("dve", XDT.ap.rearrange("p (h q) -> p h q", h=16), xv, smr(R_DT).unsqueeze(2).to_broadcast([128, 16, 64]),
                 ALU.mult, [XTM.name] + smk, [XDT.name])
            b.tt("dve", XDE.ap.rearrange("p (h q) -> p h q", h=16), xv, smr(R_DTD).unsqueeze(2).to_broadcast([128, 16, 64]),
                 ALU.mult, [XTM.name] + smk, [XDE.name])
            for g in range(4):
                bk1 = bank[1]
                b.tt("dve", Rt.ap, sm.ap[:, R_DA, 4 * g:4 * g + 4].unsqueeze(2).to_broadcast([128, 4, 128]),
                     tri.ap.unsqueeze(1).to_broadcast([128, 4, 128]), ALU.mult, smk + [tri.name], [Rt.name])
                b.mm(bk1.ap, ones.ap, Rt.ap.rearrange("p e i -> p (e i)"), True, False,
                     [ones.name, Rt.name], [bk1.name])
                b.mm(bk1.ap, identb.ap, neg4.ap.rearrange("p e i -> p (e i)"), False, True, [identb.name, neg4.name], [bk1.name])
                for e_ in range(4):
                    h = 4 * g + e_
                    b.act(Lt.ap[:, e_, :], bk1.ap[:, e_ * 128:(e_ + 1) * 128], AF.Exp, [bk1.name] + smk, [Lt.name],
                          bias=sm.ap[:, R_NCUM, h:h + 1])
                bk2 = bank[0]
                b.mm(bk2.ap[:, 128:256], XC.ap[:, 8 + g, cs], XC.ap[:, 12 + g, cs], True, True, [XC.name], [bk2.name])
                b.tt("dve", MT.ap[:, 4 * g:4 * g + 4, :], Lt.ap, bk2.ap[:, 128:256].unsqueeze(1).to_broadcast([128, 4, 128]),
                     ALU.mult, [Lt.name, bk2.name], [MT.name])

        def KA(it):
            gi, t, tt_, idx = it
            PBi = PB[idx % 2]
            XTM, BTM, ZS, sm, MT, XDT, XDE, CTk = (PBi[k_] for k_ in ("XTM", "BTM", "ZS", "sm", "MT", "XDT", "XDE", "CTk"))
            smr = lambda i, n=16: sm.ap[:, i, 0:n]
            smk = [sm.name]
            sample = (t == 16)
            tok = slice(t * 128, (t + 1) * 128)
            obk = [bank[5], bank[5]] if not sample else [bank[3], bank[4]]
            if sample:
                b.cp("dve", T2.ap.rearrange("p (h q) -> p h q", h=16), smr(R_CD).unsqueeze(2).to_broadcast([128, 16, 64]),
                     smk, [T2.name])
                for k in range(8):
                    bkc = bank[k % 2]
                    b.tr(bkc.ap[:, 0:128], T2.ap[:, k * 128:(k + 1) * 128], identf.ap, [T2.name, identf.name], [bkc.name])
                    b.cp("act", CDT.ap[:, k, :], bkc.ap[:, 0:128:8], [bkc.name], [CDT.name])
                for s_ in range(16):
                    H0, NS = H0s[s_ % 2], NSs[s_ % 2]
                    b.dma("sp", H0.ap, sa_ssm[s_].rearrange("(k p) n -> p k n", p=128), [], [H0.name])
                    for k in range(8):
                        bkc = bank[k % 2]
                        b.tr(bkc.ap[:, 0:128], H0.ap[:, k, :], identf.ap, [H0.name, identf.name], [bkc.name])
                        b.cp("act" if k % 2 else "dve", STs.ap[:, k * 128:(k + 1) * 128], bkc.ap[:, 0:128], [bkc.name], [STs.name])
                    b.tt("dve", CTm.ap, CTk.ap, SEL.ap[:, s_, :].unsqueeze(1).to_broadcast([128, 4, 128]), ALU.mult,
                         [CTk.name, SEL.name], [CTm.name])
                    for g in range(4):
                        ob = obk[g // 2]
                        b.mm(ob.ap[:, (g % 2) * 256:(g % 2 + 1) * 256], CTm.ap[:, g, :], STs.ap[:, g * 256:(g + 1) * 256],
                             s_ == 0 and g % 2 == 0, s_ == 15, [CTm.name, STs.name], [ob.name])
                    b.ts("dve", Bm.ap, BTM.ap, seqmask.ap[:, s_:s_ + 1], None, ALU.mult, None, [BTM.name, seqmask.name], [Bm.name])
                    for k in range(8):
                        g = k // 2
                        bkn = bank[2] if k % 2 == 0 else bank[5]
                        b.mm(bkn.ap[:, 0:128], XDE.ap[:, k * 128:(k + 1) * 128], Bm.ap[:, g * 128:(g + 1) * 128], True, True,
                             [XDE.name, Bm.name], [bkn.name])
                        b.stt(NS.ap[:, k, :], H0.ap[:, k, :], CDT.ap[:, k, s_:s_ + 1], bkn.ap[:, 0:128], ALU.mult, ALU.add,
                              [H0.name, CDT.name, bkn.name], [NS.name])
                    b.dma("sp", s_a_ssm[s_].rearrange("(k p) n -> p k n", p=128), NS.ap, [NS.name], ["s_a_ssm"], par=True)
            ecb = smr(R_ECUM)
            v3 = lambda ap_: ap_.rearrange("p (h q) -> p h q", h=8)
            if sample:
                for hf in range(2):
                    cols = slice(hf * 512, (hf + 1) * 512)
                    b.tt("dve", v3(T1.ap[:, cols]), v3(obk[hf].ap), ecb[:, hf * 8:(hf + 1) * 8].unsqueeze(2).to_broadcast([128, 8, 64]),
                         ALU.mult, [obk[hf].name] + smk, [T1.name])
            for h in range(16):
                bk = bank[3 + h // 8]
                b.mm(bk.ap[:, (h % 8) * 64:(h % 8 + 1) * 64], MT.ap[:, h, :], XDT.ap[:, h * 64:(h + 1) * 64], True, True,
                     [MT.name, XDT.name], [bk.name])
            for hf in range(2):
                cols = slice(hf * 512, (hf + 1) * 512)
                if not sample:
                    ob = bank[5]
                    for g2 in range(2):
                        g = 2 * hf + g2
                        b.mm(ob.ap[:, g2 * 256:(g2 + 1) * 256], CTk.ap[:, g, :], STb.ap[:, g * 256:(g + 1) * 256],
                             True, True, [CTk.name, STb.name], [ob.name])
                    b.tt("dve", v3(T1.ap[:, cols]), v3(ob.ap), ecb[:, hf * 8:(hf + 1) * 8].unsqueeze(2).to_broadcast([128, 8, 64]),
                         ALU.mult, [ob.name] + smk, [T1.name])
                b.tt("dve", T1.ap[:, cols], T1.ap[:, cols], bank[3 + hf].ap, ALU.add, [T1.name, bank[3 + hf].name], [T1.name])
                b.tt("pool", v3(T2.ap[:, cols]), v3(XTM.ap[:, cols]), dbc.ap[:, hf * 8:(hf + 1) * 8].unsqueeze(2).to_broadcast([128, 8, 64]),
                     ALU.mult, [XTM.name, dbc.name], [T2.name])
                b.tt("dve", T1.ap[:, cols], T1.ap[:, cols], T2.ap[:, cols], ALU.add, [T1.name, T2.name], [T1.name])
                b.tt("dve", T1.ap[:, cols], T1.ap[:, cols], ZS.ap[:, cols], ALU.mult, [T1.name, ZS.name], [T1.name])
            for g in range(4):
                b.act(T2.ap[:, g * 256:(g + 1) * 256], T1.ap[:, g * 256:(g + 1) * 256], AF.Square, [T1.name], [T2.name, sm.name],
                      accum=sm.ap[:, R_MS, g:g + 1])
            b.act(smr(R_RS, 4), smr(R_MS, 4), AF.Sqrt, smk, smk, bias=EPS, scale=1.0 / 256)
            b.recip(smr(R_RS, 4), smr(R_RS, 4), smk, smk)
            b.tt("dve", YAt.ap.rearrange("p (g q) -> p g q", g=4), T1.ap.rearrange("p (g q) -> p g q", g=4),
                 smr(R_RS, 4).unsqueeze(2).to_broadcast([128, 4, 256]), ALU.mult, [T1.name] + smk, [YAt.name])
            tb = tbank[1]
            for k in range(8):
                b.tr(tb.ap[:, k * 128:(k + 1) * 128], YAt.ap[:, k * 128:(k + 1) * 128], identb.ap, [YAt.name, identb.name], [tb.name])
            b.tt("dve", YA.ap[:, :, tok], tb.ap.rearrange("p (k t) -> p k t", k=8),
                 anorm.ap.unsqueeze(2).to_broadcast([128, 8, 128]), ALU.mult, [tb.name, anorm.name], [YA.name])
            if not sample:
                for g in range(4):
                    bk = bank[3 + g // 2]
                    b.mm(bk.ap[:, (g % 2) * 256:(g % 2 + 1) * 256], BTM.ap[:, g * 128:(g + 1) * 128], XDE.ap[:, g * 256:(g + 1) * 256],
                         True, True, [BTM.name, XDE.name], [bk.name])
                b.tt("dve", STf.ap.rearrange("p (h q) -> p h q", h=16), STf.ap.rearrange("p (h q) -> p h q", h=16),
                     smr(R_CD).unsqueeze(2).to_broadcast([128, 16, 64]), ALU.mult, [STf.name] + smk, [STf.name])
                for hf in range(2):
                    cols = slice(hf * 512, (hf + 1) * 512)
                    b.tt("dve", STf.ap[:, cols], STf.ap[:, cols], bank[3 + hf].ap, ALU.add, [STf.name, bank[3 + hf].name], [STf.name])
                b.cp("act", STb.ap, STf.ap, [STf.name], [STb.name])

        itemsA = []
        firstA = {}
        for gi, (t0, ntile) in enumerate(grpA):
            for tt_ in range(ntile):
                firstA.setdefault(gi, len(itemsA))
                itemsA.append((gi, t0 + tt_, tt_, len(itemsA)))
        NA = len(itemsA)
        for step in range(NA + 1):
            if step < NA:
                it = itemsA[step]
                if firstA[it[0]] == step:
                    GFA(it[0])
                FA(it)
            if step >= 1:
                KA(itemsA[step - 1])
        for k in range(8):
            bkc = bank[0] if k % 2 == 0 else bank[2]
            b.tr(bkc.ap[:, 0:128], STf.ap[:, k * 128:(k + 1) * 128], identf.ap, [STf.name, identf.name], [bkc.name])
            b.cp("act", NS.ap[:, k, :], bkc.ap[:, 0:128], [bkc.name], [NS.name])
        b.dma("sp", p_a_ssm.rearrange("(k p) n -> p k n", p=128), NS.ap, [NS.name], ["p_a_ssm"])
        if debug:
            b.debug_dump("YA", YA, [128, 8, NTOK], BF16)
        b.release(mA)
    if stop_after == "A":
        P.finalize(final_keys=b.out_keys)
        return b


    BASE = b.top
    XOFF = B.ARENA - NT * 1024 * 4

    def alloc_at(name, free_shape, dt, off):
        keep = b.top
        b.top = off
        t_ = b.alloc(name, free_shape, dt)
        b.top = keep
        return t_

    X = alloc_at("X", [NT, 1024], F32, XOFF)
    TOKG = [(0, 512), (512, 512), (1024, 512), (1536, 512), (2048, 128)]
    mC = b.mark()
    WO = b.alloc("WO", [16, 1024], BF16)
    XT0 = b.alloc("XT0c", [1024], F32)
    assert b.top <= XOFF
    for k in range(16):
        b.dma("pool", WO.ap[:, k, :], w_out[k * 128:(k + 1) * 128, :], [], [WO.name], par=True)
    for t in range(NT):
        tok = slice(t * 128, (t + 1) * 128)
        b.dma("sp", XT0.ap, x_src(t), [], [XT0.name])
        for hf in range(2):
            bk = bank[2 + (2 * t + hf) % 4]
            for k in range(16):
                src = YA if k < 8 else YB
                b.mm(bk.ap, src.ap[:, k % 8, tok], WO.ap[:, k, hf * 512:(hf + 1) * 512], k == 0, k == 15,
                     [src.name, WO.name], [bk.name])
            b.tt("dve", X.ap[:, t, hf * 512:(hf + 1) * 512], bk.ap, XT0.ap[:, hf * 512:(hf + 1) * 512], ALU.add,
                 [bk.name, XT0.name], [X.name])
    b.release(mC)
    if debug and stop_after == "C":
        b.debug_dump("X", X, [128, NT, 1024])
    if stop_after == "C":
        P.finalize(final_keys=b.out_keys)
        return b
    LOW = BASE - 2 * 8 * NTOK * 2

    def norm_all(wcol, XN):
        scr = norm_scratch()
        for t in range(NT):
            norm_tile(X.ap[:, t, :], [X.name], wcol, XN.ap[:, :, t * 128:(t + 1) * 128], XN.name, scr)

    def ffn(layer, wcol):
        b.top = LOW
        XN = b.alloc("XNf", [8, NTOK], BF16)
        HT = [b.alloc("HT0", [4, NTOK], BF16), b.alloc("HT1", [4, NTOK], BF16)]
        W1 = [b.alloc("W1a", [8, 512], BF16), b.alloc("W1b", [8, 512], BF16)]
        W2 = [b.alloc("W2a", [4, 1024], BF16), b.alloc("W2b", [4, 1024], BF16)]
        R1 = [b.alloc("R1a", [512], F32), b.alloc("R1b", [512], F32)]
        norm_all(wcol, XN)
        assert b.top <= XOFF
        ri_ = 0
        for fg in range(8):
            w1, w2, ht = W1[fg % 2], W2[fg % 2], HT[fg % 2]
            for k in range(8):
                b.dma("pool", w1.ap[:, k, :], w_ff1[layer, k * 128:(k + 1) * 128, fg * 512:(fg + 1) * 512], [], [w1.name], par=True)
            for f in range(4):
                b.dma("pool", w2.ap[:, f, :], w_ff2[layer, fg * 512 + f * 128:fg * 512 + (f + 1) * 128, :], [], [w2.name], par=True)
            for f in range(4):
                for (c0, n) in TOKG:
                    bk = bank[ri_ % 2]
                    r1 = R1[ri_ % 2]
                    ri_ += 1
                    for k in range(8):
                        b.mm(bk.ap[:, 0:n], w1.ap[:, k, f * 128:(f + 1) * 128], XN.ap[:, k, c0:c0 + n], k == 0, k == 7,
                             [w1.name, XN.name], [bk.name])
                    b.act(r1.ap[:, 0:n], bk.ap[:, 0:n], AF.Relu, [bk.name], [r1.name])
                    b.tt("pool", ht.ap[:, f, c0:c0 + n], r1.ap[:, 0:n], r1.ap[:, 0:n], ALU.mult, [r1.name], [ht.name])
            for t in range(NT):
                tok = slice(t * 128, (t + 1) * 128)
                for hf in range(2):
                    bk = bank[2 + (2 * t + hf) % 4]
                    for f in range(4):
                        b.mm(bk.ap, ht.ap[:, f, tok], w2.ap[:, f, hf * 512:(hf + 1) * 512], f == 0, f == 3,
                             [ht.name, w2.name], [bk.name])
                    xs_ = X.ap[:, t, hf * 512:(hf + 1) * 512]
                    b.tt("dve", xs_, xs_, bk.ap, ALU.add, [X.name, bk.name], [X.name])

    if "F0" not in skip:
        ffn(0, nff0)
    if debug and stop_after == "F0":
        b.debug_dump("X", X, [128, NT, 1024])
    if stop_after == "F0":
        P.finalize(final_keys=b.out_keys)
        return b

    b.top = LOW
    HP = 30 + 2048
    H = b.alloc("H", [8, HP + 608], BF16)
    HL = b.alloc("HL", [8, 256], F32)
    bpw1 = load_cols("bpw1", c_b_pw1, 16)
    bdw = load_cols("bdw", c_b_dw, 8)
    lnw = load_cols("lnw", c_ln_w, 8)
    lnb = load_cols("lnb", c_ln_b, 8)
    bpw2 = load_bcast("bpw2", c_b_pw2, 1024)
    wdw = b.alloc("wdw", [8, 31], F32)
    for w_ in range(31):
        b.dma("sp" if w_ % 2 else "act", wdw.ap[:, :, w_], c_w_dw[w_:w_ + 1, :].rearrange("o (c p) -> p (o c)", p=128),
              [], [wdw.name], **NCD)
    mX = b.mark()
    XN = b.alloc("XNc", [8, NTOK], BF16)
    WP = [b.alloc("WPa", [8, 256], BF16), b.alloc("WPb", [8, 256], BF16)]
    SGt = b.alloc("SGt", [512], F32)
    HS = b.alloc("HS", [1024], F32)
    norm_all(nmix1, XN)
    assert b.top <= XOFF
    b.memset("pool", H.ap[:, :, 0:30], 0.0, [H.name])
    for q4 in range(4):
        b.dma("sp", HS.ap[0:120, :], sc_conv[4 * q4:4 * q4 + 4].rearrange("s w c -> (s w) c"), [], [HS.name])
        for c in range(8):
            bk = bank[c % 2]
            b.tr(bk.ap[:, 0:120], HS.ap[0:120, c * 128:(c + 1) * 128], identf.ap[0:120, 0:120], [HS.name, identf.name], [bk.name])
            dst = H.ap[:, c, HP:HP + 480].rearrange("p (w s) -> p s w", s=16)[:, 4 * q4:4 * q4 + 4, :]
            b.cp("act", dst, bk.ap[:, 0:120].rearrange("p (s w) -> p s w", s=4), [bk.name], [H.name])
    for c in range(8):
        wp = WP[c % 2]
        for k in range(8):
            b.dma("pool", wp.ap[:, k, 0:128], c_w_pw1[k * 128:(k + 1) * 128, c * 128:(c + 1) * 128], [], [wp.name], par=True)
            b.dma("pool", wp.ap[:, k, 128:256], c_w_pw1[k * 128:(k + 1) * 128, 1024 + c * 128:1024 + (c + 1) * 128], [], [wp.name], par=True)
        for (c0, n) in TOKG:
            bv, bg = bank[0], bank[1]
            for k in range(8):
                b.mm(bv.ap[:, 0:n], wp.ap[:, k, 0:128], XN.ap[:, k, c0:c0 + n], k == 0, k == 7, [wp.name, XN.name], [bv.name])
            for k in range(8):
                b.mm(bg.ap[:, 0:n], wp.ap[:, k, 128:256], XN.ap[:, k, c0:c0 + n], k == 0, k == 7, [wp.name, XN.name], [bg.name])
            b.act(SGt.ap[:, 0:n], bg.ap[:, 0:n], AF.Sigmoid, [bg.name, bpw1.name], [SGt.name], bias=bpw1.ap[:, 8 + c:9 + c])
            if c0 < 2048:
                b.stt(H.ap[:, c, 30 + c0:30 + c0 + n], bv.ap[:, 0:n], bpw1.ap[:, c:c + 1], SGt.ap[:, 0:n], ALU.add, ALU.mult,
                      [bv.name, bpw1.name, SGt.name], [H.name])
                if c0 == 1536:
                    b.stt(HL.ap[:, c, 0:128], bv.ap[:, 384:512], bpw1.ap[:, c:c + 1], SGt.ap[:, 384:512], ALU.add, ALU.mult,
                          [bv.name, bpw1.name, SGt.name], [HL.name])
            else:
                dst = H.ap[:, c, HP + 480:HP + 608].rearrange("p (l s) -> p s l", s=16)
                b.stt(dst, bv.ap[:, 0:128].rearrange("p (s l) -> p s l", s=16), bpw1.ap[:, c:c + 1],
                      SGt.ap[:, 0:128].rearrange("p (s l) -> p s l", s=16), ALU.add, ALU.mult,
                      [bv.name, bpw1.name, SGt.name], [H.name])
                b.stt(HL.ap[:, c, 128:256], bv.ap[:, 0:128], bpw1.ap[:, c:c + 1], SGt.ap[:, 0:128], ALU.add, ALU.mult,
                      [bv.name, bpw1.name, SGt.name], [HL.name])
    b.release(mX)
    OUTS = b.alloc("OUTS", [1024], F32)
    for c in range(8):
        bk = bank[2 + c // 4]
        b.tr(bk.ap[0:30, (c % 4) * 128:(c % 4 + 1) * 128], HL.ap[:, c, 98:128], identf.ap, [HL.name, identf.name], [bk.name])
        if c % 4 == 3:
            b.cp("act", OUTS.ap[0:30, (c // 4) * 512:(c // 4 + 1) * 512], bk.ap[0:30, :], [bk.name], [OUTS.name])
    b.dma("sp", p_c_conv, OUTS.ap[0:30, :], [OUTS.name], ["p_c_conv"])
    for c in range(8):
        bk = bank[4 + c // 4]
        b.tr(bk.ap[:, (c % 4) * 128:(c % 4 + 1) * 128], HL.ap[:, c, 128:256], identf.ap, [HL.name, identf.name], [bk.name])
        if c % 4 == 3:
            b.cp("act", OUTS.ap[:, (c // 4) * 512:(c // 4 + 1) * 512], bk.ap, [bk.name], [OUTS.name])
    for s_ in range(16):
        b.dma("sp" if s_ % 2 else "act", s_c_conv[s_, 22:30, :], OUTS.ap[8 * s_:8 * s_ + 8, :], [OUTS.name], ["s_c_conv"], par=True)
    b.dma("sp", s_c_conv[:, 0:22, :], sc_conv[:, 8:30, :], [], ["s_c_conv"], par=True)
    WP2 = b.alloc("WP2", [8, 1024], BF16)
    for k in range(8):
        b.dma("pool", WP2.ap[:, k, :], c_w_pw2[k * 128:(k + 1) * 128, :], [], [WP2.name], par=True)
    DM = [b.alloc("DMa", [31, 128], BF16), b.alloc("DMb", [31, 128], BF16)]
    DSCR = nc.dram_tensor("dscr", [8, 128, 31 * 128], BF16).ap()
    for c in range(8):
        dm = DM[c % 2]
        for w_ in range(31):
            b.ts("pool", dm.ap[:, w_, :], identb.ap, wdw.ap[:, c, w_:w_ + 1], 1.0, ALU.mult, ALU.mult,
                 [identb.name, wdw.name], [dm.name])
        b.dma("sp", DSCR[c], dm.ap.rearrange("p w i -> p (w i)"), [dm.name], [f"dscr{c}"])
    CV = b.alloc("CV", [8, 512], F32)
    SQ = [b.alloc("SQa", [512], F32), b.alloc("SQb", [512], F32)]
    MEAN = b.alloc("MEAN", [512], F32)
    RSTD = b.alloc("RSTD", [512], F32)
    M2 = b.alloc("M2", [512], F32)
    TMP = b.alloc("TMPc", [512], F32)
    AV = b.alloc("AV", [8, 512], BF16)
    assert b.top <= XOFF, b.top
    for t in range(NT):
        b.tt("pool", X.ap[:, t, :], X.ap[:, t, :], bpw2.ap, ALU.add, [X.name, bpw2.name], [X.name])
    di = 0
    for (c0, n) in TOKG:
        sample_g = (c0 == 2048)
        s1, s2 = bank[2], bank[3]
        for c in range(8):
            dm = DM[di % 2]
            di += 1
            b.dma("sp" if di % 2 else "act", dm.ap.rearrange("p w i -> p (w i)"), DSCR[c], [f"dscr{c}"], [dm.name])
            bk = bank[c % 2]
            for w_ in range(31):
                if sample_g:
                    rhs = H.ap[:, c, HP + 16 * w_:HP + 16 * w_ + 128]
                else:
                    rhs = H.ap[:, c, c0 + w_:c0 + w_ + n]
                b.mm(bk.ap[:, 0:n], dm.ap[:, w_, :], rhs, w_ == 0, w_ == 30, [dm.name, H.name], [bk.name])
            b.act(CV.ap[:, c, 0:n], bk.ap[:, 0:n], AF.Identity, [bk.name, bdw.name], [CV.name], bias=bdw.ap[:, c:c + 1])
            sq = SQ[c % 2]
            b.act(sq.ap[:, 0:n], CV.ap[:, c, 0:n], AF.Square, [CV.name], [sq.name])
            b.mm(s1.ap[:, 0:n], ones.ap, CV.ap[:, c, 0:n], c == 0, c == 7, [ones.name, CV.name], [s1.name])
            b.mm(s2.ap[:, 0:n], ones.ap, sq.ap[:, 0:n], c == 0, c == 7, [ones.name, sq.name], [s2.name])
        b.act(MEAN.ap[:, 0:n], s1.ap[:, 0:n], AF.Copy, [s1.name], [MEAN.name], scale=1.0 / 1024)
        b.tt("dve", M2.ap[:, 0:n], MEAN.ap[:, 0:n], MEAN.ap[:, 0:n], ALU.mult, [MEAN.name], [M2.name])
        b.stt(RSTD.ap[:, 0:n], s2.ap[:, 0:n], 1.0 / 1024, M2.ap[:, 0:n], ALU.mult, ALU.subtract, [s2.name, M2.name], [RSTD.name])
        b.act(RSTD.ap[:, 0:n], RSTD.ap[:, 0:n], AF.Sqrt, [RSTD.name], [RSTD.name], bias=EPS)
        b.recip(RSTD.ap[:, 0:n], RSTD.ap[:, 0:n], [RSTD.name], [RSTD.name])
        for c in range(8):
            b.tt("dve", TMP.ap[:, 0:n], CV.ap[:, c, 0:n], MEAN.ap[:, 0:n], ALU.subtract, [CV.name, MEAN.name], [TMP.name])
            b.tt("dve", TMP.ap[:, 0:n], TMP.ap[:, 0:n], RSTD.ap[:, 0:n], ALU.mult, [TMP.name, RSTD.name], [TMP.name])
            if sample_g:
                dst = AV.ap[:, c, 0:128].rearrange("p (s l) -> p s l", s=16)
                src = TMP.ap[:, 0:128].rearrange("p (l s) -> p s l", s=16)
            else:
                dst, src = AV.ap[:, c, 0:n], TMP.ap[:, 0:n]
            b.act(dst, src, AF.Silu, [TMP.name, lnw.name, lnb.name], [AV.name], bias=lnb.ap[:, c:c + 1], scale=lnw.ap[:, c:c + 1])
        for tt_ in range(n // 128):
            t = c0 // 128 + tt_
            cs = slice(tt_ * 128, (tt_ + 1) * 128)
            for hf in range(2):
                bk = bank[4 + hf]
                for c in range(8):
                    b.mm(bk.ap, AV.ap[:, c, cs], WP2.ap[:, c, hf * 512:(hf + 1) * 512], c == 0, c == 7, [AV.name, WP2.name], [bk.name])
                xs_ = X.ap[:, t, hf * 512:(hf + 1) * 512]
                b.tt("dve", xs_, xs_, bk.ap, ALU.add, [X.name, bk.name], [X.name])
    if debug and stop_after == "L1":
        b.debug_dump("X", X, [128, NT, 1024])
    if stop_after == "L1":
        P.finalize(final_keys=b.out_keys)
        return b

    if "F1" not in skip:
        ffn(1, nff1)

    b.top = LOW
    nfin = load_bcast("nfin", norm_final, 1024)
    junk = b.alloc("junkf", [1024], BF16)
    ssf = b.alloc("ssf", [4], F32)
    YO = [b.alloc("YOa", [1024], F32), b.alloc("YOb", [1024], F32)]
    for t in range(NT):
        yo = YO[t % 2]
        b.act(junk.ap, X.ap[:, t, :], AF.Square, [X.name], [junk.name, ssf.name], accum=ssf.ap[:, 0:1])
        b.act(ssf.ap[:, 1:2], ssf.ap[:, 0:1], AF.Sqrt, [ssf.name], [ssf.name], bias=EPS, scale=1.0 / D)
        b.recip(ssf.ap[:, 2:3], ssf.ap[:, 1:2], [ssf.name], [ssf.name])
        b.stt(yo.ap, X.ap[:, t, :], ssf.ap[:, 2:3], nfin.ap, ALU.mult, ALU.mult, [X.name, ssf.name, nfin.name], [yo.name])
        if t < 16:
            b.dma("sp", y_p[t * 128:(t + 1) * 128, :], yo.ap, [yo.name], ["y_p"], par=True)
        else:
            b.dma("sp", y_s, yo.ap, [yo.name], ["y_s"])
    P.finalize(final_keys=b.out_keys)
    return b


def make_in_maps(inp):
    f = lambda a: np.ascontiguousarray(np.asarray(a, dtype=np.float32))
    shared = {
        "norm_mix": f(inp["norm_mix"]), "norm_ff": f(inp["norm_ff"]), "norm_final": f(inp["norm_final"]).reshape(1, D),
        "w_in": f(inp["w_in_ab"][0]), "a_conv_w": f(inp["a_conv_w"][0]), "a_conv_b": f(inp["a_conv_b"]),
        "a_dt_bias": f(inp["a_dt_bias"]), "a_log": f(inp["a_log"]), "a_d": f(inp["a_d"]), "a_norm": f(inp["a_norm"]),
        "lam_re": f(inp["s5_lam_re"][0]), "lam_im": f(inp["s5_lam_im"][0]), "log_step": f(inp["s5_log_step"]),
        "s5_b_re": f(inp["s5_b_re"][0]), "s5_b_im": f(inp["s5_b_im"][0]), "s5_c_re": f(inp["s5_c_re"][0]),
        "s5_c_im": f(inp["s5_c_im"][0]), "s5_d": f(inp["s5_d"]).reshape(1, 1024), "w_glu": f(inp["s5_w_glu"][0]),
        "b_glu": f(inp["s5_b_glu"]), "w_out": f(inp["w_out_ab"][0]), "c_w_pw1": f(inp["c_w_pw1"][0]),
        "c_b_pw1": f(inp["c_b_pw1"]), "c_w_dw": f(inp["c_w_dw"][0]), "c_b_dw": f(inp["c_b_dw"]),
        "c_ln_w": f(inp["c_ln_w"]), "c_ln_b": f(inp["c_ln_b"]), "c_w_pw2": f(inp["c_w_pw2"][0]),
        "c_b_pw2": f(inp["c_b_pw2"]), "w_ff1": f(inp["w_ff1"]), "w_ff2": f(inp["w_ff2"]),
    }
    maps = []
    for c in range(8):
        s = slice(16 * c, 16 * c + 16)
        m = dict(shared)
        m["xp"] = f(inp["x_prompt"][c])
        m["xs"] = f(inp["x_sample"][s]).reshape(128, D)
        m["sa_conv"] = f(inp["state_a_conv"][0, s]).reshape(48, 2048)
        m["sa_ssm"] = f(inp["state_a_ssm"][0, s]).reshape(16, 1024, 128)
        m["sb_re"] = f(inp["state_b_re"][0, s]).reshape(16, 4096)
        m["sb_im"] = f(inp["state_b_im"][0, s]).reshape(16, 4096)
        m["sc_conv"] = f(inp["state_c_conv"][0, s])
        maps.append(m)
    return maps


_CACHE = {}


def kernel(**inputs):
    if "nc" not in _CACHE:
        _CACHE["nc"] = build().nc
    nc = _CACHE["nc"]
    maps = make_in_maps(inputs)
    res = run_bass_kernel_spmd(nc, maps, core_ids=list(range(8)))
    R = res.results
    cat = lambda k: np.stack([np.asarray(r[k]) for r in R], axis=0)
    y_prompt = cat("y_p").reshape(8, 2048, D)
    y_sample = cat("y_s").reshape(128, 8, D)
    p_a_conv = cat("p_a_conv").reshape(1, 8, 3, 2048)
    p_a_ssm = cat("p_a_ssm").reshape(1, 8, 16, 64, 128)
    p_b_re = cat("p_b_re").reshape(1, 8, 64, 64)
    p_b_im = cat("p_b_im").reshape(1, 8, 64, 64)
    p_c_conv = cat("p_c_conv").reshape(1, 8, 30, D)
    s_a_conv = cat("s_a_conv").reshape(1, 128, 3, 2048)
    s_a_ssm = cat("s_a_ssm").reshape(1, 128, 16, 64, 128)
    s_b_re = cat("s_b_re").reshape(1, 128, 64, 64)
    s_b_im = cat("s_b_im").reshape(1, 128, 64, 64)
    s_c_conv = cat("s_c_conv").reshape(1, 128, 30, D)
    return tuple(np.ascontiguousarray(a, dtype=np.float32) for a in
                 (y_prompt, y_sample, p_a_conv, p_a_ssm, p_b_re, p_b_im, p_c_conv,
                  s_a_conv, s_a_ssm, s_b_re, s_b_im, s_c_conv))
```
